# Optimizing a Trainium2 kernel written in Bass

```python
import math
import jax, jax.numpy as jnp
from jax import lax
import numpy as np

D_MODEL = 1024
BATCH = 4
SEQ = 8192
DEPTH = 2

N_EVEN = (DEPTH + 1) // 2
N_ODD = DEPTH // 2
EPS = 1e-6

SSD_D_INNER = D_MODEL
SSD_HEAD_DIM = 64
SSD_N_HEADS = SSD_D_INNER // SSD_HEAD_DIM
SSD_N_GROUPS = 2
SSD_D_STATE = 128
SSD_CONV = 4
SSD_CHUNK = 128
SSD_CONV_DIM = SSD_D_INNER + 2 * SSD_N_GROUPS * SSD_D_STATE

S5_WIDTH = D_MODEL
S5_GROUP = 16
S5_N_GROUPS = S5_WIDTH // S5_GROUP
S5_STATE = 64

HYB_IN = SSD_D_INNER + SSD_CONV_DIM + SSD_N_HEADS + S5_WIDTH
HYB_OUT = SSD_D_INNER + S5_WIDTH

ATT_HEAD_DIM = 128
ATT_KV_HEADS = D_MODEL // ATT_HEAD_DIM
ATT_PATTERNS = ((128, 1), (512, 4), (2048, 16))
ATT_N_PAT = len(ATT_PATTERNS)
ATT_Q_HEADS = ATT_N_PAT * ATT_KV_HEADS
ATT_BLOCK = 128
ATT_QKV = (ATT_Q_HEADS + 2 * ATT_KV_HEADS) * ATT_HEAD_DIM
ATT_OUT = ATT_KV_HEADS * ATT_HEAD_DIM

FFN_HIDDEN = -((-8 * D_MODEL) // (3 * 256)) * 256

kernel_name = "hybrid_ssd_s5_dilated_attn_trunk"


def rms_norm(x, g):
    x32 = x.astype(jnp.float32)
    y = x32 * lax.rsqrt(jnp.mean(x32 * x32, axis=-1, keepdims=True) + EPS)
    return (y * g.astype(jnp.float32)).astype(x.dtype)


def causal_dwconv(x, w, b):
    y = lax.conv_general_dilated(
        x, w[:, None, :].astype(x.dtype), window_strides=(1,),
        padding=[(w.shape[0] - 1, 0)], dimension_numbers=('NWC', 'WIO', 'NWC'),
        feature_group_count=x.shape[-1])
    return y + b.astype(x.dtype)


def ssd_scan(x, a, b, c):
    bs, l, h, p = x.shape
    g, n = b.shape[2], b.shape[3]
    r = h // g
    t = SSD_CHUNK
    nc = l // t
    xc = x.reshape(bs, nc, t, g, r, p)
    ac = jnp.cumsum(a.reshape(bs, nc, t, g, r), axis=2)
    bc = b.reshape(bs, nc, t, g, n)
    cc = c.reshape(bs, nc, t, g, n)
    causal = jnp.tril(jnp.ones((t, t), bool))[:, :, None, None]
    decay = jnp.exp(jnp.where(causal, ac[:, :, :, None] - ac[:, :, None, :], -jnp.inf))
    cb = jnp.einsum('bclgn,bcsgn->bclsg', cc, bc)
    y_diag = jnp.einsum('bclsg,bclsgr,bcsgrp->bclgrp', cb, decay, xc)
    decay_to_end = jnp.exp(ac[:, :, -1:] - ac)
    states = jnp.einsum('bcsgn,bcsgr,bcsgrp->bcgrpn', bc, decay_to_end, xc)
    chunk_decay = jnp.exp(ac[:, :, -1])

    def step(h_prev, inp):
        dec, st = inp
        return dec[..., None, None] * h_prev + st, h_prev

    h0 = jnp.zeros((bs, g, r, p, n), x.dtype)
    _, prev = lax.scan(step, h0, (jnp.moveaxis(chunk_decay, 1, 0), jnp.moveaxis(states, 1, 0)))
    prev = jnp.moveaxis(prev, 0, 1)
    y_off = jnp.einsum('bclgn,bcgrpn,bclgr->bclgrp', cc, prev, jnp.exp(ac))
    return (y_diag + y_off).reshape(bs, l, h, p)


def ssd_mixer(z, xbc, dt_raw, conv_w, conv_b, dt_bias, a_log, d_skip, norm_g):
    f32 = jnp.float32
    bs, l, _ = z.shape
    gn = SSD_N_GROUPS * SSD_D_STATE
    xbc = jax.nn.silu(causal_dwconv(xbc, conv_w, conv_b)).astype(f32)
    xs = xbc[..., :SSD_D_INNER].reshape(bs, l, SSD_N_HEADS, SSD_HEAD_DIM)
    bm = xbc[..., SSD_D_INNER:SSD_D_INNER + gn].reshape(bs, l, SSD_N_GROUPS, SSD_D_STATE)
    cm = xbc[..., SSD_D_INNER + gn:].reshape(bs, l, SSD_N_GROUPS, SSD_D_STATE)
    dt = jax.nn.softplus(dt_raw.astype(f32) + dt_bias.astype(f32))
    a = -jnp.exp(a_log.astype(f32))
    y = ssd_scan(xs * dt[..., None], dt * a, bm, cm) + d_skip.astype(f32)[:, None] * xs
    y = y.reshape(bs, l, SSD_D_INNER) * jax.nn.silu(z.astype(f32))
    yg = y.reshape(bs, l, SSD_N_GROUPS, SSD_D_INNER // SSD_N_GROUPS)
    yg = yg * lax.rsqrt(jnp.mean(yg * yg, axis=-1, keepdims=True) + EPS)
    return (yg.reshape(bs, l, SSD_D_INNER) * norm_g.astype(f32)).astype(z.dtype)


def complex_linear_combine(e1, e2):
    a1r, a1i, b1r, b1i = e1
    a2r, a2i, b2r, b2i = e2
    return (a2r * a1r - a2i * a1i, a2r * a1i + a2i * a1r,
            a2r * b1r - a2i * b1i + b2r, a2r * b1i + a2i * b1r + b2i)


def s5_mixer(u, lam_re, lam_im, log_dt, b_re, b_im, c_re, c_im, d_skip, glu_w, glu_b):
    f32 = jnp.float32
    bs, l, _ = u.shape
    lam_re, lam_im = lam_re.astype(f32), lam_im.astype(f32)
    dt = jnp.exp(log_dt.astype(f32))[:, None]
    mag = jnp.exp(lam_re * dt)
    a_re, a_im = mag * jnp.cos(lam_im * dt), mag * jnp.sin(lam_im * dt)
    den = lam_re * lam_re + lam_im * lam_im
    q_re = ((a_re - 1.0) * lam_re + a_im * lam_im) / den
    q_im = (a_im * lam_re - (a_re - 1.0) * lam_im) / den
    b_re, b_im = b_re.astype(f32), b_im.astype(f32)
    bb_re = q_re[..., None] * b_re - q_im[..., None] * b_im
    bb_im = q_re[..., None] * b_im + q_im[..., None] * b_re
    c_re, c_im = c_re.astype(f32), c_im.astype(f32)
    u_g = u.astype(f32).reshape(bs, l, S5_N_GROUPS, S5_GROUP)

    def run_sequence(us):
        bu_re = jnp.einsum('lgc,gpc->lgp', us, bb_re)
        bu_im = jnp.einsum('lgc,gpc->lgp', us, bb_im)
        ar = jnp.broadcast_to(a_re, bu_re.shape)
        ai = jnp.broadcast_to(a_im, bu_re.shape)
        _, _, h_re, h_im = lax.associative_scan(complex_linear_combine, (ar, ai, bu_re, bu_im), axis=0)
        return jnp.einsum('lgp,gcp->lgc', h_re, c_re) - jnp.einsum('lgp,gcp->lgc', h_im, c_im)

    y = lax.map(run_sequence, u_g)
    y = y + d_skip.astype(f32).reshape(S5_N_GROUPS, S5_GROUP) * u_g
    y = jax.nn.gelu(y.reshape(bs, l, S5_WIDTH), approximate=False)
    y = y * jax.nn.sigmoid(y @ glu_w.astype(f32) + glu_b.astype(f32))
    return y.astype(u.dtype)


def dilated_window_attention(q, k, v, window, dilation):
    bs, l, h, e = q.shape
    span = window // dilation
    unit = dilation * ATT_BLOCK
    lp = -(-l // unit) * unit
    m = lp // dilation
    nb = m // ATT_BLOCK

    def to_blocks(t):
        t = jnp.pad(t, ((0, 0), (0, lp - l), (0, 0), (0, 0))).reshape(bs, m, dilation, h, e)
        return jnp.swapaxes(t, 1, 2).reshape(bs, dilation, nb, ATT_BLOCK, h, e)

    def with_prev(t):
        prev = jnp.pad(t, ((0, 0), (0, 0), (1, 0), (0, 0), (0, 0), (0, 0)))[:, :, :-1]
        return jnp.concatenate([prev, t], axis=3)

    qb = to_blocks(q)
    kb = with_prev(to_blocks(k))
    vb = with_prev(to_blocks(v))
    s = jnp.einsum('brnqhe,brnkhe->brnhqk', qb, kb).astype(jnp.float32) * (e ** -0.5)
    qi = jnp.arange(ATT_BLOCK)[:, None]
    kj = jnp.arange(2 * ATT_BLOCK)[None, :]
    dist = ATT_BLOCK + qi - kj
    band = (dist >= 0) & (dist <= span)
    has_prev = jnp.arange(nb)[:, None, None] > 0
    valid = band[None] & (has_prev | (kj >= ATT_BLOCK)[None])
    s = jnp.where(valid[:, None], s, -jnp.inf)
    s_max = jnp.max(s, axis=-1, keepdims=True)
    p = jnp.exp(s - s_max)
    den = jnp.sum(p, axis=-1, keepdims=True)
    o = jnp.einsum('brnhqk,brnkhe->brnqhe', p / den, vb.astype(jnp.float32))
    lse = (s_max + jnp.log(den))[..., 0]
    o = jnp.swapaxes(o.reshape(bs, dilation, m, h, e), 1, 2).reshape(bs, lp, h, e)[:, :l]
    lse = jnp.swapaxes(jnp.swapaxes(lse, 3, 4).reshape(bs, dilation, m, h), 1, 2).reshape(bs, lp, h)[:, :l]
    return o, lse


def dilated_attention_mixer(h, w_qkv, w_o):
    bs, l, _ = h.shape
    nq = ATT_Q_HEADS * ATT_HEAD_DIM
    nk = ATT_KV_HEADS * ATT_HEAD_DIM
    qkv = h @ w_qkv
    q = qkv[..., :nq].reshape(bs, l, ATT_N_PAT, ATT_KV_HEADS, ATT_HEAD_DIM)
    k = qkv[..., nq:nq + nk].reshape(bs, l, ATT_KV_HEADS, ATT_HEAD_DIM)
    v = qkv[..., nq + nk:].reshape(bs, l, ATT_KV_HEADS, ATT_HEAD_DIM)
    outs, lses = [], []
    for i, (window, dilation) in enumerate(ATT_PATTERNS):
        o, lse = dilated_window_attention(q[:, :, i], k, v, window, dilation)
        outs.append(o)
        lses.append(lse)
    wts = jax.nn.softmax(jnp.stack(lses), axis=0)
    o = jnp.sum(wts[..., None] * jnp.stack(outs), axis=0)
    return o.reshape(bs, l, ATT_OUT).astype(h.dtype) @ w_o


def swiglu(h, w_in, w_out):
    g, u = jnp.split(h @ w_in, 2, axis=-1)
    return (jax.nn.silu(g) * u) @ w_out


def setup_inputs(seed: int = 0) -> dict:
    key = jax.random.key(seed)
    ks = jax.random.split(key, 32)
    f32 = jnp.float32

    def nrm(k, shape, scale):
        return jax.random.normal(k, shape, f32) * scale

    D = D_MODEL
    dt0 = jnp.exp(jax.random.uniform(ks[12], (N_EVEN, SSD_N_HEADS), f32, math.log(1e-3), math.log(1e-1)))
    lam_im = jnp.pi * jnp.arange(S5_STATE, dtype=f32)
    return {
        "x": nrm(ks[0], (BATCH, SEQ, D), 1.0),
        "c": nrm(ks[1], (BATCH, D), 1.0),
        "ada_w": nrm(ks[2], (DEPTH, D, 6 * D), 0.5 * D ** -0.5),
        "ada_b": nrm(ks[3], (DEPTH, 6 * D), 0.02),
        "mix_pre_g": 1.0 + nrm(ks[4], (DEPTH, D), 0.02),
        "mix_post_g": 1.0 + nrm(ks[5], (DEPTH, D), 0.02),
        "ffn_pre_g": 1.0 + nrm(ks[6], (DEPTH, D), 0.02),
        "ffn_post_g": 1.0 + nrm(ks[7], (DEPTH, D), 0.02),
        "ffn_w_in": nrm(ks[8], (DEPTH, D, 2 * FFN_HIDDEN), D ** -0.5),
        "ffn_w_out": nrm(ks[9], (DEPTH, FFN_HIDDEN, D), FFN_HIDDEN ** -0.5),
        "hyb_w_in": nrm(ks[10], (N_EVEN, D, HYB_IN), D ** -0.5),
        "ssd_conv_w": nrm(ks[11], (N_EVEN, SSD_CONV, SSD_CONV_DIM), SSD_CONV ** -0.5),
        "ssd_conv_b": nrm(ks[13], (N_EVEN, SSD_CONV_DIM), 0.02),
        "ssd_dt_bias": dt0 + jnp.log(-jnp.expm1(-dt0)),
        "ssd_a_log": jnp.log(jax.random.uniform(ks[14], (N_EVEN, SSD_N_HEADS), f32, 1.0, 16.0)),
        "ssd_d": 1.0 + nrm(ks[15], (N_EVEN, SSD_N_HEADS), 0.1),
        "ssd_norm_g": 1.0 + nrm(ks[16], (N_EVEN, SSD_D_INNER), 0.02),
        "s5_lambda_re": -0.5 + nrm(ks[17], (N_EVEN, S5_N_GROUPS, S5_STATE), 0.01),
        "s5_lambda_im": lam_im + nrm(ks[18], (N_EVEN, S5_N_GROUPS, S5_STATE), 0.01),
        "s5_log_dt": jax.random.uniform(ks[19], (N_EVEN, S5_N_GROUPS), f32, math.log(1e-3), math.log(1e-1)),
        "s5_b_re": nrm(ks[20], (N_EVEN, S5_N_GROUPS, S5_STATE, S5_GROUP), (2 * S5_GROUP) ** -0.5),
        "s5_b_im": nrm(ks[21], (N_EVEN, S5_N_GROUPS, S5_STATE, S5_GROUP), (2 * S5_GROUP) ** -0.5),
        "s5_c_re": nrm(ks[22], (N_EVEN, S5_N_GROUPS, S5_GROUP, S5_STATE), (2 * S5_STATE) ** -0.5),
        "s5_c_im": nrm(ks[23], (N_EVEN, S5_N_GROUPS, S5_GROUP, S5_STATE), (2 * S5_STATE) ** -0.5),
        "s5_d": nrm(ks[24], (N_EVEN, S5_WIDTH), 1.0),
        "s5_glu_w": nrm(ks[25], (N_EVEN, S5_WIDTH, S5_WIDTH), S5_WIDTH ** -0.5),
        "s5_glu_b": nrm(ks[26], (N_EVEN, S5_WIDTH), 0.02),
        "hyb_w_out": nrm(ks[27], (N_EVEN, HYB_OUT, D), HYB_OUT ** -0.5),
        "attn_w_qkv": nrm(ks[28], (N_ODD, D, ATT_QKV), D ** -0.5),
        "attn_w_o": nrm(ks[29], (N_ODD, ATT_OUT, D), ATT_OUT ** -0.5),
    }


def reference(x, c, ada_w, ada_b, mix_pre_g, mix_post_g, ffn_pre_g, ffn_post_g,
              ffn_w_in, ffn_w_out, hyb_w_in, ssd_conv_w, ssd_conv_b, ssd_dt_bias,
              ssd_a_log, ssd_d, ssd_norm_g, s5_lambda_re, s5_lambda_im, s5_log_dt,
              s5_b_re, s5_b_im, s5_c_re, s5_c_im, s5_d, s5_glu_w, s5_glu_b,
              hyb_w_out, attn_w_qkv, attn_w_o):
    cond = jax.nn.silu(c)
    split_at = [SSD_D_INNER, SSD_D_INNER + SSD_CONV_DIM, SSD_D_INNER + SSD_CONV_DIM + SSD_N_HEADS]
    for i in range(DEPTH):
        mod = (cond @ ada_w[i] + ada_b[i])[:, None, :]
        sh_m, sc_m, gt_m, sh_f, sc_f, gt_f = jnp.split(mod, 6, axis=-1)
        h = rms_norm(x, mix_pre_g[i]) * (1 + sc_m) + sh_m
        j = i // 2
        if i % 2 == 0:
            proj = h @ hyb_w_in[j]
            z, xbc, dt_raw, u = jnp.split(proj, split_at, axis=-1)
            y_ssd = ssd_mixer(z, xbc, dt_raw, ssd_conv_w[j], ssd_conv_b[j], ssd_dt_bias[j],
                              ssd_a_log[j], ssd_d[j], ssd_norm_g[j])
            y_s5 = s5_mixer(u, s5_lambda_re[j], s5_lambda_im[j], s5_log_dt[j], s5_b_re[j],
                            s5_b_im[j], s5_c_re[j], s5_c_im[j], s5_d[j], s5_glu_w[j], s5_glu_b[j])
            y = jnp.concatenate([y_ssd, y_s5], axis=-1) @ hyb_w_out[j]
        else:
            y = dilated_attention_mixer(h, attn_w_qkv[j], attn_w_o[j])
        x = x + gt_m * rms_norm(y, mix_post_g[i])
        h = rms_norm(x, ffn_pre_g[i]) * (1 + sc_f) + sh_f
        x = x + gt_f * rms_norm(swiglu(h, ffn_w_in[i], ffn_w_out[i]), ffn_post_g[i])
    return x
```

```python
from contextlib import ExitStack
import numpy as np
import concourse.bass as bass
import concourse.mybir as mybir
from concourse.bass_utils import run_bass_kernel_spmd

F32 = mybir.dt.float32
BF16 = mybir.dt.bfloat16
ALU = mybir.AluOpType
AF = mybir.ActivationFunctionType

COMPUTE = ("pe", "dve", "act", "pool")
QUEUES = ("sp",)


class Buf:
    __slots__ = ("name", "w", "r")

    def __init__(self, name=""):
        self.name = name
        self.w = None
        self.r = []


class Sched:
    def __init__(self, nc, stack, n_dma_sems=32):
        self.nc = nc
        self.ops = {e: [] for e in COMPUTE + QUEUES}
        self.sems = {}
        self.cnt = {}
        for e in COMPUTE:
            self.sems[e] = stack.enter_context(nc.semaphore("sem_" + e))
            self.cnt[e] = 0
        self.dma_sems = []
        for i in range(n_dma_sems):
            k = "dma%d" % i
            self.sems[k] = stack.enter_context(nc.semaphore("sem_" + k))
            self.cnt[k] = 0
            self.dma_sems.append(k)
        self.dma_rr = 0
        self.dma_last_tok = {k: None for k in self.dma_sems}
        self.seen = {e: {} for e in COMPUTE + QUEUES}
        self.n_inst = 0

    def buf(self, name=""):
        return Buf(name)

    def _collect(self, eng, reads, writes, extra=()):
        waits = {}
        seen = self.seen[eng]

        def add(tok):
            if tok is None:
                return
            k, v = tok
            if seen.get(k, 0) >= v:
                return
            if waits.get(k, 0) < v:
                waits[k] = v

        for b in reads:
            add(b.w)
        for b in writes:
            add(b.w)
            for t in b.r:
                add(t)
        for t in extra:
            add(t)
        for k, v in waits.items():
            seen[k] = v
        return list(waits.items())

    def _commit(self, tok, reads, writes):
        for b in reads:
            b.r.append(tok)
            if len(b.r) > 64:
                m = {}
                for k, v in b.r:
                    if m.get(k, 0) < v:
                        m[k] = v
                b.r = list(m.items())
        for b in writes:
            b.w = tok
            b.r = []

    def op(self, eng, fn, reads=(), writes=()):
        waits = self._collect(eng, reads, writes)
        self.cnt[eng] += 1
        tok = (eng, self.cnt[eng])
        sems = self.sems
        mysem = sems[eng]

        def emit(e):
            for k, v in waits:
                e.wait_ge(sems[k], v)
            fn(e).then_inc(mysem, 1)

        self.ops[eng].append(emit)
        self.n_inst += 1
        self._commit(tok, reads, writes)
        return tok

    def dma(self, fn, reads=(), writes=(), queue="sp"):
        k = self.dma_sems[self.dma_rr]
        self.dma_rr = (self.dma_rr + 1) % len(self.dma_sems)
        prev = self.dma_last_tok[k]
        extra = (prev,) if prev is not None else ()
        waits = self._collect(queue, reads, writes, extra)
        self.cnt[k] += 16
        tok = (k, self.cnt[k])
        self.dma_last_tok[k] = tok
        sems = self.sems
        dsem = sems[k]

        def emit(e):
            for kk, v in waits:
                e.wait_ge(sems[kk], v)
            fn(e).then_inc(dsem, 16)

        self.ops[queue].append(emit)
        self.n_inst += 1
        self._commit(tok, reads, writes)
        return tok

    def barrier(self):
        sems = self.sems
        for eng in COMPUTE + QUEUES:
            waits = []
            for k, c in self.cnt.items():
                if k == eng or c == 0:
                    continue
                if self.seen[eng].get(k, 0) < c:
                    self.seen[eng][k] = c
                    waits.append((k, c))

            def emit(e, waits=waits):
                for kk, v in waits:
                    e.wait_ge(sems[kk], v)

            self.ops[eng].append(emit)

    def finish(self, final_toks):
        nc = self.nc
        sems = self.sems
        ops = self.ops
        fin = {}
        for t in final_toks:
            if t is None:
                continue
            k, v = t
            fin[k] = max(fin.get(k, 0), v)
        with nc.Block() as block:
            @block.tensor
            def _(e):
                for f in ops["pe"]:
                    f(e)

            @block.vector
            def _(e):
                for f in ops["dve"]:
                    f(e)

            @block.scalar
            def _(e):
                for f in ops["act"]:
                    f(e)

            @block.gpsimd
            def _(e):
                for f in ops["pool"]:
                    f(e)

            @block.sync
            def _(e):
                for f in ops["sp"]:
                    f(e)
                for k, v in fin.items():
                    e.wait_ge(sems[k], v)


D = 1024
KC = D // 128
NT = 256
FFN_H = 2816
FC = FFN_H // 128
HYB_IN = 3600
EPS = 1e-6
N_CORES = 8


class Builder:
    def __init__(self, L, layers=(0, 1), mixers=True, dbg=None):
        self.L = L
        self.layers = layers
        self.mixers = mixers
        self.dbg = dbg or []
        self.nc = bass.Bass("TRN2", target_bir_lowering=False)
        self.stack = ExitStack()
        self.s = Sched(self.nc, self.stack)
        self.rr = 0
        self._tensors()

    def dram(self, name, shape, dt, kind="Internal"):
        return self.nc.dram_tensor(name, list(shape), dt, kind=kind).ap()

    def sb(self, name, shape, dt):
        t = self.stack.enter_context(self.nc.sbuf_tensor(name, list(shape), dt))
        return t

    def _tensors(self):
        nc, s, L = self.nc, self.s, self.L
        I = "ExternalInput"
        self.x_in = self.dram("x_fm", [D, L], F32, I)
        self.out = self.dram("out_fm", [D, L], F32, "ExternalOutput")
        self.c_in = self.dram("c_fm", [128, KC], F32, I)
        self.ada_w = self.dram("ada_w", [2, D, 6 * D], F32, I)
        self.ada_b = self.dram("ada_b", [128, 2, 48], F32, I)
        self.gains = self.dram("gains", [128, 4, 2, KC], F32, I)
        self.w_f32 = {}
        self.w_bf = {}
        self.w_buf = {}
        for name, shape in (("ffn_w_in0", [D, 2 * FFN_H]), ("ffn_w_in1", [D, 2 * FFN_H]),
                            ("ffn_w_out0", [FFN_H, D]), ("ffn_w_out1", [FFN_H, D]),
                            ("hyb_w_in", [D, HYB_IN]), ("hyb_w_out", [2 * D, D]),
                            ("s5_glu_w", [D, D]), ("attn_w_qkv", [D, 5 * D]), ("attn_w_o", [D, D])):
            self.w_f32[name] = self.dram(name, shape, F32, I)
            self.w_bf[name] = self.dram(name + "_bf", shape, BF16)
            self.w_buf[name] = s.buf(name)

        self.xt = self.sb("xt", [128, KC, NT], F32); self.b_x = [s.buf("x%d" % j) for j in range(KC)]
        self.ht = self.sb("ht", [128, KC, NT], BF16); self.b_h = [s.buf("h%d" % j) for j in range(KC)]
        self.yt = self.sb("yt", [128, KC, NT], F32); self.b_y = [s.buf("y%d" % j) for j in range(KC)]
        self.sq = self.sb("sq", [128, 2, NT], BF16); self.b_sq = [s.buf("sq0"), s.buf("sq1")]
        self.rstd = self.sb("rstd", [128, NT], F32); self.b_rstd = s.buf("rstd")
        self.tmpn = self.sb("tmpn", [128, 2, NT], F32); self.b_tmpn = [s.buf("tn0"), s.buf("tn1")]
        self.b_act = [s.buf("a%d" % j) for j in range(FC)]
        self.sg = self.sb("sg", [128, 2, NT], F32); self.b_sg = [s.buf("sg0"), s.buf("sg1")]
        self.NW = 3
        self.wp = self.sb("wp", [128, self.NW, 4096], BF16); self.b_wp = [s.buf("wp%d" % i) for i in range(self.NW)]
        self.wp_rr = 0
        self.b_arena = [s.buf("ar%d" % i) for i in range(2)]
        self._tensors_mix()
        self.arena = self.A32[:, 0:4096].rearrange("p (a n) -> p a n", a=2)
        self.act = self.A16[:, 0:FC * NT].rearrange("p (j t) -> p j t", j=FC)
        self.ones_bf = self.sb("ones_bf", [128, 128], BF16); self.b_const = s.buf("const")
        self.cst = self.sb("cst", [128, 4], F32)
        self.cond = self.sb("cond", [128, KC], F32)
        self.mod = self.sb("mod", [128, 2, 48], F32); self.b_mod = s.buf("mod")
        self.adab = self.sb("adab", [128, 2, 48], F32)
        self.gn = self.sb("gn", [128, 4, 2, KC], F32)
        self.vec = self.sb("vec", [128, 2, 6, KC], F32); self.b_vec = s.buf("vec")
        self.ps = []
        self.b_ps = []
        for i in range(8):
            self.ps.append(self.stack.enter_context(nc.psum_tensor("ps%d" % i, [128, 512], F32)))
            self.b_ps.append(s.buf("ps%d" % i))
        self.ps_rr = 0

    POOL_BANKS = (0, 1, 2, 3, 7)

    def psum(self):
        i = self.POOL_BANKS[self.ps_rr]
        self.ps_rr = (self.ps_rr + 1) % len(self.POOL_BANKS)
        return self.ps[i], self.b_ps[i]

    def psum_fixed(self, i):
        return self.ps[i], self.b_ps[i]

    def eng3(self):
        e = ("act", "pool", "dve")[self.rr % 3]
        self.rr += 1
        return e

    def cast_weights(self, names):
        s = self.s
        i = 0
        for name in names:
            src, dst = self.w_f32[name], self.w_bf[name]
            K, N = src.shape
            for kb in range(K // 128):
                for c0 in range(0, N, 2048):
                    cw = min(2048, N - c0)
                    slot = i % 2
                    wslot = i % 3
                    i += 1
                    st32 = self.arena[:, slot, 0:cw]
                    stbf = self.wp[:, wslot, 0:cw]
                    s.dma(lambda e, a=st32, b=src[kb * 128:(kb + 1) * 128, c0:c0 + cw]: e.dma_start(out=a, in_=b),
                          writes=[self.b_arena[slot]])
                    eng = self.eng3()
                    if eng == "act":
                        s.op("act", lambda e, a=stbf, b=st32: e.activation(out=a, in_=b, func=AF.Copy),
                             reads=[self.b_arena[slot]], writes=[self.b_wp[wslot]])
                    else:
                        s.op(eng, lambda e, a=stbf, b=st32: e.tensor_copy(a, b),
                             reads=[self.b_arena[slot]], writes=[self.b_wp[wslot]])
                    s.dma(lambda e, a=dst[kb * 128:(kb + 1) * 128, c0:c0 + cw], b=stbf: e.dma_start(out=a, in_=b),
                          reads=[self.b_wp[wslot]], writes=[self.w_buf[name]])

    def setup_consts(self):
        s = self.s
        s.op("pool", lambda e: e.memset(self.ones_bf[:, :], 1.0), writes=[self.b_const])
        s.op("pool", lambda e: e.memset(self.cst[:, 0:1], EPS), writes=[self.b_const])
        s.op("pool", lambda e: e.memset(self.cst[:, 1:2], 1.0), writes=[self.b_const])
        s.op("pool", lambda e: e.memset(self.cst[:, 2:3], 0.0), writes=[self.b_const])
        s.dma(lambda e: e.dma_start(out=self.cond[:, :], in_=self.c_in), writes=[self.b_const])
        s.dma(lambda e: e.dma_start(out=self.adab[:, :, :], in_=self.ada_b), writes=[self.b_const])
        s.dma(lambda e: e.dma_start(out=self.gn[:, :, :, :], in_=self.gains), writes=[self.b_const])
        s.op("act", lambda e: e.activation(out=self.cond[:, :], in_=self.cond[:, :], func=AF.Silu),
             reads=[self.b_const], writes=[self.b_const])

    def ada_mod(self):
        s = self.s
        for li in range(2):
            ps, bps = self.psum()
            for cp in range(12):
                slot = cp % 3
                for half in range(2):
                    c0 = cp * 512 + half * 256
                    slot = (cp * 2 + half) % 2
                    view = self.arena[:, slot, 0:2048].rearrange("p (k n) -> p k n", k=KC)
                    s.dma(lambda e, a=view, b=self.ada_w[li, :, c0:c0 + 256].rearrange("(k p) n -> p k n", p=128):
                          e.dma_start(out=a, in_=b), writes=[self.b_arena[slot]])
                    for mm in range(2):
                        m = (c0 // 128) + mm
                        for k in range(KC):
                            s.op("pe", lambda e, o=ps[:, m:m + 1], w=view[:, k, mm * 128:(mm + 1) * 128], r=self.cond[:, k:k + 1],
                                 st=(k == 0), sp=(k == KC - 1): e.matmul(o, w, r, start=st, stop=sp),
                                 reads=[self.b_arena[slot], self.b_const], writes=[bps])
            s.op("dve", lambda e, o=self.mod[:, li, :], a=ps[:, 0:48], b=self.adab[:, li, :]:
                 e.tensor_tensor(o, a, b, ALU.add), reads=[bps, self.b_const], writes=[self.b_mod])
        for li in range(2):
            for half, (gpre, gpost) in enumerate(((0, 1), (2, 3))):
                o = half * 24
                sh = self.mod[:, li, o:o + 8]
                sc = self.mod[:, li, o + 8:o + 16]
                gt = self.mod[:, li, o + 16:o + 24]
                s.op("dve", lambda e, out=self.vec[:, li, half * 3 + 0, :], sc=sc, g=self.gn[:, gpre, li, :]:
                     e.scalar_tensor_tensor(out, sc, 1.0, g, ALU.add, ALU.mult),
                     reads=[self.b_mod, self.b_const], writes=[self.b_vec])
                s.op("dve", lambda e, out=self.vec[:, li, half * 3 + 1, :], sh=sh: e.tensor_copy(out, sh),
                     reads=[self.b_mod], writes=[self.b_vec])
                s.op("dve", lambda e, out=self.vec[:, li, half * 3 + 2, :], gt=gt, g=self.gn[:, gpost, li, :]:
                     e.tensor_tensor(out, gt, g, ALU.mult), reads=[self.b_mod, self.b_const], writes=[self.b_vec])

    def wpanel(self, name, kc, c0, cols, k0=0):
        s = self.s
        assert kc * cols <= 4096
        slot = self.wp_rr
        self.wp_rr = (self.wp_rr + 1) % self.NW
        view = self.wp[:, slot, 0:kc * cols].rearrange("p (k n) -> p k n", k=kc)
        src = self.w_bf[name][k0 * 128:(k0 + kc) * 128, c0:c0 + cols].rearrange("(k p) n -> p k n", p=128)
        s.dma(lambda e, a=view, b=src: e.dma_start(out=a, in_=b), reads=[self.w_buf[name]], writes=[self.b_wp[slot]])
        return view, self.b_wp[slot]

    def sumsq_rstd(self, src, b_src, nchunks, denom):
        s = self.s
        ps, bps = self.psum()
        for j in range(nchunks):
            q = j % 2
            eng = "pool" if j % 2 == 0 else "act"
            if eng == "act":
                s.op("act", lambda e, o=self.sq[:, q, :], a=src[j]: e.activation(out=o, in_=a, func=AF.Square),
                     reads=[b_src[j]], writes=[self.b_sq[q]])
            else:
                s.op("pool", lambda e, o=self.sq[:, q, :], a=src[j]: e.tensor_tensor(o, a, a, ALU.mult),
                     reads=[b_src[j]], writes=[self.b_sq[q]])
            s.op("pe", lambda e, o=ps[:, 0:NT], r=self.sq[:, q, :], st=(j == 0), sp=(j == nchunks - 1):
                 e.matmul(o, self.ones_bf[:, :], r, start=st, stop=sp),
                 reads=[self.b_sq[q], self.b_const], writes=[bps])
        s.op("act", lambda e, o=self.rstd[:, :], a=ps[:, 0:NT]: e.activation(out=o, in_=a, func=AF.Sqrt,
                                                                        bias=self.cst[:, 0:1], scale=1.0 / denom),
             reads=[bps, self.b_const], writes=[self.b_rstd])
        s.op("dve", lambda e, o=self.rstd[:, :]: e.reciprocal(o, o), reads=[self.b_rstd], writes=[self.b_rstd])

    def pre_norm(self, li, which):
        s = self.s
        self.sumsq_rstd([self.xt[:, j, :] for j in range(KC)], self.b_x, KC, float(D))
        for j in range(KC):
            q = j % 2
            s.op("dve", lambda e, o=self.tmpn[:, q, :], a=self.xt[:, j, :]: e.tensor_tensor(o, a, self.rstd[:, :], ALU.mult),
                 reads=[self.b_x[j], self.b_rstd], writes=[self.b_tmpn[q]])
            s.op("act", lambda e, o=self.ht[:, j, :], a=self.tmpn[:, q, :], sc=self.vec[:, li, which * 3 + 0, j:j + 1],
                 bi=self.vec[:, li, which * 3 + 1, j:j + 1]: e.activation(out=o, in_=a, func=AF.Identity, bias=bi, scale=sc),
                 reads=[self.b_tmpn[q], self.b_vec], writes=[self.b_h[j]])

    def post_norm_residual(self, li, which):
        s = self.s
        self.sumsq_rstd([self.yt[:, j, :] for j in range(KC)], self.b_y, KC, float(D))
        for j in range(KC):
            q = j % 2
            eng = "dve" if j % 2 == 0 else "pool"
            s.op(eng, lambda e, o=self.tmpn[:, q, :], a=self.yt[:, j, :]: e.tensor_tensor(o, a, self.rstd[:, :], ALU.mult),
                 reads=[self.b_y[j], self.b_rstd], writes=[self.b_tmpn[q]])
            s.op("dve", lambda e, o=self.xt[:, j, :], a=self.tmpn[:, q, :], g=self.vec[:, li, which * 3 + 2, j:j + 1]:
                 e.scalar_tensor_tensor(o, a, g, o, ALU.mult, ALU.add),
                 reads=[self.b_tmpn[q], self.b_vec], writes=[self.b_x[j]])

    def ffn(self, li):
        s = self.s
        win = "ffn_w_in%d" % li
        wout = "ffn_w_out%d" % li
        self.pre_norm(li, 1)
        for mp in range(0, FC, 4):
            nm = min(4, FC - mp)
            wg, bwg = self.wpanel(win, KC, mp * 128, nm * 128)
            wu, bwu = self.wpanel(win, KC, FFN_H + mp * 128, nm * 128)
            for mi in range(nm):
                m = mp + mi
                pg, bpg = self.psum()
                pu, bpu = self.psum()
                for k in range(KC):
                    s.op("pe", lambda e, o=pg[:, 0:NT], w=wg[:, k, mi * 128:(mi + 1) * 128], r=self.ht[:, k, :], st=(k == 0), sp=(k == KC - 1):
                         e.matmul(o, w, r, start=st, stop=sp), reads=[bwg, self.b_h[k]], writes=[bpg])
                for k in range(KC):
                    s.op("pe", lambda e, o=pu[:, 0:NT], w=wu[:, k, mi * 128:(mi + 1) * 128], r=self.ht[:, k, :], st=(k == 0), sp=(k == KC - 1):
                         e.matmul(o, w, r, start=st, stop=sp), reads=[bwu, self.b_h[k]], writes=[bpu])
                q = m % 2
                s.op("act", lambda e, o=self.sg[:, q, :], a=pg[:, 0:NT]: e.activation(out=o, in_=a, func=AF.Silu),
                     reads=[bpg], writes=[self.b_sg[q]])
                s.op("dve", lambda e, o=self.act[:, m, :], a=pu[:, 0:NT], b=self.sg[:, q, :]: e.tensor_tensor(o, a, b, ALU.mult),
                     reads=[bpu, self.b_sg[q]], writes=[self.b_act[m]])
        for m in range(KC):
            w, bw = self.wpanel(wout, FC, m * 128, 128)
            po, bpo = self.psum()
            for k in range(FC):
                s.op("pe", lambda e, o=po[:, 0:NT], w_=w[:, k, :], r=self.act[:, k, :], st=(k == 0), sp=(k == FC - 1):
                     e.matmul(o, w_, r, start=st, stop=sp), reads=[bw, self.b_act[k]], writes=[bpo])
            s.op("act", lambda e, o=self.yt[:, m, :], a=po[:, 0:NT]: e.activation(out=o, in_=a, func=AF.Copy),
                 reads=[bpo], writes=[self.b_y[m]])
        self.post_norm_residual(li, 1)

    def _tensors_mix(self):
        nc, s, L = self.nc, self.s, self.L
        I = "ExternalInput"
        self.d_cmask = self.dram("cmask", [128, 11, 128], F32, I)
        self.d_sel = self.dram("sel", [16, 16 * 128], F32, I)
        self.d_convw = self.dram("convw", [128, 12, 4], F32, I)
        self.d_fvec = self.dram("fvec", [128, 5, 12], F32, I)
        self.d_ssd16 = self.dram("ssd16", [16, 2], F32, I)
        self.d_s5v1 = self.dram("s5v1", [128, 3, 32], F32, I)
        self.d_s5v2 = self.dram("s5v2", [128, 5, 1024], F32, I)
        self.d_s5wc = self.dram("s5wc", [128, 32 * 2 * 128], F32, I)
        self.Kd = self.dram("Kd", [8, 128, 2048 + L], BF16)
        self.Vd = self.dram("Vd", [8, 128, 2048 + L], BF16)
        self.b_Kd = s.buf("Kd"); self.b_Vd = s.buf("Vd")
        self.cmask = self.sb("cmask_sb", [128, 11, 128], F32)
        self.ident_bf = self.sb("ident_bf", [128, 128], BF16)
        self.sel = self.sb("sel_sb", [16, 16, 128], F32)
        self.ones16 = self.sb("ones16", [16, 128], F32)
        self.convw = self.sb("convw_sb", [128, 12, 4], F32)
        self.fvec = self.sb("fvec_sb", [128, 5, 12], F32)
        self.ssd16 = self.sb("ssd16_sb", [16, 4], F32)
        self.A32 = self.sb("A32", [128, 7424], F32); self.b_A32 = s.buf("A32")
        self.A16 = self.sb("A16", [128, 19456], BF16); self.b_A16 = s.buf("A16")
        self.Hst = self.sb("Hst", [128, 1024], F32); self.b_Hst = [s.buf("Hst%d" % h) for h in range(16)]
        self.Hbf = self.sb("Hbf", [128, 1024], BF16); self.b_Hbf = [s.buf("Hbf%d" % h) for h in range(16)]
        self.ctail = self.sb("ctail", [128, 12, 3], F32); self.b_ctail = s.buf("ctail")
        self.dtt = self.sb("dtt", [16, 5, NT], F32); self.b_dtt = [s.buf("dtt%d" % i) for i in range(5)]
        self.tok = self.sb("tok", [128, 2, 48], F32); self.b_tok = [s.buf("tok0"), s.buf("tok1")]
        self.xdt = self.sb("xdt", [128, 2, 1024], BF16); self.b_xdt = [s.buf("xdt"), s.buf("xdtw")]
        self.Btok = self.sb("Btok", [128, 256], BF16); self.b_Btok = s.buf("Btok")
        self.cbm = self.sb("cbm", [128, 256], F32); self.b_cbm = s.buf("cbm")
        self.D1 = self.sb("D1", [128, 2, 128], F32); self.b_D1 = [s.buf("D1a"), s.buf("D1b")]
        self.Eh = self.sb("Eh", [128, 2, 128], F32); self.b_Eh = [s.buf("Eha"), s.buf("Ehb")]
        self.Mh = self.sb("Mh", [128, 2, 128], BF16); self.b_Mh = [s.buf("Mha"), s.buf("Mhb")]
        self.Csh = self.sb("Csh", [128, 2, 128], BF16); self.b_Csh = [s.buf("Csa"), s.buf("Csb")]
        self.Wc = self.sb("Wc", [128, 32, 2, 128], BF16)
        self.Wb = self.sb("Wb", [128, 8, 2, 128], BF16)
        self.s5c = self.sb("s5c", [128, 4, 32], F32)
        self.Hall = self.sb("Hall", [128, 2, 32, 33], F32); self.b_Hall = [s.buf("Hall0"), s.buf("Hall1")]
        self.Hs16 = self.sb("Hs16", [128, 2, 32, 32], BF16); self.b_Hs16 = s.buf("Hs16")
        self.t12 = self.sb("t12", [128, 2, 2, 2, 16], F32); self.b_t12 = [s.buf("t12a"), s.buf("t12b")]
        self.b_s5w = s.buf("s5w")
        a32 = self.A32
        self.convbuf = a32[:, 0:12 * (NT + 3)].rearrange("p (j t) -> p j t", j=12)
        o = 12 * (NT + 3)
        self.cacc = a32[:, o:o + 2 * NT].rearrange("p (j t) -> p j t", j=2); o += 2 * NT
        self.ys = a32[:, o:o + 8 * NT].rearrange("p (j t) -> p j t", j=8); o += 8 * NT
        self.yg = a32[:, 0:8 * NT].rearrange("p (j t) -> p j t", j=8)
        assert o <= 7424, o
        a16 = self.A16
        o = 0
        def v16(n, j):
            nonlocal o
            r = a16[:, o:o + n].rearrange("p (j t) -> p j t", j=j)
            o += n
            return r
        self.zs = v16(8 * NT, 8)
        self.xsb = v16(8 * NT, 8)
        self.Bfm = v16(2 * NT, 2)
        self.Cfm = v16(2 * NT, 2)
        self.ubf = v16(8 * NT, 8)
        self.cat = v16(16 * NT, 16)
        self.g5b = v16(8 * NT, 8)
        assert o <= 19456, o
        self.b_zs = [s.buf() for _ in range(8)]
        self.b_xsb = [s.buf() for _ in range(8)]
        self.b_BC = [s.buf() for _ in range(4)]
        self.b_ubf = [s.buf() for _ in range(8)]
        self.b_cat = [s.buf() for _ in range(16)]
        self.b_g5b = [s.buf() for _ in range(8)]
        self.b_conv = [s.buf() for _ in range(12)]
        self.b_cacc = [s.buf(), s.buf()]
        self.b_ys = s.buf()
        self.b_yg = [s.buf() for _ in range(8)]
        o = 0
        self.Qb = v16(24 * NT, 24)
        self.kvst = v16(2 * NT, 2)
        self.Kw = a16[:, o:o + 2048 + NT]; o += 2048 + NT
        self.Vw = a16[:, o:o + 2048 + NT]; o += 2048 + NT
        self.attn = v16(8 * NT, 8)
        self.Vtok = v16(43 * 128, 43)
        self.PT = v16(4 * 128, 4)
        assert o <= 19456, o
        self.b_Qb = [s.buf() for _ in range(24)]
        self.b_kvst = [s.buf(), s.buf()]
        self.b_Kw = s.buf(); self.b_Vw = s.buf()
        self.b_attn = [s.buf() for _ in range(8)]
        self.b_Vtok = [s.buf() for _ in range(6)]
        self.b_PT = [s.buf() for _ in range(4)]
        self.Oacc = a32[:, 0:2 * NT].rearrange("p (j t) -> p j t", j=2)
        self.b_Oacc = [s.buf(), s.buf()]
        self.pt_rr = 0

    def setup_mix(self):
        s = self.s
        bc = self.b_const
        s.dma(lambda e: e.dma_start(out=self.cmask[:, :, :], in_=self.d_cmask), writes=[bc])
        s.dma(lambda e: e.dma_start(out=self.sel[:, :, :], in_=self.d_sel.rearrange("k (h m) -> k h m", h=16)), writes=[bc])
        s.dma(lambda e: e.dma_start(out=self.convw[:, :, :], in_=self.d_convw), writes=[bc])
        s.dma(lambda e: e.dma_start(out=self.fvec[:, :, :], in_=self.d_fvec), writes=[bc])
        s.dma(lambda e: e.dma_start(out=self.ssd16[:, 0:2], in_=self.d_ssd16), writes=[bc])
        s.op("pool", lambda e: e.memset(self.ones16[:, :], 1.0), writes=[bc])
        s.op("dve", lambda e: e.tensor_copy(self.ident_bf[:, :], self.cmask[:, 0, :]), reads=[bc], writes=[bc])
        s.op("act", lambda e: e.activation(out=self.ssd16[:, 2:3], in_=self.ssd16[:, 1:2], func=AF.Exp), reads=[bc], writes=[bc])
        s.op("dve", lambda e: e.tensor_scalar(self.ssd16[:, 2:3], self.ssd16[:, 2:3], -1.0, None, ALU.mult), reads=[bc], writes=[bc])
        for h in range(16):
            s.op("pool", lambda e, h=h: e.memset(self.Hst[:, h * 64:(h + 1) * 64], 0.0), writes=[self.b_Hst[h]])
            s.op("pool", lambda e, h=h: e.memset(self.Hbf[:, h * 64:(h + 1) * 64], 0.0), writes=[self.b_Hbf[h]])
        s.op("pool", lambda e: e.memset(self.ctail[:, :, :], 0.0), writes=[self.b_ctail])
        for hf in range(2):
            s.op("pool", lambda e, hf=hf: e.memset(self.Hall[:, :, hf * 16:(hf + 1) * 16, :], 0.0), writes=[self.b_Hall[hf]])
        self.setup_s5()
        if 1 in self.layers:
            s.op("pool", lambda e: e.memset(self.wp[:, 0, 0:2048], 0.0), writes=[self.b_wp[0]])
            for h in range(8):
                s.dma(lambda e, h=h: e.dma_start(out=self.Kd[h, :, 0:2048], in_=self.wp[:, 0, 0:2048]), reads=[self.b_wp[0]], writes=[self.b_Kd])
                s.dma(lambda e, h=h: e.dma_start(out=self.Vd[h, :, 0:2048], in_=self.wp[:, 0, 0:2048]), reads=[self.b_wp[0]], writes=[self.b_Vd])

    def setup_s5(self):
        s = self.s
        ba = self.b_A32
        PI = float(np.pi)
        sc = self.A32
        I32 = mybir.dt.int32

        def dve(fn, reads=(), writes=()):
            s.op("dve", fn, reads=[ba] + list(reads), writes=[ba] + list(writes))

        def act(fn):
            s.op("act", fn, reads=[ba], writes=[ba])

        def coeffs(lr, li, ld, N, base):
            t = [sc[:, base + i * N: base + (i + 1) * N] for i in range(8)]
            dt, mag, ang, r, kf, m, ar, ai = t
            ki = kf.bitcast(I32)
            act(lambda e: e.activation(out=dt, in_=ld, func=AF.Exp))
            dve(lambda e: e.tensor_tensor(mag, lr, dt, ALU.mult))
            act(lambda e: e.activation(out=mag, in_=mag, func=AF.Exp))
            dve(lambda e: e.tensor_tensor(ang, li, dt, ALU.mult))

            def reduce_sin(shift, out):
                dve(lambda e: e.tensor_scalar(r, ang, shift, None, ALU.add))
                dve(lambda e: e.tensor_scalar(m, r, 1.0 / (2 * PI), None, ALU.mult))
                dve(lambda e: e.tensor_copy(ki, m))
                dve(lambda e: e.tensor_copy(m, ki))
                dve(lambda e: e.scalar_tensor_tensor(r, m, -2 * PI, r, ALU.mult, ALU.add))
                dve(lambda e: e.tensor_scalar(m, r, PI, None, ALU.is_gt))
                dve(lambda e: e.scalar_tensor_tensor(r, m, -2 * PI, r, ALU.mult, ALU.add))
                dve(lambda e: e.tensor_scalar(m, r, -PI, None, ALU.is_lt))
                dve(lambda e: e.scalar_tensor_tensor(r, m, 2 * PI, r, ALU.mult, ALU.add))
                act(lambda e: e.activation(out=out, in_=r, func=AF.Sin))

            reduce_sin(0.0, ai)
            reduce_sin(PI / 2, ar)
            dve(lambda e: e.tensor_tensor(ar, ar, mag, ALU.mult))
            dve(lambda e: e.tensor_tensor(ai, ai, mag, ALU.mult))
            return ar, ai

        v1 = sc[:, 0:96].rearrange("p (a q) -> p a q", a=3)
        s.dma(lambda e: e.dma_start(out=v1, in_=self.d_s5v1), writes=[ba])
        ar, ai = coeffs(v1[:, 0, :], v1[:, 1, :], v1[:, 2, :], 32, 128)
        dve(lambda e, ar=ar: e.tensor_copy(self.s5c[:, 0, :], ar), writes=[self.b_s5w])
        dve(lambda e, ar=ar: e.tensor_copy(self.s5c[:, 1, :], ar), writes=[self.b_s5w])
        dve(lambda e, ai=ai: e.tensor_copy(self.s5c[:, 2, :], ai), writes=[self.b_s5w])
        dve(lambda e, ai=ai: e.tensor_scalar(self.s5c[:, 3, :], ai, -1.0, None, ALU.mult), writes=[self.b_s5w])
        N = 128
        for fc in range(8):
            inp = [sc[:, i * N:(i + 1) * N] for i in range(5)]
            for i in range(5):
                s.dma(lambda e, i=i, fc=fc: e.dma_start(out=inp[i], in_=self.d_s5v2[:, i, fc * 128:(fc + 1) * 128]), writes=[ba])
            lr, li, ld, Bre, Bim = inp
            ar, ai = coeffs(lr, li, ld, N, 5 * N)
            den, am1, qre, qim, t1, t2 = [sc[:, (13 + i) * N:(14 + i) * N] for i in range(6)]
            dve(lambda e: e.tensor_tensor(den, lr, lr, ALU.mult))
            dve(lambda e: e.tensor_tensor(t1, li, li, ALU.mult))
            dve(lambda e: e.tensor_tensor(den, den, t1, ALU.add))
            dve(lambda e: e.reciprocal(den, den))
            dve(lambda e: e.tensor_scalar(am1, ar, -1.0, None, ALU.add))
            dve(lambda e: e.tensor_tensor(t1, am1, lr, ALU.mult))
            dve(lambda e: e.tensor_tensor(t2, ai, li, ALU.mult))
            dve(lambda e: e.tensor_tensor(qre, t1, t2, ALU.add))
            dve(lambda e: e.tensor_tensor(qre, qre, den, ALU.mult))
            dve(lambda e: e.tensor_tensor(t1, ai, lr, ALU.mult))
            dve(lambda e: e.tensor_tensor(t2, am1, li, ALU.mult))
            dve(lambda e: e.tensor_tensor(qim, t1, t2, ALU.subtract))
            dve(lambda e: e.tensor_tensor(qim, qim, den, ALU.mult))
            dve(lambda e: e.tensor_tensor(t1, qre, Bre, ALU.mult))
            dve(lambda e: e.tensor_tensor(t2, qim, Bim, ALU.mult))
            dve(lambda e, fc=fc: e.tensor_tensor(self.Wb[:, fc, 0, :], t1, t2, ALU.subtract), writes=[self.b_s5w])
            dve(lambda e: e.tensor_tensor(t1, qre, Bim, ALU.mult))
            dve(lambda e: e.tensor_tensor(t2, qim, Bre, ALU.mult))
            dve(lambda e, fc=fc: e.tensor_tensor(self.Wb[:, fc, 1, :], t1, t2, ALU.add), writes=[self.b_s5w])
        for q in range(32):
            st = sc[:, 0:256].rearrange("p (r m) -> p r m", r=2)
            s.dma(lambda e, q=q: e.dma_start(out=sc[:, 0:256], in_=self.d_s5wc[:, q * 256:(q + 1) * 256]), writes=[ba])
            dve(lambda e, q=q: e.tensor_copy(self.Wc[:, q, 0, :], st[:, 0, :]), writes=[self.b_s5w])
            dve(lambda e, q=q: e.tensor_scalar(self.Wc[:, q, 1, :], st[:, 1, :], -1.0, None, ALU.mult), writes=[self.b_s5w])

    def evac(self, eng, out, in_, reads, writes):
        s = self.s
        if eng == "act":
            s.op("act", lambda e: e.activation(out=out, in_=in_, func=AF.Copy), reads=reads, writes=writes)
        else:
            s.op(eng, lambda e: e.tensor_copy(out, in_), reads=reads, writes=writes)

    def proj(self, wname, c0, ncols, nk, rhs_fn, rhs_bufs, on_chunk, k0=0, panel_cols=512):
        s = self.s
        pc = min(panel_cols, (4096 // nk) // 128 * 128) if ncols >= 128 else ncols
        mi = 0
        for p0 in range(0, ncols, pc):
            pw = min(pc, ncols - p0)
            w, bw = self.wpanel(wname, nk, c0 + p0, pw, k0=k0)
            for m0 in range(0, pw, 128):
                mw = min(128, pw - m0)
                ps, bps = self.psum()
                for k in range(nk):
                    s.op("pe", lambda e, o=ps[0:mw, 0:NT], w_=w[:, k, m0:m0 + mw], r=rhs_fn(k), st=(k == 0), sp=(k == nk - 1):
                         e.matmul(o, w_, r, start=st, stop=sp), reads=[bw, rhs_bufs[k]], writes=[bps])
                on_chunk(mi, mw, ps, bps)
                mi += 1

    def mixer0(self, ti):
        s = self.s
        fv = self.fvec
        self.pre_norm(0, 0)
        hrhs = lambda k: self.ht[:, k, :]
        def on_z(mi, mw, ps, bps):
            s.op("act", lambda e: e.activation(out=self.zs[:, mi, :], in_=ps[:, 0:NT], func=AF.Silu),
                 reads=[bps], writes=[self.b_zs[mi], self.b_A16])
        self.proj("hyb_w_in", 0, 1024, KC, hrhs, self.b_h, on_z)

        def on_xbc(mi, mw, ps, bps):
            self.evac("dve", self.convbuf[:, mi, 3:3 + NT], ps[:, 0:NT], [bps], [self.b_conv[mi], self.b_A32])
        self.proj("hyb_w_in", 1024, 1536, KC, hrhs, self.b_h, on_xbc)

        def on_dt(mi, mw, ps, bps):
            s.op("act", lambda e: e.activation(out=self.dtt[:, 0, :], in_=ps[0:16, 0:NT], func=AF.Exp, bias=self.ssd16[:, 0:1], scale=1.0),
                 reads=[bps, self.b_const], writes=[self.b_dtt[0]])
        self.proj("hyb_w_in", 2560, 16, KC, hrhs, self.b_h, on_dt)

        def on_u(mi, mw, ps, bps):
            self.evac("act", self.ubf[:, mi, :], ps[:, 0:NT], [bps], [self.b_ubf[mi]])
        self.proj("hyb_w_in", 2576, 1024, KC, hrhs, self.b_h, on_u)

        for j in range(12):
            cb = self.convbuf
            q = j % 2
            acc = self.cacc[:, q, :]
            s.op("dve", lambda e, j=j: e.tensor_copy(cb[:, j, 0:3], self.ctail[:, j, :]), reads=[self.b_ctail], writes=[self.b_conv[j]])
            s.op("dve", lambda e, j=j, acc=acc: e.tensor_scalar(acc, cb[:, j, 0:NT], self.convw[:, j, 0:1], None, ALU.mult),
                 reads=[self.b_conv[j], self.b_const], writes=[self.b_cacc[q]])
            for k in range(1, 4):
                s.op("dve", lambda e, j=j, k=k, acc=acc: e.scalar_tensor_tensor(acc, cb[:, j, k:k + NT], self.convw[:, j, k:k + 1], acc, ALU.mult, ALU.add),
                     reads=[self.b_conv[j], self.b_const], writes=[self.b_cacc[q]])
            s.op("dve", lambda e, j=j: e.tensor_copy(self.ctail[:, j, :], cb[:, j, NT:NT + 3]), reads=[self.b_conv[j]], writes=[self.b_ctail])
            if j < 8:
                dst, bd = self.xsb[:, j, :], self.b_xsb[j]
            elif j < 10:
                dst, bd = self.Bfm[:, j - 8, :], self.b_BC[j - 8]
            else:
                dst, bd = self.Cfm[:, j - 10, :], self.b_BC[j - 8]
            s.op("act", lambda e, j=j, acc=acc, dst=dst: e.activation(out=dst, in_=acc, func=AF.Silu, bias=fv[:, 0, j:j + 1], scale=1.0),
                 reads=[self.b_cacc[q], self.b_const], writes=[bd])

        dt_e, dtv, a_, ac, dtw = [self.dtt[:, i, :] for i in range(5)]
        bd = self.b_dtt
        s.op("act", lambda e: e.activation(out=dtv, in_=dt_e, func=AF.Ln, bias=self.cst[0:16, 1:2], scale=1.0),
             reads=[bd[0], self.b_const], writes=[bd[1]])
        s.op("dve", lambda e: e.tensor_scalar(a_, dtv, self.ssd16[:, 2:3], None, ALU.mult), reads=[bd[1], self.b_const], writes=[bd[2]])
        NCH = NT // 128
        for c in range(NCH):
            sl = slice(c * 128, (c + 1) * 128)
            s.op("dve", lambda e, sl=sl: e.tensor_tensor_scan(ac[:, sl], self.ones16[:, :], a_[:, sl], 0.0, ALU.mult, ALU.add),
                 reads=[bd[2], self.b_const], writes=[bd[3]])
        for c in range(NCH):
            sl = slice(c * 128, (c + 1) * 128)
            s.op("act", lambda e, sl=sl, c=c: e.activation(out=dt_e[:, sl], in_=ac[:, sl], func=AF.Exp, bias=ac[:, c * 128 + 127:c * 128 + 128], scale=-1.0),
                 reads=[bd[3]], writes=[bd[0]])
        s.op("dve", lambda e: e.tensor_tensor(dtw, dtv, dt_e, ALU.mult), reads=[bd[0], bd[1]], writes=[bd[4]])

        idf = self.cmask[0:16, 0, 0:16]
        for c in range(NCH):
            self.ssd_chunk(c, ac, dtv, dtw, bd, idf)
        for g in range(2):
            self.sumsq_rstd([self.yg[:, 4 * g + jj, :] for jj in range(4)], self.b_yg[4 * g:4 * g + 4], 4, 512.0)
            for jj in range(4):
                j = 4 * g + jj
                q = j % 2
                s.op("dve", lambda e, j=j, q=q: e.tensor_tensor(self.tmpn[:, q, :], self.yg[:, j, :], self.rstd[:, :], ALU.mult),
                     reads=[self.b_yg[j], self.b_rstd], writes=[self.b_tmpn[q]])
                s.op("act", lambda e, j=j, q=q: e.activation(out=self.cat[:, j, :], in_=self.tmpn[:, q, :], func=AF.Copy, scale=fv[:, 2, j:j + 1]),
                     reads=[self.b_tmpn[q], self.b_const], writes=[self.b_cat[j]])
        self.s5_tile()
        if "cat" in self.dbg and ti == 0:
            self.d_dbg = self.dram("dbg_cat", [128, 16, NT], BF16, "ExternalOutput")
            s.dma(lambda e: e.dma_start(out=self.d_dbg, in_=self.cat), reads=self.b_cat)
            self.d_dbg2 = self.dram("dbg_yg", [128, 8, NT], F32, "ExternalOutput")
            s.dma(lambda e: e.dma_start(out=self.d_dbg2, in_=self.yg), reads=self.b_yg)
            self.d_dbg4 = self.dram("dbg_xsb", [128, 8, NT], BF16, "ExternalOutput")
            s.dma(lambda e: e.dma_start(out=self.d_dbg4, in_=self.xsb), reads=self.b_xsb)
            self.d_dbg5 = self.dram("dbg_bc", [128, 4, NT], BF16, "ExternalOutput")
            s.dma(lambda e: e.dma_start(out=self.d_dbg5[:, 0:2, :], in_=self.Bfm), reads=self.b_BC)
            s.dma(lambda e: e.dma_start(out=self.d_dbg5[:, 2:4, :], in_=self.Cfm), reads=self.b_BC)
            self.d_dbg6 = self.dram("dbg_dtt", [16, 5, NT], F32, "ExternalOutput")
            s.dma(lambda e: e.dma_start(out=self.d_dbg6, in_=self.dtt[:, :, :]), reads=self.b_dtt)
            self.d_dbg3 = self.dram("dbg_ys", [128, 8, NT], F32, "ExternalOutput")
            s.dma(lambda e: e.dma_start(out=self.d_dbg3, in_=self.ys), reads=[self.b_ys])
            s.barrier()
        def on_o(mi, mw, ps, bps):
            self.evac("act", self.yt[:, mi, :], ps[:, 0:NT], [bps], [self.b_y[mi]])
        self.proj("hyb_w_out", 0, 1024, 16, lambda k: self.cat[:, k, :], self.b_cat, on_o, panel_cols=256)
        self.post_norm_residual(0, 0)

    def ssd_chunk(self, c, ac, dtv, dtw, bd, idf):
        s = self.s
        fv = self.fvec
        sl = slice(c * 128, (c + 1) * 128)
        tq = c % 2
        tok = self.tok[:, tq, :]
        btok = self.b_tok[tq]
        pt, bpt = self.psum()
        for i, (src, bsrc) in enumerate(((ac, bd[3]), (dtv, bd[1]), (dtw, bd[4]))):
            s.op("pe", lambda e, i=i, src=src, sl=sl: e.transpose(pt[:, i * 16:(i + 1) * 16], src[:, sl], idf),
                 reads=[bsrc, self.b_const], writes=[bpt])
        self.evac("dve", tok, pt[:, 0:48], [bpt], [btok])
        px, bpx = self.psum()
        pxb = px[:, :].bitcast(BF16)
        for j in range(8):
            s.op("pe", lambda e, j=j, sl=sl: e.transpose(pxb[:, j * 128:(j + 1) * 128], self.xsb[:, j, sl], self.ident_bf[:, :]),
                 reads=[self.b_xsb[j], self.b_const], writes=[bpx])
        for i in range(2):
            s.op("dve", lambda e, i=i: e.tensor_tensor(self.xdt[:, i, :].rearrange("p (h d) -> p h d", h=16),
                                                       pxb.rearrange("p (h d) -> p h d", h=16),
                                                       tok[:, 16 * (i + 1):16 * (i + 2)].unsqueeze(2).to_broadcast([128, 16, 64]), ALU.mult),
                 reads=[bpx, btok], writes=[self.b_xdt[i]])
        pb, bpb = self.psum()
        pbb = pb[:, :].bitcast(BF16)
        for g in range(2):
            s.op("pe", lambda e, g=g, sl=sl: e.transpose(pbb[:, g * 128:(g + 1) * 128], self.Bfm[:, g, sl], self.ident_bf[:, :]),
                 reads=[self.b_BC[g], self.b_const], writes=[bpb])
        self.evac("act", self.Btok[:, :], pbb[:, 0:256], [bpb], [self.b_Btok])
        pc, bpc = self.psum()
        for g in range(2):
            s.op("pe", lambda e, g=g, sl=sl: e.matmul(pc[:, g * 128:(g + 1) * 128], self.Bfm[:, g, sl], self.Cfm[:, g, sl], start=True, stop=True),
                 reads=[self.b_BC[g], self.b_BC[2 + g]], writes=[bpc])
        s.op("dve", lambda e: e.tensor_tensor(self.cbm[:, :].rearrange("p (g l) -> p g l", g=2), pc[:, 0:256].rearrange("p (g l) -> p g l", g=2),
                                              self.cmask[:, 1, :].unsqueeze(1).to_broadcast([128, 2, 128]), ALU.mult),
             reads=[bpc, self.b_const], writes=[self.b_cbm])
        pst = []
        for g in range(2):
            p_, bp_ = self.psum_fixed(4 + g)
            s.op("pe", lambda e, g=g, p_=p_: e.matmul(p_[:, :], self.Btok[:, g * 128:(g + 1) * 128], self.xdt[:, 1, g * 512:(g + 1) * 512], start=True, stop=True),
                 reads=[self.b_Btok, self.b_xdt[1]], writes=[bp_])
            pst.append((p_, bp_))
        py = None
        for h in range(16):
            g = h // 8
            hq = h % 2
            if h % 8 == 0:
                py, bpy = self.psum_fixed(6)
            j4 = (h % 8) // 2
            pa, bpa = self.psum()
            s.op("pe", lambda e, h=h, sl=sl, pa=pa: e.matmul(pa[:, 0:128], self.sel[:, h, :], ac[:, sl], start=True, stop=True),
                 reads=[bd[3], self.b_const], writes=[bpa])
            s.op("dve", lambda e, h=h, hq=hq, pa=pa: e.tensor_scalar(self.D1[:, hq, :], pa[:, 0:128], tok[:, h:h + 1], 0.0, ALU.subtract, ALU.min),
                 reads=[bpa, btok], writes=[self.b_D1[hq]])
            s.op("act", lambda e, hq=hq: e.activation(out=self.D1[:, hq, :], in_=self.D1[:, hq, :], func=AF.Exp),
                 reads=[self.b_D1[hq]], writes=[self.b_D1[hq]])
            s.op("pool", lambda e, hq=hq, g=g: e.tensor_tensor(self.Mh[:, hq, :], self.D1[:, hq, :], self.cbm[:, g * 128:(g + 1) * 128], ALU.mult),
                 reads=[self.b_D1[hq], self.b_cbm], writes=[self.b_Mh[hq]])
            s.op("act", lambda e, hq=hq, pa=pa: e.activation(out=self.Eh[:, hq, :], in_=pa[:, 0:128], func=AF.Exp),
                 reads=[bpa], writes=[self.b_Eh[hq]])
            s.op("pool", lambda e, hq=hq, g=g, sl=sl: e.tensor_tensor(self.Csh[:, hq, :], self.Cfm[:, g, sl], self.Eh[:, hq, :], ALU.mult),
                 reads=[self.b_BC[2 + g], self.b_Eh[hq]], writes=[self.b_Csh[hq]])
            yo = py[hq * 64:(hq + 1) * 64, j4 * 128:(j4 + 1) * 128]
            s.op("pe", lambda e, h=h, hq=hq, yo=yo: e.matmul(yo, self.xdt[:, 0, h * 64:(h + 1) * 64], self.Mh[:, hq, :], start=True, stop=False),
                 reads=[self.b_xdt[0], self.b_Mh[hq]], writes=[bpy])
            s.op("pe", lambda e, h=h, hq=hq, yo=yo: e.matmul(yo, self.Hbf[:, h * 64:(h + 1) * 64], self.Csh[:, hq, :], start=False, stop=True),
                 reads=[self.b_Hbf[h], self.b_Csh[hq]], writes=[bpy])
            p_, bp_ = pst[g]
            hs = slice(h * 64, (h + 1) * 64)
            s.op("dve", lambda e, hs=hs, hq=hq, p_=p_, h=h: e.scalar_tensor_tensor(self.Hst[:, hs], self.Hst[:, hs], self.Eh[:, hq, 127:128],
                                                                                p_[:, (h % 8) * 64:(h % 8 + 1) * 64], ALU.mult, ALU.add),
                 reads=[self.b_Eh[hq], bp_], writes=[self.b_Hst[h]])
            s.op("pool", lambda e, hs=hs: e.tensor_copy(self.Hbf[:, hs], self.Hst[:, hs]), reads=[self.b_Hst[h]], writes=[self.b_Hbf[h]])
            if h % 8 == 7:
                for jj in range(4):
                    j = 4 * g + jj
                    s.op("dve", lambda e, j=j, jj=jj, sl=sl, py=py: e.scalar_tensor_tensor(self.yg[:, j, sl], self.xsb[:, j, sl], fv[:, 1, j:j + 1],
                                                                                             py[:, jj * 128:(jj + 1) * 128], ALU.mult, ALU.add),
                         reads=[self.b_xsb[j], bpy, self.b_const], writes=[self.b_yg[j]] + self.b_conv)
                    s.op("pool", lambda e, j=j, sl=sl: e.tensor_tensor(self.yg[:, j, sl], self.yg[:, j, sl], self.zs[:, j, sl], ALU.mult),
                         reads=[self.b_zs[j]], writes=[self.b_yg[j]])

    def s5_sub(self, ts, T):
        s = self.s
        for b in range(4):
            ri = b // 2
            pb, bpb = self.psum()
            for slot in range(16):
                q = (b % 2) * 16 + slot
                fc, rb = q // 4, q % 4
                s.op("pe", lambda e, pb=pb, slot=slot, fc=fc, rb=rb, ri=ri: e.matmul(
                    pb[:, slot * T:(slot + 1) * T], self.Wb[32 * rb:32 * rb + 32, fc, ri, :], self.ubf[32 * rb:32 * rb + 32, fc, ts:ts + T],
                    start=True, stop=True, tile_position=(32 * rb, 0)), reads=[self.b_s5w, self.b_ubf[fc]], writes=[bpb])
            hf = b % 2
            s.op("act", lambda e, pb=pb, ri=ri, hf=hf: e.activation(out=self.Hall[:, ri, hf * 16:(hf + 1) * 16, 1:1 + T],
                                                                  in_=pb[:, :].rearrange("p (q t) -> p q t", q=16), func=AF.Copy),
                 reads=[bpb], writes=[self.b_Hall[hf]])
        for hf, eng in ((0, "dve"), (1, "pool")):
            qs = slice(hf * 16, (hf + 1) * 16)
            t1 = self.t12[:, hf, 0, :, :]
            t2 = self.t12[:, hf, 1, :, :]
            bh = self.b_Hall[hf]
            bt = self.b_t12[hf]
            for t in range(1, T + 1):
                s.op(eng, lambda e, t=t, t1=t1, qs=qs: e.tensor_tensor(t1, self.s5c[:, 0:2, qs], self.Hall[:, :, qs, t - 1], ALU.mult),
                     reads=[bh, self.b_s5w], writes=[bt])
                s.op(eng, lambda e, t=t, t2=t2, qs=qs: e.tensor_tensor(t2[:, 0, :], self.s5c[:, 3, qs], self.Hall[:, 1, qs, t - 1], ALU.mult),
                     reads=[bh, self.b_s5w], writes=[bt])
                s.op(eng, lambda e, t=t, t2=t2, qs=qs: e.tensor_tensor(t2[:, 1, :], self.s5c[:, 2, qs], self.Hall[:, 0, qs, t - 1], ALU.mult),
                     reads=[bh, self.b_s5w], writes=[bt])
                s.op(eng, lambda e, t1=t1, t2=t2: e.tensor_tensor(t1, t1, t2, ALU.add), reads=[bt], writes=[bt])
                s.op(eng, lambda e, t=t, t1=t1, qs=qs: e.tensor_tensor(self.Hall[:, :, qs, t], self.Hall[:, :, qs, t], t1, ALU.add),
                     reads=[bt], writes=[bh])
        s.op("act", lambda e: e.activation(out=self.Hs16[:, :, :, :], in_=self.Hall[:, :, :, 1:1 + T], func=AF.Copy),
             reads=self.b_Hall, writes=[self.b_Hs16])
        for hf, eng in ((0, "dve"), (1, "pool")):
            qs = slice(hf * 16, (hf + 1) * 16)
            s.op(eng, lambda e, qs=qs: e.tensor_copy(self.Hall[:, :, qs, 0], self.Hall[:, :, qs, T]), reads=[self.b_Hs16], writes=[self.b_Hall[hf]])
        po, bpo = self.psum()
        for fc in range(8):
            n = 0
            for rb in range(4):
                q = 4 * fc + rb
                for ri in range(2):
                    s.op("pe", lambda e, fc=fc, q=q, ri=ri, n=n: e.matmul(po[:, fc * T:(fc + 1) * T], self.Wc[:, q, ri, :], self.Hs16[:, ri, q, :],
                                                                         start=(n == 0), stop=(n == 7)),
                         reads=[self.b_s5w, self.b_Hs16], writes=[bpo])
                    n += 1
        s.op("act", lambda e, po=po: e.activation(out=self.ys[:, :, ts:ts + T], in_=po[:, 0:8 * T].rearrange("p (j t) -> p j t", j=8), func=AF.Copy),
             reads=[bpo], writes=[self.b_ys])

    def s5_tile(self):
        s = self.s
        fv = self.fvec
        T = 32
        for st_i in range(NT // T):
            self.s5_sub(st_i * T, T)
        for j in range(8):
            q = j % 2
            s.op("dve", lambda e, j=j, q=q: e.scalar_tensor_tensor(self.tmpn[:, q, :], self.ubf[:, j, :], fv[:, 3, j:j + 1], self.ys[:, j, :], ALU.mult, ALU.add),
                 reads=[self.b_ubf[j], self.b_ys, self.b_const], writes=[self.b_tmpn[q]])
            s.op("act", lambda e, j=j, q=q: e.activation(out=self.g5b[:, j, :], in_=self.tmpn[:, q, :], func=AF.Gelu),
                 reads=[self.b_tmpn[q]], writes=[self.b_g5b[j]])

        def on_g(mi, mw, ps, bps):
            q = mi % 2
            s.op("act", lambda e: e.activation(out=self.tmpn[:, q, :], in_=ps[:, 0:NT], func=AF.Sigmoid, bias=fv[:, 4, mi:mi + 1], scale=1.0),
                 reads=[bps, self.b_const], writes=[self.b_tmpn[q]])
            s.op("dve", lambda e: e.tensor_tensor(self.cat[:, 8 + mi, :], self.g5b[:, mi, :], self.tmpn[:, q, :], ALU.mult),
                 reads=[self.b_tmpn[q], self.b_g5b[mi]], writes=[self.b_cat[8 + mi]])
        self.proj("s5_glu_w", 0, 1024, KC, lambda k: self.g5b[:, k, :], self.b_g5b, on_g)

    def mixer1(self, ti):
        s = self.s
        t0 = ti * NT
        self.pre_norm(1, 0)
        hrhs = lambda k: self.ht[:, k, :]

        def on_q(mi, mw, ps, bps):
            self.evac("act" if mi % 2 == 0 else "dve", self.Qb[:, mi, :], ps[:, 0:NT], [bps], [self.b_Qb[mi]])
        self.proj("attn_w_qkv", 0, 3072, KC, hrhs, self.b_h, on_q)

        def mk_kv(dst, bdst):
            def on_kv(mi, mw, ps, bps):
                q = mi % 2
                self.evac("act" if mi % 2 == 0 else "dve", self.kvst[:, q, :], ps[:, 0:NT], [bps], [self.b_kvst[q]])
                s.dma(lambda e: e.dma_start(out=dst[mi, :, 2048 + t0:2048 + t0 + NT], in_=self.kvst[:, q, :]),
                      reads=[self.b_kvst[q]], writes=[bdst])
            return on_kv
        self.proj("attn_w_qkv", 3072, 1024, KC, hrhs, self.b_h, mk_kv(self.Kd, self.b_Kd))
        self.proj("attn_w_qkv", 4096, 1024, KC, hrhs, self.b_h, mk_kv(self.Vd, self.b_Vd))
        for h in range(8):
            self.attn_head(ti, h)

        def on_o(mi, mw, ps, bps):
            self.evac("act", self.yt[:, mi, :], ps[:, 0:NT], [bps], [self.b_y[mi]])
        self.proj("attn_w_o", 0, 1024, KC, lambda k: self.attn[:, k, :], self.b_attn, on_o)
        self.post_norm_residual(1, 0)

    def attn_block(self, qap, nq, keys, ocols, Ob, bO, Db, bD):
        s = self.s
        scale = float(128 ** -0.5)
        n = len(keys)
        for i, (kap, nk, vidx, vbuf, mask) in enumerate(keys):
            pS, bpS = self.psum()
            slot = self.pt_rr
            self.pt_rr = (self.pt_rr + 1) % 4
            pt = self.PT[0:nk, slot, 0:nq]
            bpt = self.b_PT[slot]
            s.op("pe", lambda e, pS=pS, kap=kap, nk=nk: e.matmul(pS[0:nk, 0:nq], kap, qap, start=True, stop=True),
                 reads=[self.b_Kw] + self._qbufs, writes=[bpS])
            s.op("act", lambda e, pS=pS, pt=pt, nk=nk: e.activation(out=pt, in_=pS[0:nk, 0:nq], func=AF.Exp, scale=scale),
                 reads=[bpS], writes=[bpt])
            eng = "dve" if (slot % 2 == 0) else "pool"
            s.op(eng, lambda e, pt=pt, mask=mask: e.tensor_tensor(pt, pt, mask, ALU.mult), reads=[self.b_const], writes=[bpt])
            s.op("pe", lambda e, pt=pt, i=i, nk=nk: e.matmul(Db[:, ocols], self.ones_bf[0:nk, :], pt, start=(i == 0), stop=(i == n - 1)),
                 reads=[bpt, self.b_const], writes=[bD])
            s.op("pe", lambda e, pt=pt, i=i, vidx=vidx, nk=nk: e.matmul(Ob[:, ocols], self.Vtok[0:nk, vidx, :], pt, start=(i == 0), stop=(i == n - 1)),
                 reads=[bpt, vbuf], writes=[bO])

    def attn_head(self, ti, h):
        s = self.s
        t0 = ti * NT
        W = 2048 + NT
        cm = self.cmask
        s.dma(lambda e: e.dma_start(out=self.Kw[:, 0:W], in_=self.Kd[h, :, t0:t0 + W]), reads=[self.b_Kd], writes=[self.b_Kw])
        s.dma(lambda e: e.dma_start(out=self.Vw[:, 0:W], in_=self.Vd[h, :, t0:t0 + W]), reads=[self.b_Vd], writes=[self.b_Vw])
        blocks = []
        for i in range(3):
            blocks.append((i, slice(1920 + 128 * i, 2048 + 128 * i), 128))
        for r in range(4):
            blocks.append((3 + r, slice(1536 + r, 2048, 4), 128))
        for r in range(4):
            blocks.append((7 + r, slice(2048 + r, W, 4), 64))
        for r in range(16):
            blocks.append((11 + r, slice(r, 2048, 16), 128))
        for r in range(16):
            blocks.append((27 + r, slice(2048 + r, W, 16), 16))
        groups = [(0, 7, 128), (7, 11, 64), (11, 19, 128), (19, 27, 128), (27, 35, 16), (35, 43, 16)]
        vb = {}
        for gi, (a, b, nk) in enumerate(groups):
            pv, bpv = self.psum()
            pvb = pv[:, :].bitcast(BF16)
            for (vidx, sl, nk_) in blocks[a:b]:
                c0 = (vidx - a) * 128
                s.op("pe", lambda e, pvb=pvb, sl=sl, c0=c0, nk_=nk_: e.transpose(pvb[0:nk_, c0:c0 + 128], self.Vw[:, sl], self.ident_bf[:, :]),
                     reads=[self.b_Vw, self.b_const], writes=[bpv])
                vb[vidx] = self.b_Vtok[gi]
            eng = "dve" if gi % 2 == 0 else "act"
            self.evac(eng, self.Vtok[0:nk, a:b, :], pvb[0:nk, 0:(b - a) * 128].rearrange("p (j e) -> p j e", j=b - a), [bpv], [self.b_Vtok[gi]])
        Ob, bO = self.psum_fixed(4)
        Db, bD = self.psum_fixed(5)
        Oa, Da = self.Oacc[:, 0, :], self.Oacc[:, 1, :]
        self._qbufs = [self.b_Qb[h]]
        for qb in range(NT // 128):
            keys = []
            if not (ti == 0 and qb == 0):
                keys.append((self.Kw[:, 1920 + 128 * qb:2048 + 128 * qb], 128, qb, vb[qb], cm[:, 2, :]))
            keys.append((self.Kw[:, 2048 + 128 * qb:2176 + 128 * qb], 128, qb + 1, vb[qb + 1], cm[:, 3, :]))
            self.attn_block(self.Qb[:, h, qb * 128:(qb + 1) * 128], 128, keys, slice(qb * 128, (qb + 1) * 128), Ob, bO, Db, bD)
        self.evac("act", Oa, Ob[:, 0:NT], [bO], [self.b_Oacc[0]])
        self.evac("act", Da, Db[:, 0:NT], [bD], [self.b_Oacc[1]])
        self._qbufs = [self.b_Qb[8 + h]]
        for r in range(4):
            keys = []
            if ti >= 2:
                keys.append((self.Kw[:, 1536 + r:2048:4], 128, 3 + r, vb[3 + r], cm[:, 2, 0:64]))
            elif ti == 1:
                keys.append((self.Kw[:, 1536 + r:2048:4], 128, 3 + r, vb[3 + r], cm[:, 3 + 4, 0:64]))
            keys.append((self.Kw[:, 2048 + r:W:4], 64, 7 + r, vb[7 + r], cm[0:64, 3, 0:64]))
            self.attn_block(self.Qb[:, 8 + h, r:NT:4], 64, keys, slice(r, NT, 4), Ob, bO, Db, bD)
        s.op("dve", lambda e: e.tensor_tensor(Oa, Oa, Ob[:, 0:NT], ALU.add), reads=[bO], writes=[self.b_Oacc[0]])
        s.op("dve", lambda e: e.tensor_tensor(Da, Da, Db[:, 0:NT], ALU.add), reads=[bD], writes=[self.b_Oacc[1]])
        self._qbufs = [self.b_Qb[16 + h]]
        for r in range(16):
            keys = []
            if ti >= 8:
                keys.append((self.Kw[:, r:2048:16], 128, 11 + r, vb[11 + r], cm[:, 2, 0:16]))
            elif ti >= 1:
                keys.append((self.Kw[:, r:2048:16], 128, 11 + r, vb[11 + r], cm[:, 3 + ti, 0:16]))
            keys.append((self.Kw[:, 2048 + r:W:16], 16, 27 + r, vb[27 + r], cm[0:16, 3, 0:16]))
            self.attn_block(self.Qb[:, 16 + h, r:NT:16], 16, keys, slice(r, NT, 16), Ob, bO, Db, bD)
        s.op("dve", lambda e: e.tensor_tensor(Oa, Oa, Ob[:, 0:NT], ALU.add), reads=[bO], writes=[self.b_Oacc[0]])
        s.op("dve", lambda e: e.tensor_tensor(Da, Da, Db[:, 0:NT], ALU.add), reads=[bD], writes=[self.b_Oacc[1]])
        s.op("dve", lambda e: e.reciprocal(Da, Da), reads=[self.b_Oacc[1]], writes=[self.b_Oacc[1]])
        s.op("dve", lambda e: e.tensor_tensor(self.attn[:, h, :], Oa, Da, ALU.mult), reads=self.b_Oacc, writes=[self.b_attn[h]])


    def build(self):
        s = self.s
        L = self.L
        self.setup_consts()
        self.ada_mod()
        names = []
        for li in self.layers:
            names += ["ffn_w_in%d" % li, "ffn_w_out%d" % li]
        if self.mixers and 0 in self.layers:
            names += ["hyb_w_in", "hyb_w_out", "s5_glu_w"]
        if self.mixers and 1 in self.layers:
            names += ["attn_w_qkv", "attn_w_o"]
        self.cast_weights(names)
        s.barrier()
        if self.mixers:
            self.setup_mix()
            s.barrier()
        last = []
        for ti in range(L // NT):
            t0 = ti * NT
            for j in range(KC):
                s.dma(lambda e, a=self.xt[:, j, :], b=self.x_in[j * 128:(j + 1) * 128, t0:t0 + NT]: e.dma_start(out=a, in_=b),
                      writes=[self.b_x[j]])
            for li in self.layers:
                if self.mixers:
                    if li == 0:
                        self.mixer0(ti)
                    else:
                        self.mixer1(ti)
                self.ffn(li)
            for j in range(KC):
                t = s.dma(lambda e, a=self.out[j * 128:(j + 1) * 128, t0:t0 + NT], b=self.xt[:, j, :]: e.dma_start(out=a, in_=b),
                          reads=[self.b_x[j]])
                last.append(t)
        s.finish(last)
        return self.nc


def make_inmaps(inp, L, n_cores=N_CORES):
    f = np.float32
    common = {}
    common["ada_w"] = np.ascontiguousarray(inp["ada_w"], dtype=f)
    common["ada_b"] = np.ascontiguousarray(inp["ada_b"].reshape(2, 48, 128).transpose(2, 0, 1), dtype=f)
    g = np.stack([inp["mix_pre_g"], inp["mix_post_g"], inp["ffn_pre_g"], inp["ffn_post_g"]])
    common["gains"] = np.ascontiguousarray(g.reshape(4, 2, KC, 128).transpose(3, 0, 1, 2), dtype=f)
    for li in range(2):
        common["ffn_w_in%d" % li] = np.ascontiguousarray(inp["ffn_w_in"][li], dtype=f)
        common["ffn_w_out%d" % li] = np.ascontiguousarray(inp["ffn_w_out"][li], dtype=f)
    common["hyb_w_in"] = np.ascontiguousarray(inp["hyb_w_in"][0], dtype=f)
    common["hyb_w_out"] = np.ascontiguousarray(inp["hyb_w_out"][0], dtype=f)
    common["s5_glu_w"] = np.ascontiguousarray(inp["s5_glu_w"][0], dtype=f)
    common["attn_w_qkv"] = np.ascontiguousarray(inp["attn_w_qkv"][0], dtype=f)
    common["attn_w_o"] = np.ascontiguousarray(inp["attn_w_o"][0], dtype=f)
    kj = np.arange(128)[:, None]; qi = np.arange(128)[None, :]
    cm = np.zeros((128, 11, 128), f)
    cm[:, 0, :] = np.eye(128, dtype=f)
    cm[:, 1, :] = (qi >= kj)
    cm[:, 2, :] = (kj >= qi)
    cm[:, 3, :] = (kj <= qi)
    for u in range(1, 8):
        cm[:, 3 + u, :] = (kj >= qi) & (kj >= 128 - 16 * u)
    common["cmask"] = cm
    sel = np.zeros((16, 16, 128), f)
    for h in range(16):
        sel[h, h, :] = 1.0
    common["sel"] = sel.reshape(16, 16 * 128)
    common["convw"] = np.ascontiguousarray(inp["ssd_conv_w"][0].reshape(4, 12, 128).transpose(2, 1, 0), dtype=f)
    fv = np.zeros((128, 5, 12), f)
    fv[:, 0, :] = inp["ssd_conv_b"][0].reshape(12, 128).T
    fv[:, 1, :8] = np.repeat(inp["ssd_d"][0], 64).reshape(8, 128).T
    fv[:, 2, :8] = inp["ssd_norm_g"][0].reshape(8, 128).T
    fv[:, 3, :8] = inp["s5_d"][0].reshape(8, 128).T
    fv[:, 4, :8] = inp["s5_glu_b"][0].reshape(8, 128).T
    common["fvec"] = fv
    common["ssd16"] = np.ascontiguousarray(np.stack([inp["ssd_dt_bias"][0], inp["ssd_a_log"][0]], axis=1), dtype=f)
    lre, lim, ldt = inp["s5_lambda_re"][0], inp["s5_lambda_im"][0], inp["s5_log_dt"][0]
    v1 = np.zeros((128, 3, 32), f)
    for q in range(32):
        for gi in range(2):
            g = 2 * q + gi
            v1[gi * 64:(gi + 1) * 64, 0, q] = lre[g]
            v1[gi * 64:(gi + 1) * 64, 1, q] = lim[g]
            v1[gi * 64:(gi + 1) * 64, 2, q] = ldt[g]
    common["s5v1"] = v1
    v2 = np.zeros((128, 5, 8, 2, 64), f)
    bre, bim = inp["s5_b_re"][0], inp["s5_b_im"][0]
    for fc in range(8):
        for rb in range(4):
            for gi2 in range(2):
                g = 8 * fc + 2 * rb + gi2
                rows = slice(32 * rb, 32 * rb + 32)
                v2[rows, 0, fc, gi2, :] = lre[g][None, :]
                v2[rows, 1, fc, gi2, :] = lim[g][None, :]
                v2[rows, 2, fc, gi2, :] = ldt[g]
                r2 = slice(32 * rb + 16 * gi2, 32 * rb + 16 * gi2 + 16)
                v2[r2, 3, fc, gi2, :] = bre[g].T
                v2[r2, 4, fc, gi2, :] = bim[g].T
    common["s5v2"] = v2.reshape(128, 5, 1024)
    cre, cim = inp["s5_c_re"][0], inp["s5_c_im"][0]
    wc = np.zeros((2, 64, 32, 2, 8, 16), f)
    for q in range(32):
        rb = q % 4
        for gi in range(2):
            g = 2 * q + gi
            wc[gi, :, q, 0, 2 * rb + gi, :] = cre[g].T
            wc[gi, :, q, 1, 2 * rb + gi, :] = cim[g].T
    common["s5wc"] = wc.reshape(128, 32 * 2 * 128)
    maps = []
    nb = inp["x"].shape[0]
    for c in range(n_cores):
        b = c % nb
        m = dict(common)
        m["x_fm"] = np.ascontiguousarray(inp["x"][b, :L].T, dtype=f)
        m["c_fm"] = np.ascontiguousarray(inp["c"][b].reshape(KC, 128).T, dtype=f)
        maps.append(m)
    return maps


_NC_CACHE = {}


def kernel(**inputs):
    inp = {k: np.asarray(v) for k, v in inputs.items()}
    B_, L, _ = inp["x"].shape
    if L not in _NC_CACHE:
        _NC_CACHE[L] = Builder(L).build()
    nc = _NC_CACHE[L]
    maps = make_inmaps(inp, L)
    res = run_bass_kernel_spmd(nc, maps, core_ids=list(range(N_CORES)))
    out = np.stack([res.results[b]["out_fm"].T for b in range(B_)])
    return np.ascontiguousarray(out.astype(np.float32))
```

```python
from contextlib import ExitStack
import numpy as np
import concourse.bass as bass
import concourse.mybir as mybir
from concourse.bass_utils import run_bass_kernel_spmd

F32 = mybir.dt.float32
BF16 = mybir.dt.bfloat16
ALU = mybir.AluOpType
AF = mybir.ActivationFunctionType

COMPUTE = ("pe", "dve", "act", "pool")
QUEUES = ("sp",)


class Buf:
    __slots__ = ("name", "w", "r")

    def __init__(self, name=""):
        self.name = name
        self.w = None
        self.r = []


class Sched:
    def __init__(self, nc, stack, n_dma_sems=32):
        self.nc = nc
        self.ops = {e: [] for e in COMPUTE + QUEUES}
        self.sems = {}
        self.cnt = {}
        for e in COMPUTE:
            self.sems[e] = stack.enter_context(nc.semaphore("sem_" + e))
            self.cnt[e] = 0
        self.dma_sems = []
        for i in range(n_dma_sems):
            k = "dma%d" % i
            self.sems[k] = stack.enter_context(nc.semaphore("sem_" + k))
            self.cnt[k] = 0
            self.dma_sems.append(k)
        self.dma_rr = 0
        self.dma_last_tok = {k: None for k in self.dma_sems}
        self.seen = {e: {} for e in COMPUTE + QUEUES}
        self.n_inst = 0

    def buf(self, name=""):
        return Buf(name)

    def _collect(self, eng, reads, writes, extra=()):
        waits = {}
        seen = self.seen[eng]

        def add(tok):
            if tok is None:
                return
            k, v = tok
            if seen.get(k, 0) >= v:
                return
            if waits.get(k, 0) < v:
                waits[k] = v

        for b in reads:
            add(b.w)
        for b in writes:
            add(b.w)
            for t in b.r:
                add(t)
        for t in extra:
            add(t)
        for k, v in waits.items():
            seen[k] = v
        return list(waits.items())

    def _commit(self, tok, reads, writes):
        for b in reads:
            b.r.append(tok)
            if len(b.r) > 64:
                m = {}
                for k, v in b.r:
                    if m.get(k, 0) < v:
                        m[k] = v
                b.r = list(m.items())
        for b in writes:
            b.w = tok
            b.r = []

    def op(self, eng, fn, reads=(), writes=()):
        waits = self._collect(eng, reads, writes)
        self.cnt[eng] += 1
        tok = (eng, self.cnt[eng])
        sems = self.sems
        mysem = sems[eng]

        def emit(e):
            for k, v in waits:
                e.wait_ge(sems[k], v)
            fn(e).then_inc(mysem, 1)

        self.ops[eng].append(emit)
        self.n_inst += 1
        self._commit(tok, reads, writes)
        return tok

    def dma(self, fn, reads=(), writes=(), queue="sp"):
        k = self.dma_sems[self.dma_rr]
        self.dma_rr = (self.dma_rr + 1) % len(self.dma_sems)
        prev = self.dma_last_tok[k]
        extra = (prev,) if prev is not None else ()
        waits = self._collect(queue, reads, writes, extra)
        self.cnt[k] += 16
        tok = (k, self.cnt[k])
        self.dma_last_tok[k] = tok
        sems = self.sems
        dsem = sems[k]

        def emit(e):
            for kk, v in waits:
                e.wait_ge(sems[kk], v)
            fn(e).then_inc(dsem, 16)

        self.ops[queue].append(emit)
        self.n_inst += 1
        self._commit(tok, reads, writes)
        return tok

    def barrier(self):
        sems = self.sems
        for eng in COMPUTE + QUEUES:
            waits = []
            for k, c in self.cnt.items():
                if k == eng or c == 0:
                    continue
                if self.seen[eng].get(k, 0) < c:
                    self.seen[eng][k] = c
                    waits.append((k, c))

            def emit(e, waits=waits):
                for kk, v in waits:
                    e.wait_ge(sems[kk], v)

            self.ops[eng].append(emit)

    def finish(self, final_toks):
        nc = self.nc
        sems = self.sems
        ops = self.ops
        fin = {}
        for t in final_toks:
            if t is None:
                continue
            k, v = t
            fin[k] = max(fin.get(k, 0), v)
        with nc.Block() as block:
            @block.tensor
            def _(e):
                for f in ops["pe"]:
                    f(e)

            @block.vector
            def _(e):
                for f in ops["dve"]:
                    f(e)

            @block.scalar
            def _(e):
                for f in ops["act"]:
                    f(e)

            @block.gpsimd
            def _(e):
                for f in ops["pool"]:
                    f(e)

            @block.sync
            def _(e):
                for f in ops["sp"]:
                    f(e)
                for k, v in fin.items():
                    e.wait_ge(sems[k], v)


D = 1024
KC = D // 128
NT = 256
FFN_H = 2816
FC = FFN_H // 128
HYB_IN = 3600
EPS = 1e-6
N_CORES = 8


class Builder:
    def __init__(self, L, layers=(0, 1), mixers=True, dbg=None):
        self.L = L
        self.layers = layers
        self.mixers = mixers
        self.dbg = dbg or []
        self.nc = bass.Bass("TRN2", target_bir_lowering=False)
        self.stack = ExitStack()
        self.s = Sched(self.nc, self.stack)
        self.rr = 0
        self._tensors()

    def dram(self, name, shape, dt, kind="Internal"):
        return self.nc.dram_tensor(name, list(shape), dt, kind=kind).ap()

    def sb(self, name, shape, dt):
        t = self.stack.enter_context(self.nc.sbuf_tensor(name, list(shape), dt))
        return t

    def _tensors(self):
        nc, s, L = self.nc, self.s, self.L
        I = "ExternalInput"
        self.x_in = self.dram("x_fm", [D, L], F32, I)
        self.out = self.dram("out_fm", [D, L], F32, "ExternalOutput")
        self.c_in = self.dram("c_fm", [128, KC], F32, I)
        self.ada_w = self.dram("ada_w", [2, D, 6 * D], F32, I)
        self.ada_b = self.dram("ada_b", [128, 2, 48], F32, I)
        self.gains = self.dram("gains", [128, 4, 2, KC], F32, I)
        self.w_f32 = {}
        self.w_bf = {}
        self.w_buf = {}
        for name, shape in (("ffn_w_in0", [D, 2 * FFN_H]), ("ffn_w_in1", [D, 2 * FFN_H]),
                            ("ffn_w_out0", [FFN_H, D]), ("ffn_w_out1", [FFN_H, D]),
                            ("hyb_w_in", [D, HYB_IN]), ("hyb_w_out", [2 * D, D]),
                            ("s5_glu_w", [D, D]), ("attn_w_qkv", [D, 5 * D]), ("attn_w_o", [D, D])):
            self.w_f32[name] = self.dram(name, shape, F32, I)
            self.w_bf[name] = self.dram(name + "_bf", shape, BF16)
            self.w_buf[name] = s.buf(name)

        self.xt = self.sb("xt", [128, KC, NT], F32); self.b_x = [s.buf("x%d" % j) for j in range(KC)]
        self.ht = self.sb("ht", [128, KC, NT], BF16); self.b_h = [s.buf("h%d" % j) for j in range(KC)]
        self.yt = self.sb("yt", [128, KC, NT], F32); self.b_y = [s.buf("y%d" % j) for j in range(KC)]
        self.sq = self.sb("sq", [128, 2, NT], BF16); self.b_sq = [s.buf("sq0"), s.buf("sq1")]
        self.rstd = self.sb("rstd", [128, NT], F32); self.b_rstd = s.buf("rstd")
        self.tmpn = self.sb("tmpn", [128, 2, NT], F32); self.b_tmpn = [s.buf("tn0"), s.buf("tn1")]
        self.b_act = [s.buf("a%d" % j) for j in range(FC)]
        self.sg = self.sb("sg", [128, 2, NT], F32); self.b_sg = [s.buf("sg0"), s.buf("sg1")]
        self.NW = 2
        self.wp = self.sb("wp", [128, self.NW, 4096], BF16); self.b_wp = [s.buf("wp%d" % i) for i in range(self.NW)]
        self.wp_rr = 0
        self.b_arena = [s.buf("ar%d" % i) for i in range(2)]
        self._tensors_mix()
        self.arena = self.A32[:, 0:4096].rearrange("p (a n) -> p a n", a=2)
        self.act = self.A16[:, 0:FC * NT].rearrange("p (j t) -> p j t", j=FC)
        self.ones_bf = self.sb("ones_bf", [128, 128], BF16); self.b_const = s.buf("const")
        self.cst = self.sb("cst", [128, 4], F32)
        self.cond = self.sb("cond", [128, KC], F32)
        self.mod = self.sb("mod", [128, 2, 48], F32); self.b_mod = s.buf("mod")
        self.adab = self.sb("adab", [128, 2, 48], F32)
        self.gn = self.sb("gn", [128, 4, 2, KC], F32)
        self.vec = self.sb("vec", [128, 2, 6, KC], F32); self.b_vec = s.buf("vec")
        self.ps = []
        self.b_ps = []
        for i in range(8):
            self.ps.append(self.stack.enter_context(nc.psum_tensor("ps%d" % i, [128, 512], F32)))
            self.b_ps.append(s.buf("ps%d" % i))
        self.ps_rr = 0
        self.psbf_rr = 0

    POOL_BANKS = (0, 1, 2)
    BF_BANKS = (3, 7)

    def psum_bf(self):
        i = self.BF_BANKS[self.psbf_rr]
        self.psbf_rr = (self.psbf_rr + 1) % len(self.BF_BANKS)
        return self.ps[i], self.b_ps[i]

    def psum(self):
        i = self.POOL_BANKS[self.ps_rr]
        self.ps_rr = (self.ps_rr + 1) % len(self.POOL_BANKS)
        return self.ps[i], self.b_ps[i]

    def psum_fixed(self, i):
        return self.ps[i], self.b_ps[i]

    def eng3(self):
        e = ("act", "pool", "dve")[self.rr % 3]
        self.rr += 1
        return e

    def cast_weights(self, names):
        s = self.s
        i = 0
        for name in names:
            src, dst = self.w_f32[name], self.w_bf[name]
            K, N = src.shape
            for kb in range(K // 128):
                for c0 in range(0, N, 2048):
                    cw = min(2048, N - c0)
                    slot = i % 2
                    wslot = i % 2
                    i += 1
                    st32 = self.arena[:, slot, 0:cw]
                    stbf = self.wp[:, wslot, 0:cw]
                    s.dma(lambda e, a=st32, b=src[kb * 128:(kb + 1) * 128, c0:c0 + cw]: e.dma_start(out=a, in_=b),
                          writes=[self.b_arena[slot]])
                    eng = self.eng3()
                    if eng == "act":
                        s.op("act", lambda e, a=stbf, b=st32: e.activation(out=a, in_=b, func=AF.Copy),
                             reads=[self.b_arena[slot]], writes=[self.b_wp[wslot]])
                    else:
                        s.op(eng, lambda e, a=stbf, b=st32: e.tensor_copy(a, b),
                             reads=[self.b_arena[slot]], writes=[self.b_wp[wslot]])
                    s.dma(lambda e, a=dst[kb * 128:(kb + 1) * 128, c0:c0 + cw], b=stbf: e.dma_start(out=a, in_=b),
                          reads=[self.b_wp[wslot]], writes=[self.w_buf[name]])

    def setup_consts(self):
        s = self.s
        s.op("pool", lambda e: e.memset(self.ones_bf[:, :], 1.0), writes=[self.b_const])
        s.op("pool", lambda e: e.memset(self.cst[:, 0:1], EPS), writes=[self.b_const])
        s.op("pool", lambda e: e.memset(self.cst[:, 1:2], 1.0), writes=[self.b_const])
        s.op("pool", lambda e: e.memset(self.cst[:, 2:3], 0.0), writes=[self.b_const])
        s.dma(lambda e: e.dma_start(out=self.cond[:, :], in_=self.c_in), writes=[self.b_const])
        s.dma(lambda e: e.dma_start(out=self.adab[:, :, :], in_=self.ada_b), writes=[self.b_const])
        s.dma(lambda e: e.dma_start(out=self.gn[:, :, :, :], in_=self.gains), writes=[self.b_const])
        s.op("act", lambda e: e.activation(out=self.cond[:, :], in_=self.cond[:, :], func=AF.Silu),
             reads=[self.b_const], writes=[self.b_const])

    def ada_mod(self):
        s = self.s
        for li in range(2):
            ps, bps = self.psum()
            for cp in range(12):
                slot = cp % 3
                for half in range(2):
                    c0 = cp * 512 + half * 256
                    slot = (cp * 2 + half) % 2
                    view = self.arena[:, slot, 0:2048].rearrange("p (k n) -> p k n", k=KC)
                    s.dma(lambda e, a=view, b=self.ada_w[li, :, c0:c0 + 256].rearrange("(k p) n -> p k n", p=128):
                          e.dma_start(out=a, in_=b), writes=[self.b_arena[slot]])
                    for mm in range(2):
                        m = (c0 // 128) + mm
                        for k in range(KC):
                            s.op("pe", lambda e, o=ps[:, m:m + 1], w=view[:, k, mm * 128:(mm + 1) * 128], r=self.cond[:, k:k + 1],
                                 st=(k == 0), sp=(k == KC - 1): e.matmul(o, w, r, start=st, stop=sp),
                                 reads=[self.b_arena[slot], self.b_const], writes=[bps])
            s.op("dve", lambda e, o=self.mod[:, li, :], a=ps[:, 0:48], b=self.adab[:, li, :]:
                 e.tensor_tensor(o, a, b, ALU.add), reads=[bps, self.b_const], writes=[self.b_mod])
        for li in range(2):
            for half, (gpre, gpost) in enumerate(((0, 1), (2, 3))):
                o = half * 24
                sh = self.mod[:, li, o:o + 8]
                sc = self.mod[:, li, o + 8:o + 16]
                gt = self.mod[:, li, o + 16:o + 24]
                s.op("dve", lambda e, out=self.vec[:, li, half * 3 + 0, :], sc=sc, g=self.gn[:, gpre, li, :]:
                     e.scalar_tensor_tensor(out, sc, 1.0, g, ALU.add, ALU.mult),
                     reads=[self.b_mod, self.b_const], writes=[self.b_vec])
                s.op("dve", lambda e, out=self.vec[:, li, half * 3 + 1, :], sh=sh: e.tensor_copy(out, sh),
                     reads=[self.b_mod], writes=[self.b_vec])
                s.op("dve", lambda e, out=self.vec[:, li, half * 3 + 2, :], gt=gt, g=self.gn[:, gpost, li, :]:
                     e.tensor_tensor(out, gt, g, ALU.mult), reads=[self.b_mod, self.b_const], writes=[self.b_vec])

    def wpanel(self, name, kc, c0, cols, k0=0):
        s = self.s
        assert kc * cols <= 4096
        slot = self.wp_rr
        self.wp_rr = (self.wp_rr + 1) % self.NW
        view = self.wp[:, slot, 0:kc * cols].rearrange("p (k n) -> p k n", k=kc)
        src = self.w_bf[name][k0 * 128:(k0 + kc) * 128, c0:c0 + cols].rearrange("(k p) n -> p k n", p=128)
        s.dma(lambda e, a=view, b=src: e.dma_start(out=a, in_=b), reads=[self.w_buf[name]], writes=[self.b_wp[slot]])
        return view, self.b_wp[slot]

    def sumsq_rstd(self, src, b_src, nchunks, denom):
        s = self.s
        ps, bps = self.psum()
        for j in range(nchunks):
            q = j % 2
            eng = "pool" if j % 2 == 0 else "act"
            if eng == "act":
                s.op("act", lambda e, o=self.sq[:, q, :], a=src[j]: e.activation(out=o, in_=a, func=AF.Square),
                     reads=[b_src[j]], writes=[self.b_sq[q]])
            else:
                s.op("pool", lambda e, o=self.sq[:, q, :], a=src[j]: e.tensor_tensor(o, a, a, ALU.mult),
                     reads=[b_src[j]], writes=[self.b_sq[q]])
            s.op("pe", lambda e, o=ps[:, 0:NT], r=self.sq[:, q, :], st=(j == 0), sp=(j == nchunks - 1):
                 e.matmul(o, self.ones_bf[:, :], r, start=st, stop=sp),
                 reads=[self.b_sq[q], self.b_const], writes=[bps])
        s.op("act", lambda e, o=self.rstd[:, :], a=ps[:, 0:NT]: e.activation(out=o, in_=a, func=AF.Sqrt,
                                                                        bias=self.cst[:, 0:1], scale=1.0 / denom),
             reads=[bps, self.b_const], writes=[self.b_rstd])
        s.op("dve", lambda e, o=self.rstd[:, :]: e.reciprocal(o, o), reads=[self.b_rstd], writes=[self.b_rstd])

    def pre_norm(self, li, which):
        s = self.s
        self.sumsq_rstd([self.xt[:, j, :] for j in range(KC)], self.b_x, KC, float(D))
        for j in range(KC):
            q = j % 2
            s.op("dve", lambda e, o=self.tmpn[:, q, :], a=self.xt[:, j, :]: e.tensor_tensor(o, a, self.rstd[:, :], ALU.mult),
                 reads=[self.b_x[j], self.b_rstd], writes=[self.b_tmpn[q]])
            s.op("act", lambda e, o=self.ht[:, j, :], a=self.tmpn[:, q, :], sc=self.vec[:, li, which * 3 + 0, j:j + 1],
                 bi=self.vec[:, li, which * 3 + 1, j:j + 1]: e.activation(out=o, in_=a, func=AF.Identity, bias=bi, scale=sc),
                 reads=[self.b_tmpn[q], self.b_vec], writes=[self.b_h[j]])

    def post_norm_residual(self, li, which):
        s = self.s
        self.sumsq_rstd([self.yt[:, j, :] for j in range(KC)], self.b_y, KC, float(D))
        for j in range(KC):
            q = j % 2
            eng = "dve" if j % 2 == 0 else "pool"
            s.op(eng, lambda e, o=self.tmpn[:, q, :], a=self.yt[:, j, :]: e.tensor_tensor(o, a, self.rstd[:, :], ALU.mult),
                 reads=[self.b_y[j], self.b_rstd], writes=[self.b_tmpn[q]])
            s.op("dve", lambda e, o=self.xt[:, j, :], a=self.tmpn[:, q, :], g=self.vec[:, li, which * 3 + 2, j:j + 1]:
                 e.scalar_tensor_tensor(o, a, g, o, ALU.mult, ALU.add),
                 reads=[self.b_tmpn[q], self.b_vec], writes=[self.b_x[j]])

    def ffn(self, li):
        s = self.s
        win = "ffn_w_in%d" % li
        wout = "ffn_w_out%d" % li
        self.pre_norm(li, 1)
        for mp in range(0, FC, 4):
            nm = min(4, FC - mp)
            wg, bwg = self.wpanel(win, KC, mp * 128, nm * 128)
            wu, bwu = self.wpanel(win, KC, FFN_H + mp * 128, nm * 128)
            for mi in range(nm):
                m = mp + mi
                pg, bpg = self.psum()
                pu, bpu = self.psum()
                for k in range(KC):
                    s.op("pe", lambda e, o=pg[:, 0:NT], w=wg[:, k, mi * 128:(mi + 1) * 128], r=self.ht[:, k, :], st=(k == 0), sp=(k == KC - 1):
                         e.matmul(o, w, r, start=st, stop=sp), reads=[bwg, self.b_h[k]], writes=[bpg])
                for k in range(KC):
                    s.op("pe", lambda e, o=pu[:, 0:NT], w=wu[:, k, mi * 128:(mi + 1) * 128], r=self.ht[:, k, :], st=(k == 0), sp=(k == KC - 1):
                         e.matmul(o, w, r, start=st, stop=sp), reads=[bwu, self.b_h[k]], writes=[bpu])
                q = m % 2
                s.op("act", lambda e, o=self.sg[:, q, :], a=pg[:, 0:NT]: e.activation(out=o, in_=a, func=AF.Silu),
                     reads=[bpg], writes=[self.b_sg[q]])
                s.op("dve", lambda e, o=self.act[:, m, :], a=pu[:, 0:NT], b=self.sg[:, q, :]: e.tensor_tensor(o, a, b, ALU.mult),
                     reads=[bpu, self.b_sg[q]], writes=[self.b_act[m]])
        for m in range(KC):
            w, bw = self.wpanel(wout, FC, m * 128, 128)
            po, bpo = self.psum()
            for k in range(FC):
                s.op("pe", lambda e, o=po[:, 0:NT], w_=w[:, k, :], r=self.act[:, k, :], st=(k == 0), sp=(k == FC - 1):
                     e.matmul(o, w_, r, start=st, stop=sp), reads=[bw, self.b_act[k]], writes=[bpo])
            s.op("act", lambda e, o=self.yt[:, m, :], a=po[:, 0:NT]: e.activation(out=o, in_=a, func=AF.Copy),
                 reads=[bpo], writes=[self.b_y[m]])
        self.post_norm_residual(li, 1)

    def _tensors_mix(self):
        nc, s, L = self.nc, self.s, self.L
        I = "ExternalInput"
        self.d_cmask = self.dram("cmask", [128, 11, 128], F32, I)
        self.d_sel = self.dram("sel", [16, 16 * 128], F32, I)
        self.d_convw = self.dram("convw", [128, 12, 4], F32, I)
        self.d_fvec = self.dram("fvec", [128, 5, 12], F32, I)
        self.d_ssd16 = self.dram("ssd16", [16, 2], F32, I)
        self.d_s5v1 = self.dram("s5v1", [128, 3, 32], F32, I)
        self.d_s5v2 = self.dram("s5v2", [128, 5, 1024], F32, I)
        self.d_s5wc = self.dram("s5wc", [128, 32 * 2 * 128], F32, I)
        self.Kd = self.dram("Kd", [8, 128, 2048 + L], BF16)
        self.Vd = self.dram("Vd", [8, 128, 2048 + L], BF16)
        self.b_Kd = s.buf("Kd"); self.b_Vd = s.buf("Vd")
        self.cmask = self.sb("cmask_sb", [128, 11, 128], F32)
        self.ident_bf = self.sb("ident_bf", [128, 128], BF16)
        self.sel = self.sb("sel_sb", [16, 16, 128], F32)
        self.ones16 = self.sb("ones16", [16, 128], F32)
        self.convw = self.sb("convw_sb", [128, 12, 4], F32)
        self.fvec = self.sb("fvec_sb", [128, 5, 12], F32)
        self.ssd16 = self.sb("ssd16_sb", [16, 4], F32)
        self.A32 = self.sb("A32", [128, 7424], F32); self.b_A32 = s.buf("A32")
        self.A16 = self.sb("A16", [128, 19456], BF16); self.b_A16 = s.buf("A16")
        self.Hst = self.sb("Hst", [128, 1024], F32); self.b_Hst = [s.buf("Hst%d" % h) for h in range(16)]
        self.Hbf = self.sb("Hbf", [128, 1024], BF16); self.b_Hbf = [s.buf("Hbf%d" % h) for h in range(16)]
        self.ctail = self.sb("ctail", [128, 12, 3], F32); self.b_ctail = s.buf("ctail")
        self.dtt = self.sb("dtt", [16, 5, NT], F32); self.b_dtt = [s.buf("dtt%d" % i) for i in range(5)]
        self.tok = self.sb("tok", [128, 2, 48], F32); self.b_tok = [s.buf("tok0"), s.buf("tok1")]
        self.xdt = self.sb("xdt", [128, 2, 1024], BF16); self.b_xdt = [s.buf("xdt"), s.buf("xdtw")]
        self.Btok = self.sb("Btok", [128, 256], BF16); self.b_Btok = s.buf("Btok")
        self.cbm = self.sb("cbm", [128, 256], F32); self.b_cbm = s.buf("cbm")
        self.D1 = self.sb("D1", [128, 2, 128], F32); self.b_D1 = [s.buf("D1a"), s.buf("D1b")]
        self.Eh = self.sb("Eh", [128, 2, 128], F32); self.b_Eh = [s.buf("Eha"), s.buf("Ehb")]
        self.Mh = self.sb("Mh", [128, 2, 128], BF16); self.b_Mh = [s.buf("Mha"), s.buf("Mhb")]
        self.Csh = self.sb("Csh", [128, 2, 128], BF16); self.b_Csh = [s.buf("Csa"), s.buf("Csb")]
        self.Wc = self.sb("Wc", [128, 32, 2, 128], BF16)
        self.Wb = self.sb("Wb", [128, 8, 2, 128], BF16)
        self.s5c = self.sb("s5c", [128, 4, 32], F32)
        self.TS = 16
        TS = self.TS
        self.Hall = self.sb("Hall", [128, 2, 32, TS + 1], F32); self.b_Hall = [s.buf("Hall0"), s.buf("Hall1")]
        self.Hs16 = self.sb("Hs16", [128, 2, 32, TS], BF16); self.b_Hs16 = [s.buf("Hs16a"), s.buf("Hs16b")]
        self.Tinv = self.sb("Tinv", [128, 2, 32, TS], F32)
        self.Tfwd = self.sb("Tfwd", [128, 2, 32, TS], F32)
        self.smask = self.sb("smask", [128, 2, 32, TS + 1], BF16)
        self.b_m12 = s.buf("m12"); self.b_m3 = s.buf("m3")
        self.b_s5w = s.buf("s5w")
        a32 = self.A32
        self.convbuf = a32[:, 0:12 * (NT + 3)].rearrange("p (j t) -> p j t", j=12)
        o = 12 * (NT + 3)
        self.cacc = a32[:, o:o + 2 * NT].rearrange("p (j t) -> p j t", j=2); o += 2 * NT
        self.ys = a32[:, o:o + 8 * NT].rearrange("p (j t) -> p j t", j=8); o += 8 * NT
        self.yg = a32[:, 0:8 * NT].rearrange("p (j t) -> p j t", j=8)
        assert o <= 7424, o
        self.mtmp = a32[:, 2048:2048 + 3 * 32 * self.TS].rearrange("p (a q t) -> p a q t", a=3, q=32)
        a16 = self.A16
        o = 0
        def v16(n, j):
            nonlocal o
            r = a16[:, o:o + n].rearrange("p (j t) -> p j t", j=j)
            o += n
            return r
        self.zs = v16(8 * NT, 8)
        self.xsb = v16(8 * NT, 8)
        self.Bfm = v16(2 * NT, 2)
        self.Cfm = v16(2 * NT, 2)
        self.ubf = v16(8 * NT, 8)
        self.cat = v16(16 * NT, 16)
        self.g5b = v16(8 * NT, 8)
        assert o <= 19456, o
        self.b_zs = [s.buf() for _ in range(8)]
        self.b_xsb = [s.buf() for _ in range(8)]
        self.b_BC = [s.buf() for _ in range(4)]
        self.b_ubf = [s.buf() for _ in range(8)]
        self.b_cat = [s.buf() for _ in range(16)]
        self.b_g5b = [s.buf() for _ in range(8)]
        self.b_conv = [s.buf() for _ in range(12)]
        self.b_cacc = [s.buf(), s.buf()]
        self.b_ys = s.buf()
        self.b_yg = [s.buf() for _ in range(8)]
        o = 0
        self.Qb = v16(24 * NT, 24)
        self.kvst = v16(2 * NT, 2)
        self.Kw = a16[:, o:o + 2048 + NT]; o += 2048 + NT
        self.Vw = a16[:, o:o + 2048 + NT]; o += 2048 + NT
        self.attn = v16(8 * NT, 8)
        self.Vtok = v16(43 * 128, 43)
        self.PT = v16(4 * 128, 4)
        assert o <= 19456, o
        self.b_Qb = [s.buf() for _ in range(24)]
        self.b_kvst = [s.buf(), s.buf()]
        self.b_Kw = s.buf(); self.b_Vw = s.buf()
        self.b_attn = [s.buf() for _ in range(8)]
        self.b_Vtok = [s.buf() for _ in range(6)]
        self.b_PT = [s.buf() for _ in range(4)]
        self.Oacc = a32[:, 0:2 * NT].rearrange("p (j t) -> p j t", j=2)
        self.b_Oacc = [s.buf(), s.buf()]
        self.pt_rr = 0

    def setup_mix(self):
        s = self.s
        bc = self.b_const
        s.dma(lambda e: e.dma_start(out=self.cmask[:, :, :], in_=self.d_cmask), writes=[bc])
        s.dma(lambda e: e.dma_start(out=self.sel[:, :, :], in_=self.d_sel.rearrange("k (h m) -> k h m", h=16)), writes=[bc])
        s.dma(lambda e: e.dma_start(out=self.convw[:, :, :], in_=self.d_convw), writes=[bc])
        s.dma(lambda e: e.dma_start(out=self.fvec[:, :, :], in_=self.d_fvec), writes=[bc])
        s.dma(lambda e: e.dma_start(out=self.ssd16[:, 0:2], in_=self.d_ssd16), writes=[bc])
        s.op("pool", lambda e: e.memset(self.ones16[:, :], 1.0), writes=[bc])
        s.op("dve", lambda e: e.tensor_copy(self.ident_bf[:, :], self.cmask[:, 0, :]), reads=[bc], writes=[bc])
        s.op("act", lambda e: e.activation(out=self.ssd16[:, 2:3], in_=self.ssd16[:, 1:2], func=AF.Exp), reads=[bc], writes=[bc])
        s.op("dve", lambda e: e.tensor_scalar(self.ssd16[:, 2:3], self.ssd16[:, 2:3], -1.0, None, ALU.mult), reads=[bc], writes=[bc])
        for h in range(16):
            s.op("pool", lambda e, h=h: e.memset(self.Hst[:, h * 64:(h + 1) * 64], 0.0), writes=[self.b_Hst[h]])
            s.op("pool", lambda e, h=h: e.memset(self.Hbf[:, h * 64:(h + 1) * 64], 0.0), writes=[self.b_Hbf[h]])
        s.op("pool", lambda e: e.memset(self.ctail[:, :, :], 0.0), writes=[self.b_ctail])
        for ri in range(2):
            s.op("pool", lambda e, ri=ri: e.memset(self.Hall[:, ri, :, :], 0.0), writes=[self.b_Hall[ri]])
        s.op("pool", lambda e: e.memset(self.smask[:, :, :, :], 1.0), writes=[self.b_s5w])
        s.op("pool", lambda e: e.memset(self.smask[:, :, :, 0:1], 0.0), writes=[self.b_s5w])
        self.setup_s5()
        if 1 in self.layers:
            s.op("pool", lambda e: e.memset(self.wp[:, 0, 0:2048], 0.0), writes=[self.b_wp[0]])
            for h in range(8):
                s.dma(lambda e, h=h: e.dma_start(out=self.Kd[h, :, 0:2048], in_=self.wp[:, 0, 0:2048]), reads=[self.b_wp[0]], writes=[self.b_Kd])
                s.dma(lambda e, h=h: e.dma_start(out=self.Vd[h, :, 0:2048], in_=self.wp[:, 0, 0:2048]), reads=[self.b_wp[0]], writes=[self.b_Vd])

    def setup_s5(self):
        s = self.s
        ba = self.b_A32
        PI = float(np.pi)
        sc = self.A32
        I32 = mybir.dt.int32

        def dve(fn, reads=(), writes=()):
            s.op("dve", fn, reads=[ba] + list(reads), writes=[ba] + list(writes))

        def act(fn):
            s.op("act", fn, reads=[ba], writes=[ba])

        def coeffs(lr, li, ld, N, base, ts=1.0):
            t = [sc[:, base + i * N: base + (i + 1) * N] for i in range(8)]
            dt, mag, ang, r, kf, m, ar, ai = t
            ki = kf.bitcast(I32)
            act(lambda e: e.activation(out=dt, in_=ld, func=AF.Exp))
            if ts != 1.0:
                dve(lambda e: e.tensor_scalar(dt, dt, float(ts), None, ALU.mult))
            dve(lambda e: e.tensor_tensor(mag, lr, dt, ALU.mult))
            act(lambda e: e.activation(out=mag, in_=mag, func=AF.Exp))
            dve(lambda e: e.tensor_tensor(ang, li, dt, ALU.mult))

            def reduce_sin(shift, out):
                dve(lambda e: e.tensor_scalar(r, ang, shift, None, ALU.add))
                dve(lambda e: e.tensor_scalar(m, r, 1.0 / (2 * PI), None, ALU.mult))
                dve(lambda e: e.tensor_copy(ki, m))
                dve(lambda e: e.tensor_copy(m, ki))
                dve(lambda e: e.scalar_tensor_tensor(r, m, -2 * PI, r, ALU.mult, ALU.add))
                dve(lambda e: e.tensor_scalar(m, r, PI, None, ALU.is_gt))
                dve(lambda e: e.scalar_tensor_tensor(r, m, -2 * PI, r, ALU.mult, ALU.add))
                dve(lambda e: e.tensor_scalar(m, r, -PI, None, ALU.is_lt))
                dve(lambda e: e.scalar_tensor_tensor(r, m, 2 * PI, r, ALU.mult, ALU.add))
                act(lambda e: e.activation(out=out, in_=r, func=AF.Sin))

            reduce_sin(0.0, ai)
            reduce_sin(PI / 2, ar)
            dve(lambda e: e.tensor_tensor(ar, ar, mag, ALU.mult))
            dve(lambda e: e.tensor_tensor(ai, ai, mag, ALU.mult))
            return ar, ai

        v1 = sc[:, 0:96].rearrange("p (a q) -> p a q", a=3)
        s.dma(lambda e: e.dma_start(out=v1, in_=self.d_s5v1), writes=[ba])
        ar, ai = coeffs(v1[:, 0, :], v1[:, 1, :], v1[:, 2, :], 32, 128)
        dve(lambda e, ar=ar: e.tensor_copy(self.s5c[:, 0, :], ar), writes=[self.b_s5w])
        dve(lambda e, ar=ar: e.tensor_copy(self.s5c[:, 1, :], ar), writes=[self.b_s5w])
        dve(lambda e, ai=ai: e.tensor_copy(self.s5c[:, 2, :], ai), writes=[self.b_s5w])
        dve(lambda e, ai=ai: e.tensor_scalar(self.s5c[:, 3, :], ai, -1.0, None, ALU.mult), writes=[self.b_s5w])
        for t in range(1, self.TS + 1):
            for tab, sgn in ((self.Tfwd, 1.0), (self.Tinv, -1.0)):
                ar_t, ai_t = coeffs(v1[:, 0, :], v1[:, 1, :], v1[:, 2, :], 32, 128, ts=sgn * t)
                dve(lambda e, ar_t=ar_t, tab=tab, t=t: e.tensor_copy(tab[:, 0, :, t - 1], ar_t), writes=[self.b_s5w])
                dve(lambda e, ai_t=ai_t, tab=tab, t=t: e.tensor_copy(tab[:, 1, :, t - 1], ai_t), writes=[self.b_s5w])
        N = 128
        for fc in range(8):
            inp = [sc[:, i * N:(i + 1) * N] for i in range(5)]
            for i in range(5):
                s.dma(lambda e, i=i, fc=fc: e.dma_start(out=inp[i], in_=self.d_s5v2[:, i, fc * 128:(fc + 1) * 128]), writes=[ba])
            lr, li, ld, Bre, Bim = inp
            ar, ai = coeffs(lr, li, ld, N, 5 * N)
            den, am1, qre, qim, t1, t2 = [sc[:, (13 + i) * N:(14 + i) * N] for i in range(6)]
            dve(lambda e: e.tensor_tensor(den, lr, lr, ALU.mult))
            dve(lambda e: e.tensor_tensor(t1, li, li, ALU.mult))
            dve(lambda e: e.tensor_tensor(den, den, t1, ALU.add))
            dve(lambda e: e.reciprocal(den, den))
            dve(lambda e: e.tensor_scalar(am1, ar, -1.0, None, ALU.add))
            dve(lambda e: e.tensor_tensor(t1, am1, lr, ALU.mult))
            dve(lambda e: e.tensor_tensor(t2, ai, li, ALU.mult))
            dve(lambda e: e.tensor_tensor(qre, t1, t2, ALU.add))
            dve(lambda e: e.tensor_tensor(qre, qre, den, ALU.mult))
            dve(lambda e: e.tensor_tensor(t1, ai, lr, ALU.mult))
            dve(lambda e: e.tensor_tensor(t2, am1, li, ALU.mult))
            dve(lambda e: e.tensor_tensor(qim, t1, t2, ALU.subtract))
            dve(lambda e: e.tensor_tensor(qim, qim, den, ALU.mult))
            dve(lambda e: e.tensor_tensor(t1, qre, Bre, ALU.mult))
            dve(lambda e: e.tensor_tensor(t2, qim, Bim, ALU.mult))
            dve(lambda e, fc=fc: e.tensor_tensor(self.Wb[:, fc, 0, :], t1, t2, ALU.subtract), writes=[self.b_s5w])
            dve(lambda e: e.tensor_tensor(t1, qre, Bim, ALU.mult))
            dve(lambda e: e.tensor_tensor(t2, qim, Bre, ALU.mult))
            dve(lambda e, fc=fc: e.tensor_tensor(self.Wb[:, fc, 1, :], t1, t2, ALU.add), writes=[self.b_s5w])
        for q in range(32):
            st = sc[:, 0:256].rearrange("p (r m) -> p r m", r=2)
            s.dma(lambda e, q=q: e.dma_start(out=sc[:, 0:256], in_=self.d_s5wc[:, q * 256:(q + 1) * 256]), writes=[ba])
            dve(lambda e, q=q: e.tensor_copy(self.Wc[:, q, 0, :], st[:, 0, :]), writes=[self.b_s5w])
            dve(lambda e, q=q: e.tensor_scalar(self.Wc[:, q, 1, :], st[:, 1, :], -1.0, None, ALU.mult), writes=[self.b_s5w])

    def evac(self, eng, out, in_, reads, writes):
        s = self.s
        if eng == "act":
            s.op("act", lambda e: e.activation(out=out, in_=in_, func=AF.Copy), reads=reads, writes=writes)
        else:
            s.op(eng, lambda e: e.tensor_copy(out, in_), reads=reads, writes=writes)

    def proj(self, wname, c0, ncols, nk, rhs_fn, rhs_bufs, on_chunk, k0=0, panel_cols=512):
        s = self.s
        pc = min(panel_cols, (4096 // nk) // 128 * 128) if ncols >= 128 else ncols
        mi = 0
        for p0 in range(0, ncols, pc):
            pw = min(pc, ncols - p0)
            w, bw = self.wpanel(wname, nk, c0 + p0, pw, k0=k0)
            for m0 in range(0, pw, 128):
                mw = min(128, pw - m0)
                ps, bps = self.psum()
                for k in range(nk):
                    s.op("pe", lambda e, o=ps[0:mw, 0:NT], w_=w[:, k, m0:m0 + mw], r=rhs_fn(k), st=(k == 0), sp=(k == nk - 1):
                         e.matmul(o, w_, r, start=st, stop=sp), reads=[bw, rhs_bufs[k]], writes=[bps])
                on_chunk(mi, mw, ps, bps)
                mi += 1

    def mixer0(self, ti):
        s = self.s
        fv = self.fvec
        self.pre_norm(0, 0)
        hrhs = lambda k: self.ht[:, k, :]
        def on_z(mi, mw, ps, bps):
            s.op("act", lambda e: e.activation(out=self.zs[:, mi, :], in_=ps[:, 0:NT], func=AF.Silu),
                 reads=[bps], writes=[self.b_zs[mi], self.b_A16])
        self.proj("hyb_w_in", 0, 1024, KC, hrhs, self.b_h, on_z)

        def on_xbc(mi, mw, ps, bps):
            self.evac("dve", self.convbuf[:, mi, 3:3 + NT], ps[:, 0:NT], [bps], [self.b_conv[mi], self.b_A32])
        self.proj("hyb_w_in", 1024, 1536, KC, hrhs, self.b_h, on_xbc)

        def on_dt(mi, mw, ps, bps):
            s.op("act", lambda e: e.activation(out=self.dtt[:, 0, :], in_=ps[0:16, 0:NT], func=AF.Exp, bias=self.ssd16[:, 0:1], scale=1.0),
                 reads=[bps, self.b_const], writes=[self.b_dtt[0]])
        self.proj("hyb_w_in", 2560, 16, KC, hrhs, self.b_h, on_dt)

        def on_u(mi, mw, ps, bps):
            self.evac("act", self.ubf[:, mi, :], ps[:, 0:NT], [bps], [self.b_ubf[mi]])
        self.proj("hyb_w_in", 2576, 1024, KC, hrhs, self.b_h, on_u)

        for j in range(12):
            cb = self.convbuf
            q = j % 2
            acc = self.cacc[:, q, :]
            s.op("dve", lambda e, j=j: e.tensor_copy(cb[:, j, 0:3], self.ctail[:, j, :]), reads=[self.b_ctail], writes=[self.b_conv[j]])
            s.op("dve", lambda e, j=j, acc=acc: e.tensor_scalar(acc, cb[:, j, 0:NT], self.convw[:, j, 0:1], None, ALU.mult),
                 reads=[self.b_conv[j], self.b_const], writes=[self.b_cacc[q]])
            for k in range(1, 4):
                s.op("dve", lambda e, j=j, k=k, acc=acc: e.scalar_tensor_tensor(acc, cb[:, j, k:k + NT], self.convw[:, j, k:k + 1], acc, ALU.mult, ALU.add),
                     reads=[self.b_conv[j], self.b_const], writes=[self.b_cacc[q]])
            s.op("dve", lambda e, j=j: e.tensor_copy(self.ctail[:, j, :], cb[:, j, NT:NT + 3]), reads=[self.b_conv[j]], writes=[self.b_ctail])
            if j < 8:
                dst, bd = self.xsb[:, j, :], self.b_xsb[j]
            elif j < 10:
                dst, bd = self.Bfm[:, j - 8, :], self.b_BC[j - 8]
            else:
                dst, bd = self.Cfm[:, j - 10, :], self.b_BC[j - 8]
            s.op("act", lambda e, j=j, acc=acc, dst=dst: e.activation(out=dst, in_=acc, func=AF.Silu, bias=fv[:, 0, j:j + 1], scale=1.0),
                 reads=[self.b_cacc[q], self.b_const], writes=[bd])

        dt_e, dtv, a_, ac, dtw = [self.dtt[:, i, :] for i in range(5)]
        bd = self.b_dtt
        s.op("act", lambda e: e.activation(out=dtv, in_=dt_e, func=AF.Ln, bias=self.cst[0:16, 1:2], scale=1.0),
             reads=[bd[0], self.b_const], writes=[bd[1]])
        s.op("dve", lambda e: e.tensor_scalar(a_, dtv, self.ssd16[:, 2:3], None, ALU.mult), reads=[bd[1], self.b_const], writes=[bd[2]])
        NCH = NT // 128
        for c in range(NCH):
            sl = slice(c * 128, (c + 1) * 128)
            s.op("dve", lambda e, sl=sl: e.tensor_tensor_scan(ac[:, sl], self.ones16[:, :], a_[:, sl], 0.0, ALU.mult, ALU.add),
                 reads=[bd[2], self.b_const], writes=[bd[3]])
        for c in range(NCH):
            sl = slice(c * 128, (c + 1) * 128)
            s.op("act", lambda e, sl=sl, c=c: e.activation(out=dt_e[:, sl], in_=ac[:, sl], func=AF.Exp, bias=ac[:, c * 128 + 127:c * 128 + 128], scale=-1.0),
                 reads=[bd[3]], writes=[bd[0]])
        s.op("dve", lambda e: e.tensor_tensor(dtw, dtv, dt_e, ALU.mult), reads=[bd[0], bd[1]], writes=[bd[4]])

        idf = self.cmask[0:16, 0, 0:16]
        for c in range(NCH):
            self.ssd_chunk(c, ac, dtv, dtw, bd, idf)
        for g in range(2):
            self.sumsq_rstd([self.yg[:, 4 * g + jj, :] for jj in range(4)], self.b_yg[4 * g:4 * g + 4], 4, 512.0)
            for jj in range(4):
                j = 4 * g + jj
                q = j % 2
                s.op("dve", lambda e, j=j, q=q: e.tensor_tensor(self.tmpn[:, q, :], self.yg[:, j, :], self.rstd[:, :], ALU.mult),
                     reads=[self.b_yg[j], self.b_rstd], writes=[self.b_tmpn[q]])
                s.op("act", lambda e, j=j, q=q: e.activation(out=self.cat[:, j, :], in_=self.tmpn[:, q, :], func=AF.Copy, scale=fv[:, 2, j:j + 1]),
                     reads=[self.b_tmpn[q], self.b_const], writes=[self.b_cat[j]])
        self.s5_tile()
        if "cat" in self.dbg and ti == 0:
            self.d_dbg = self.dram("dbg_cat", [128, 16, NT], BF16, "ExternalOutput")
            s.dma(lambda e: e.dma_start(out=self.d_dbg, in_=self.cat), reads=self.b_cat)
            self.d_dbg2 = self.dram("dbg_yg", [128, 8, NT], F32, "ExternalOutput")
            s.dma(lambda e: e.dma_start(out=self.d_dbg2, in_=self.yg), reads=self.b_yg)
            self.d_dbg4 = self.dram("dbg_xsb", [128, 8, NT], BF16, "ExternalOutput")
            s.dma(lambda e: e.dma_start(out=self.d_dbg4, in_=self.xsb), reads=self.b_xsb)
            self.d_dbg5 = self.dram("dbg_bc", [128, 4, NT], BF16, "ExternalOutput")
            s.dma(lambda e: e.dma_start(out=self.d_dbg5[:, 0:2, :], in_=self.Bfm), reads=self.b_BC)
            s.dma(lambda e: e.dma_start(out=self.d_dbg5[:, 2:4, :], in_=self.Cfm), reads=self.b_BC)
            self.d_dbg6 = self.dram("dbg_dtt", [16, 5, NT], F32, "ExternalOutput")
            s.dma(lambda e: e.dma_start(out=self.d_dbg6, in_=self.dtt[:, :, :]), reads=self.b_dtt)
            self.d_dbg3 = self.dram("dbg_ys", [128, 8, NT], F32, "ExternalOutput")
            s.dma(lambda e: e.dma_start(out=self.d_dbg3, in_=self.ys), reads=[self.b_ys])
            s.barrier()
        def on_o(mi, mw, ps, bps):
            self.evac("act", self.yt[:, mi, :], ps[:, 0:NT], [bps], [self.b_y[mi]])
        self.proj("hyb_w_out", 0, 1024, 16, lambda k: self.cat[:, k, :], self.b_cat, on_o, panel_cols=256)
        self.post_norm_residual(0, 0)

    def ssd_chunk(self, c, ac, dtv, dtw, bd, idf):
        s = self.s
        fv = self.fvec
        sl = slice(c * 128, (c + 1) * 128)
        tq = c % 2
        tok = self.tok[:, tq, :]
        btok = self.b_tok[tq]
        pt, bpt = self.psum()
        for i, (src, bsrc) in enumerate(((ac, bd[3]), (dtv, bd[1]), (dtw, bd[4]))):
            s.op("pe", lambda e, i=i, src=src, sl=sl: e.transpose(pt[:, i * 16:(i + 1) * 16], src[:, sl], idf),
                 reads=[bsrc, self.b_const], writes=[bpt])
        self.evac("dve", tok, pt[:, 0:48], [bpt], [btok])
        px, bpx = self.psum_bf()
        pxb = px[:, :].bitcast(BF16)
        for j in range(8):
            s.op("pe", lambda e, j=j, sl=sl: e.transpose(pxb[:, j * 128:(j + 1) * 128], self.xsb[:, j, sl], self.ident_bf[:, :]),
                 reads=[self.b_xsb[j], self.b_const], writes=[bpx])
        for i in range(2):
            s.op("dve", lambda e, i=i: e.tensor_tensor(self.xdt[:, i, :].rearrange("p (h d) -> p h d", h=16),
                                                       pxb.rearrange("p (h d) -> p h d", h=16),
                                                       tok[:, 16 * (i + 1):16 * (i + 2)].unsqueeze(2).to_broadcast([128, 16, 64]), ALU.mult),
                 reads=[bpx, btok], writes=[self.b_xdt[i]])
        pb, bpb = self.psum_bf()
        pbb = pb[:, :].bitcast(BF16)
        for g in range(2):
            s.op("pe", lambda e, g=g, sl=sl: e.transpose(pbb[:, g * 128:(g + 1) * 128], self.Bfm[:, g, sl], self.ident_bf[:, :]),
                 reads=[self.b_BC[g], self.b_const], writes=[bpb])
        self.evac("act", self.Btok[:, :], pbb[:, 0:256], [bpb], [self.b_Btok])
        pc, bpc = self.psum()
        for g in range(2):
            s.op("pe", lambda e, g=g, sl=sl: e.matmul(pc[:, g * 128:(g + 1) * 128], self.Bfm[:, g, sl], self.Cfm[:, g, sl], start=True, stop=True),
                 reads=[self.b_BC[g], self.b_BC[2 + g]], writes=[bpc])
        s.op("dve", lambda e: e.tensor_tensor(self.cbm[:, :].rearrange("p (g l) -> p g l", g=2), pc[:, 0:256].rearrange("p (g l) -> p g l", g=2),
                                              self.cmask[:, 1, :].unsqueeze(1).to_broadcast([128, 2, 128]), ALU.mult),
             reads=[bpc, self.b_const], writes=[self.b_cbm])
        pst = []
        for g in range(2):
            p_, bp_ = self.psum_fixed(4 + g)
            s.op("pe", lambda e, g=g, p_=p_: e.matmul(p_[:, :], self.Btok[:, g * 128:(g + 1) * 128], self.xdt[:, 1, g * 512:(g + 1) * 512], start=True, stop=True),
                 reads=[self.b_Btok, self.b_xdt[1]], writes=[bp_])
            pst.append((p_, bp_))
        py = None
        for h in range(16):
            g = h // 8
            hq = h % 2
            if h % 8 == 0:
                py, bpy = self.psum_fixed(6)
            j4 = (h % 8) // 2
            pa, bpa = self.psum()
            s.op("pe", lambda e, h=h, sl=sl, pa=pa: e.matmul(pa[:, 0:128], self.sel[:, h, :], ac[:, sl], start=True, stop=True),
                 reads=[bd[3], self.b_const], writes=[bpa])
            s.op("dve", lambda e, h=h, hq=hq, pa=pa: e.tensor_scalar(self.D1[:, hq, :], pa[:, 0:128], tok[:, h:h + 1], 0.0, ALU.subtract, ALU.min),
                 reads=[bpa, btok], writes=[self.b_D1[hq]])
            s.op("act", lambda e, hq=hq: e.activation(out=self.D1[:, hq, :], in_=self.D1[:, hq, :], func=AF.Exp),
                 reads=[self.b_D1[hq]], writes=[self.b_D1[hq]])
            s.op("pool", lambda e, hq=hq, g=g: e.tensor_tensor(self.Mh[:, hq, :], self.D1[:, hq, :], self.cbm[:, g * 128:(g + 1) * 128], ALU.mult),
                 reads=[self.b_D1[hq], self.b_cbm], writes=[self.b_Mh[hq]])
            s.op("act", lambda e, hq=hq, pa=pa: e.activation(out=self.Eh[:, hq, :], in_=pa[:, 0:128], func=AF.Exp),
                 reads=[bpa], writes=[self.b_Eh[hq]])
            s.op("pool", lambda e, hq=hq, g=g, sl=sl: e.tensor_tensor(self.Csh[:, hq, :], self.Cfm[:, g, sl], self.Eh[:, hq, :], ALU.mult),
                 reads=[self.b_BC[2 + g], self.b_Eh[hq]], writes=[self.b_Csh[hq]])
            yo = py[hq * 64:(hq + 1) * 64, j4 * 128:(j4 + 1) * 128]
            s.op("pe", lambda e, h=h, hq=hq, yo=yo: e.matmul(yo, self.xdt[:, 0, h * 64:(h + 1) * 64], self.Mh[:, hq, :], start=True, stop=False),
                 reads=[self.b_xdt[0], self.b_Mh[hq]], writes=[bpy])
            s.op("pe", lambda e, h=h, hq=hq, yo=yo: e.matmul(yo, self.Hbf[:, h * 64:(h + 1) * 64], self.Csh[:, hq, :], start=False, stop=True),
                 reads=[self.b_Hbf[h], self.b_Csh[hq]], writes=[bpy])
            p_, bp_ = pst[g]
            hs = slice(h * 64, (h + 1) * 64)
            s.op("dve", lambda e, hs=hs, hq=hq, p_=p_, h=h: e.scalar_tensor_tensor(self.Hst[:, hs], self.Hst[:, hs], self.Eh[:, hq, 127:128],
                                                                                p_[:, (h % 8) * 64:(h % 8 + 1) * 64], ALU.mult, ALU.add),
                 reads=[self.b_Eh[hq], bp_], writes=[self.b_Hst[h]])
            s.op("pool", lambda e, hs=hs: e.tensor_copy(self.Hbf[:, hs], self.Hst[:, hs]), reads=[self.b_Hst[h]], writes=[self.b_Hbf[h]])
            if h % 8 == 7:
                for jj in range(4):
                    j = 4 * g + jj
                    s.op("dve", lambda e, j=j, jj=jj, sl=sl, py=py: e.scalar_tensor_tensor(self.yg[:, j, sl], self.xsb[:, j, sl], fv[:, 1, j:j + 1],
                                                                                             py[:, jj * 128:(jj + 1) * 128], ALU.mult, ALU.add),
                         reads=[self.b_xsb[j], bpy, self.b_const], writes=[self.b_yg[j]] + self.b_conv)
                    s.op("pool", lambda e, j=j, sl=sl: e.tensor_tensor(self.yg[:, j, sl], self.yg[:, j, sl], self.zs[:, j, sl], ALU.mult),
                         reads=[self.b_zs[j]], writes=[self.b_yg[j]])

    def s5_sub(self, ts, T):
        s = self.s
        H = self.Hall
        bH = self.b_Hall
        m = self.mtmp
        alias = self.b_conv + self.b_cacc
        for ri in range(2):
            pb, bpb = self.psum()
            for q in range(32):
                fc, rb = q // 4, q % 4
                s.op("pe", lambda e, pb=pb, q=q, fc=fc, rb=rb, ri=ri: e.matmul(
                    pb[:, q * T:(q + 1) * T], self.Wb[32 * rb:32 * rb + 32, fc, ri, :], self.ubf[32 * rb:32 * rb + 32, fc, ts:ts + T],
                    start=True, stop=True, tile_position=(32 * rb, 0)), reads=[self.b_s5w, self.b_ubf[fc]], writes=[bpb])
            s.op("act", lambda e, pb=pb, ri=ri: e.activation(out=H[:, ri, :, 1:1 + T], in_=pb[:, 0:32 * T].rearrange("p (q t) -> p q t", q=32), func=AF.Copy),
                 reads=[bpb], writes=[bH[ri]])
        Br, Bi = H[:, 0, :, 1:1 + T], H[:, 1, :, 1:1 + T]
        s.op("dve", lambda e: e.tensor_tensor(m[:, 0:2, :, :], H[:, :, :, 1:1 + T], self.Tinv[:, :, :, :], ALU.mult),
             reads=[bH[0], bH[1], self.b_s5w], writes=[self.b_m12] + alias)
        s.op("pool", lambda e: e.tensor_tensor(m[:, 2, :, :], Br, self.Tinv[:, 1, :, :], ALU.mult),
             reads=[bH[0], self.b_s5w], writes=[self.b_m3] + alias)
        s.op("dve", lambda e: e.tensor_tensor(Bi, Bi, self.Tinv[:, 0, :, :], ALU.mult), reads=[self.b_s5w], writes=[bH[1]])
        s.op("dve", lambda e: e.tensor_tensor(Br, m[:, 0, :, :], m[:, 1, :, :], ALU.subtract), reads=[self.b_m12], writes=[bH[0]])
        s.op("dve", lambda e: e.tensor_tensor(Bi, Bi, m[:, 2, :, :], ALU.add), reads=[self.b_m3], writes=[bH[1]])
        flat = H[:, :, :, :].rearrange("p a q t -> p (a q t)")
        s.op("dve", lambda e: e.tensor_tensor_scan(flat, self.smask[:, :, :, :].rearrange("p a q t -> p (a q t)"), flat, 0.0, ALU.mult, ALU.add),
             reads=[self.b_s5w], writes=[bH[0], bH[1]])
        s.op("dve", lambda e: e.tensor_tensor(m[:, 0:2, :, :], H[:, :, :, 1:1 + T], self.Tfwd[:, :, :, :], ALU.mult),
             reads=[bH[0], bH[1], self.b_s5w], writes=[self.b_m12])
        s.op("pool", lambda e: e.tensor_tensor(m[:, 2, :, :], Br, self.Tfwd[:, 1, :, :], ALU.mult),
             reads=[bH[0], self.b_s5w], writes=[self.b_m3])
        s.op("dve", lambda e: e.tensor_tensor(Bi, Bi, self.Tfwd[:, 0, :, :], ALU.mult), reads=[self.b_s5w], writes=[bH[1]])
        s.op("dve", lambda e: e.tensor_tensor(self.Hs16[:, 0, :, :], m[:, 0, :, :], m[:, 1, :, :], ALU.subtract),
             reads=[self.b_m12], writes=[self.b_Hs16[0]])
        s.op("pool", lambda e: e.tensor_tensor(self.Hs16[:, 1, :, :], m[:, 2, :, :], Bi, ALU.add),
             reads=[self.b_m3, bH[1]], writes=[self.b_Hs16[1]])
        s.op("dve", lambda e: e.tensor_tensor(H[:, 0, :, 0], m[:, 0, :, T - 1], m[:, 1, :, T - 1], ALU.subtract),
             reads=[self.b_m12], writes=[bH[0]])
        s.op("pool", lambda e: e.tensor_tensor(H[:, 1, :, 0], m[:, 2, :, T - 1], H[:, 1, :, T], ALU.add),
             reads=[self.b_m3], writes=[bH[1]])
        po, bpo = self.psum()
        for fc in range(8):
            n = 0
            for rb in range(4):
                q = 4 * fc + rb
                for ri in range(2):
                    s.op("pe", lambda e, fc=fc, q=q, ri=ri, n=n: e.matmul(po[:, fc * T:(fc + 1) * T], self.Wc[:, q, ri, :], self.Hs16[:, ri, q, :],
                                                                         start=(n == 0), stop=(n == 7)),
                         reads=[self.b_s5w, self.b_Hs16[ri]], writes=[bpo])
                    n += 1
        s.op("act", lambda e, po=po: e.activation(out=self.ys[:, :, ts:ts + T], in_=po[:, 0:8 * T].rearrange("p (j t) -> p j t", j=8), func=AF.Copy),
             reads=[bpo], writes=[self.b_ys])

    def s5_tile(self):
        s = self.s
        fv = self.fvec
        T = self.TS
        for st_i in range(NT // T):
            self.s5_sub(st_i * T, T)
        for j in range(8):
            q = j % 2
            s.op("dve", lambda e, j=j, q=q: e.scalar_tensor_tensor(self.tmpn[:, q, :], self.ubf[:, j, :], fv[:, 3, j:j + 1], self.ys[:, j, :], ALU.mult, ALU.add),
                 reads=[self.b_ubf[j], self.b_ys, self.b_const], writes=[self.b_tmpn[q]])
            s.op("act", lambda e, j=j, q=q: e.activation(out=self.g5b[:, j, :], in_=self.tmpn[:, q, :], func=AF.Gelu),
                 reads=[self.b_tmpn[q]], writes=[self.b_g5b[j]])

        def on_g(mi, mw, ps, bps):
            q = mi % 2
            s.op("act", lambda e: e.activation(out=self.tmpn[:, q, :], in_=ps[:, 0:NT], func=AF.Sigmoid, bias=fv[:, 4, mi:mi + 1], scale=1.0),
                 reads=[bps, self.b_const], writes=[self.b_tmpn[q]])
            s.op("dve", lambda e: e.tensor_tensor(self.cat[:, 8 + mi, :], self.g5b[:, mi, :], self.tmpn[:, q, :], ALU.mult),
                 reads=[self.b_tmpn[q], self.b_g5b[mi]], writes=[self.b_cat[8 + mi]])
        self.proj("s5_glu_w", 0, 1024, KC, lambda k: self.g5b[:, k, :], self.b_g5b, on_g)

    def mixer1(self, ti):
        s = self.s
        t0 = ti * NT
        self.pre_norm(1, 0)
        hrhs = lambda k: self.ht[:, k, :]

        def on_q(mi, mw, ps, bps):
            self.evac("act" if mi % 2 == 0 else "dve", self.Qb[:, mi, :], ps[:, 0:NT], [bps], [self.b_Qb[mi]])
        self.proj("attn_w_qkv", 0, 3072, KC, hrhs, self.b_h, on_q)

        def mk_kv(dst, bdst):
            def on_kv(mi, mw, ps, bps):
                q = mi % 2
                self.evac("act" if mi % 2 == 0 else "dve", self.kvst[:, q, :], ps[:, 0:NT], [bps], [self.b_kvst[q]])
                s.dma(lambda e: e.dma_start(out=dst[mi, :, 2048 + t0:2048 + t0 + NT], in_=self.kvst[:, q, :]),
                      reads=[self.b_kvst[q]], writes=[bdst])
            return on_kv
        self.proj("attn_w_qkv", 3072, 1024, KC, hrhs, self.b_h, mk_kv(self.Kd, self.b_Kd))
        self.proj("attn_w_qkv", 4096, 1024, KC, hrhs, self.b_h, mk_kv(self.Vd, self.b_Vd))
        for h in range(8):
            self.attn_head(ti, h)

        def on_o(mi, mw, ps, bps):
            self.evac("act", self.yt[:, mi, :], ps[:, 0:NT], [bps], [self.b_y[mi]])
        self.proj("attn_w_o", 0, 1024, KC, lambda k: self.attn[:, k, :], self.b_attn, on_o)
        self.post_norm_residual(1, 0)

    def attn_pat(self, ti, h, d, qoff, vprev, vcur, ncur, vb, exp_mask, PT, Ob, bO, Db, bD, Oa, Da):
        s = self.s
        cm = self.cmask
        W = 2048 + NT
        nq = NT // d
        kprev0 = 2048 - 128 * d
        u = ti * nq // 16
        has_prev = ti > 0
        if nq * ti >= 128:
            pmask = cm[:, 2, 0:nq]
        elif has_prev:
            pmask = cm[:, 3 + (nq * ti) // 16, 0:nq]
        pS, bpS = self.psum()
        for r in range(d):
            if has_prev:
                s.op("pe", lambda e, r=r: e.matmul(pS[:, r * nq:(r + 1) * nq], self.Kw[:, kprev0 + r:2048:d], self.Qb[:, qoff + h, r:NT:d], start=True, stop=True),
                     reads=[self.b_Kw, self.b_Qb[qoff + h]], writes=[bpS])
            s.op("pe", lambda e, r=r: e.matmul(pS[0:ncur, 256 + r * nq:256 + (r + 1) * nq], self.Kw[:, 2048 + r:W:d], self.Qb[:, qoff + h, r:NT:d], start=True, stop=True),
                 reads=[self.b_Kw, self.b_Qb[qoff + h]], writes=[bpS])
        parts = []
        if has_prev:
            parts.append((128, 0, NT, pmask.unsqueeze(1).to_broadcast([128, d, nq]), d, nq))
        parts.append((ncur, 256, NT, cm[0:ncur, 3, 0:nq].unsqueeze(1).to_broadcast([ncur, d, nq]), d, nq))
        exp_mask(pS, bpS, parts)
        dview = Db[:, 0:NT]
        if has_prev:
            s.op("pe", lambda e: e.matmul(dview, self.ones_bf[:, :], PT[:, 0:NT], start=True, stop=False),
                 reads=[self.b_PT[0], self.b_const], writes=[bD])
        s.op("pe", lambda e: e.matmul(dview, self.ones_bf[0:ncur, :], PT[0:ncur, 256:256 + NT], start=(not has_prev), stop=True),
             reads=[self.b_PT[0], self.b_const], writes=[bD])
        for r in range(d):
            if has_prev:
                s.op("pe", lambda e, r=r: e.matmul(Ob[:, r:NT:d], self.Vtok[:, vprev + r, :], PT[:, r * nq:(r + 1) * nq], start=True, stop=False),
                     reads=[self.b_PT[0], vb[vprev + r]], writes=[bO])
            s.op("pe", lambda e, r=r: e.matmul(Ob[:, r:NT:d], self.Vtok[0:ncur, vcur + r, :], PT[0:ncur, 256 + r * nq:256 + (r + 1) * nq], start=(not has_prev), stop=True),
                 reads=[self.b_PT[0], vb[vcur + r]], writes=[bO])
        s.op("dve", lambda e: e.tensor_tensor(Oa, Oa, Ob[:, 0:NT], ALU.add), reads=[bO], writes=[self.b_Oacc[0]])
        s.op("dve", lambda e: e.tensor_tensor(Da.rearrange("p (q r) -> p r q", r=d), Da.rearrange("p (q r) -> p r q", r=d),
                                              Db[:, 0:NT].rearrange("p (r q) -> p r q", r=d), ALU.add), reads=[bD], writes=[self.b_Oacc[1]])

    def attn_head(self, ti, h):
        s = self.s
        t0 = ti * NT
        W = 2048 + NT
        cm = self.cmask
        scale = float(128 ** -0.5)
        s.dma(lambda e: e.dma_start(out=self.Kw[:, 0:W], in_=self.Kd[h, :, t0:t0 + W]), reads=[self.b_Kd], writes=[self.b_Kw])
        s.dma(lambda e: e.dma_start(out=self.Vw[:, 0:W], in_=self.Vd[h, :, t0:t0 + W]), reads=[self.b_Vd], writes=[self.b_Vw])
        blocks = []
        for i in range(3):
            blocks.append((i, slice(1920 + 128 * i, 2048 + 128 * i), 128))
        for r in range(4):
            blocks.append((3 + r, slice(1536 + r, 2048, 4), 128))
        for r in range(4):
            blocks.append((7 + r, slice(2048 + r, W, 4), 64))
        for r in range(16):
            blocks.append((11 + r, slice(r, 2048, 16), 128))
        for r in range(16):
            blocks.append((27 + r, slice(2048 + r, W, 16), 16))
        groups = [(0, 7, 128), (7, 11, 64), (11, 19, 128), (19, 27, 128), (27, 35, 16), (35, 43, 16)]
        vb = {}
        for gi, (a, b, nk) in enumerate(groups):
            pv, bpv = self.psum_bf()
            pvb = pv[:, :].bitcast(BF16)
            for (vidx, sl, nk_) in blocks[a:b]:
                c0 = (vidx - a) * 128
                s.op("pe", lambda e, pvb=pvb, sl=sl, c0=c0, nk_=nk_: e.transpose(pvb[0:nk_, c0:c0 + 128], self.Vw[:, sl], self.ident_bf[:, :]),
                     reads=[self.b_Vw, self.b_const], writes=[bpv])
                vb[vidx] = self.b_Vtok[gi]
            eng = "dve" if gi % 2 == 0 else "act"
            self.evac(eng, self.Vtok[0:nk, a:b, :], pvb[0:nk, 0:(b - a) * 128].rearrange("p (j e) -> p j e", j=b - a), [bpv], [self.b_Vtok[gi]])
        Ob, bO = self.psum_fixed(4)
        Db, bD = self.psum_fixed(5)
        Oa, Da = self.Oacc[:, 0, :], self.Oacc[:, 1, :]
        PT = self.PT[:, :, :].rearrange("p a n -> p (a n)")

        def exp_mask(pS, bpS, parts):
            for (nk, c0, nc_, mask, nrep, nq) in parts:
                s.op("act", lambda e, nk=nk, c0=c0, nc_=nc_: e.activation(out=PT[0:nk, c0:c0 + nc_], in_=pS[0:nk, c0:c0 + nc_], func=AF.Exp, scale=scale),
                     reads=[bpS], writes=[self.b_PT[0]])
                s.op("dve", lambda e, nk=nk, c0=c0, nc_=nc_, mask=mask, nrep=nrep, nq=nq: e.tensor_tensor(
                    PT[0:nk, c0:c0 + nc_].rearrange("p (a q) -> p a q", a=nrep), PT[0:nk, c0:c0 + nc_].rearrange("p (a q) -> p a q", a=nrep),
                    mask, ALU.mult), reads=[self.b_const], writes=[self.b_PT[0]])

        pS, bpS = self.psum()
        NQB = NT // 128
        for qb in range(NQB):
            for blk in range(2):
                c0 = (qb * 2 + blk) * 128
                s.op("pe", lambda e, qb=qb, blk=blk, c0=c0: e.matmul(pS[:, c0:c0 + 128], self.Kw[:, 1920 + 128 * (qb + blk):2048 + 128 * (qb + blk)],
                                                                    self.Qb[:, h, qb * 128:(qb + 1) * 128], start=True, stop=True),
                     reads=[self.b_Kw, self.b_Qb[h]], writes=[bpS])
        exp_mask(pS, bpS, [(128, qb * 256, 256, cm[:, 2:4, :], 2, 128) for qb in range(NQB)])
        if ti == 0:
            s.op("dve", lambda e: e.memset(PT[:, 0:128], 0.0), writes=[self.b_PT[0]])
        PT4 = PT[:, 0:NQB * 256].rearrange("p (a b q) -> p a b q", a=NQB, b=2)
        for blk in range(2):
            s.op("pe", lambda e, blk=blk: e.matmul(Db[:, 0:NT].rearrange("p (a q) -> p a q", a=NQB), self.ones_bf[:, :], PT4[:, :, blk, :],
                                                  start=(blk == 0), stop=(blk == 1)), reads=[self.b_PT[0], self.b_const], writes=[bD])
        for qb in range(NQB):
            for blk in range(2):
                s.op("pe", lambda e, qb=qb, blk=blk: e.matmul(Ob[:, qb * 128:(qb + 1) * 128], self.Vtok[:, qb + blk, :], PT4[:, qb, blk, :],
                                                             start=(blk == 0), stop=(blk == 1)), reads=[self.b_PT[0], vb[qb + blk]], writes=[bO])
        self.evac("act", Oa, Ob[:, 0:NT], [bO], [self.b_Oacc[0]])
        self.evac("act", Da, Db[:, 0:NT], [bD], [self.b_Oacc[1]])

        for (d, qoff, vprev, vcur, ncur) in ((4, 8, 3, 7, 64), (16, 16, 11, 27, 16)):
            self.attn_pat(ti, h, d, qoff, vprev, vcur, ncur, vb, exp_mask, PT, Ob, bO, Db, bD, Oa, Da)
        s.op("dve", lambda e: e.reciprocal(Da, Da), reads=[self.b_Oacc[1]], writes=[self.b_Oacc[1]])
        s.op("dve", lambda e: e.tensor_tensor(self.attn[:, h, :], Oa, Da, ALU.mult), reads=self.b_Oacc, writes=[self.b_attn[h]])

    def build(self):
        s = self.s
        L = self.L
        self.setup_consts()
        self.ada_mod()
        names = []
        for li in self.layers:
            names += ["ffn_w_in%d" % li, "ffn_w_out%d" % li]
        if self.mixers and 0 in self.layers:
            names += ["hyb_w_in", "hyb_w_out", "s5_glu_w"]
        if self.mixers and 1 in self.layers:
            names += ["attn_w_qkv", "attn_w_o"]
        self.cast_weights(names)
        s.barrier()
        if self.mixers:
            self.setup_mix()
            s.barrier()
        last = []
        for ti in range(L // NT):
            t0 = ti * NT
            for j in range(KC):
                s.dma(lambda e, a=self.xt[:, j, :], b=self.x_in[j * 128:(j + 1) * 128, t0:t0 + NT]: e.dma_start(out=a, in_=b),
                      writes=[self.b_x[j]])
            for li in self.layers:
                if self.mixers:
                    if li == 0:
                        self.mixer0(ti)
                    else:
                        self.mixer1(ti)
                self.ffn(li)
            for j in range(KC):
                t = s.dma(lambda e, a=self.out[j * 128:(j + 1) * 128, t0:t0 + NT], b=self.xt[:, j, :]: e.dma_start(out=a, in_=b),
                          reads=[self.b_x[j]])
                last.append(t)
        s.finish(last)
        return self.nc


def make_inmaps(inp, L, n_cores=N_CORES):
    f = np.float32
    common = {}
    common["ada_w"] = np.ascontiguousarray(inp["ada_w"], dtype=f)
    common["ada_b"] = np.ascontiguousarray(inp["ada_b"].reshape(2, 48, 128).transpose(2, 0, 1), dtype=f)
    g = np.stack([inp["mix_pre_g"], inp["mix_post_g"], inp["ffn_pre_g"], inp["ffn_post_g"]])
    common["gains"] = np.ascontiguousarray(g.reshape(4, 2, KC, 128).transpose(3, 0, 1, 2), dtype=f)
    for li in range(2):
        common["ffn_w_in%d" % li] = np.ascontiguousarray(inp["ffn_w_in"][li], dtype=f)
        common["ffn_w_out%d" % li] = np.ascontiguousarray(inp["ffn_w_out"][li], dtype=f)
    common["hyb_w_in"] = np.ascontiguousarray(inp["hyb_w_in"][0], dtype=f)
    common["hyb_w_out"] = np.ascontiguousarray(inp["hyb_w_out"][0], dtype=f)
    common["s5_glu_w"] = np.ascontiguousarray(inp["s5_glu_w"][0], dtype=f)
    common["attn_w_qkv"] = np.ascontiguousarray(inp["attn_w_qkv"][0], dtype=f)
    common["attn_w_o"] = np.ascontiguousarray(inp["attn_w_o"][0], dtype=f)
    kj = np.arange(128)[:, None]; qi = np.arange(128)[None, :]
    cm = np.zeros((128, 11, 128), f)
    cm[:, 0, :] = np.eye(128, dtype=f)
    cm[:, 1, :] = (qi >= kj)
    cm[:, 2, :] = (kj >= qi)
    cm[:, 3, :] = (kj <= qi)
    for u in range(1, 8):
        cm[:, 3 + u, :] = (kj >= qi) & (kj >= 128 - 16 * u)
    common["cmask"] = cm
    sel = np.zeros((16, 16, 128), f)
    for h in range(16):
        sel[h, h, :] = 1.0
    common["sel"] = sel.reshape(16, 16 * 128)
    common["convw"] = np.ascontiguousarray(inp["ssd_conv_w"][0].reshape(4, 12, 128).transpose(2, 1, 0), dtype=f)
    fv = np.zeros((128, 5, 12), f)
    fv[:, 0, :] = inp["ssd_conv_b"][0].reshape(12, 128).T
    fv[:, 1, :8] = np.repeat(inp["ssd_d"][0], 64).reshape(8, 128).T
    fv[:, 2, :8] = inp["ssd_norm_g"][0].reshape(8, 128).T
    fv[:, 3, :8] = inp["s5_d"][0].reshape(8, 128).T
    fv[:, 4, :8] = inp["s5_glu_b"][0].reshape(8, 128).T
    common["fvec"] = fv
    common["ssd16"] = np.ascontiguousarray(np.stack([inp["ssd_dt_bias"][0], inp["ssd_a_log"][0]], axis=1), dtype=f)
    lre, lim, ldt = inp["s5_lambda_re"][0], inp["s5_lambda_im"][0], inp["s5_log_dt"][0]
    v1 = np.zeros((128, 3, 32), f)
    for q in range(32):
        for gi in range(2):
            g = 2 * q + gi
            v1[gi * 64:(gi + 1) * 64, 0, q] = lre[g]
            v1[gi * 64:(gi + 1) * 64, 1, q] = lim[g]
            v1[gi * 64:(gi + 1) * 64, 2, q] = ldt[g]
    common["s5v1"] = v1
    v2 = np.zeros((128, 5, 8, 2, 64), f)
    bre, bim = inp["s5_b_re"][0], inp["s5_b_im"][0]
    for fc in range(8):
        for rb in range(4):
            for gi2 in range(2):
                g = 8 * fc + 2 * rb + gi2
                rows = slice(32 * rb, 32 * rb + 32)
                v2[rows, 0, fc, gi2, :] = lre[g][None, :]
                v2[rows, 1, fc, gi2, :] = lim[g][None, :]
                v2[rows, 2, fc, gi2, :] = ldt[g]
                r2 = slice(32 * rb + 16 * gi2, 32 * rb + 16 * gi2 + 16)
                v2[r2, 3, fc, gi2, :] = bre[g].T
                v2[r2, 4, fc, gi2, :] = bim[g].T
    common["s5v2"] = v2.reshape(128, 5, 1024)
    cre, cim = inp["s5_c_re"][0], inp["s5_c_im"][0]
    wc = np.zeros((2, 64, 32, 2, 8, 16), f)
    for q in range(32):
        rb = q % 4
        for gi in range(2):
            g = 2 * q + gi
            wc[gi, :, q, 0, 2 * rb + gi, :] = cre[g].T
            wc[gi, :, q, 1, 2 * rb + gi, :] = cim[g].T
    common["s5wc"] = wc.reshape(128, 32 * 2 * 128)
    maps = []
    nb = inp["x"].shape[0]
    for c in range(n_cores):
        b = c % nb
        m = dict(common)
        m["x_fm"] = np.ascontiguousarray(inp["x"][b, :L].T, dtype=f)
        m["c_fm"] = np.ascontiguousarray(inp["c"][b].reshape(KC, 128).T, dtype=f)
        maps.append(m)
    return maps


_NC_CACHE = {}


def kernel(**inputs):
    inp = {k: np.asarray(v) for k, v in inputs.items()}
    B_, L, _ = inp["x"].shape
    if L not in _NC_CACHE:
        _NC_CACHE[L] = Builder(L).build()
    nc = _NC_CACHE[L]
    maps = make_inmaps(inp, L)
    res = run_bass_kernel_spmd(nc, maps, core_ids=list(range(N_CORES)))
    out = np.stack([res.results[b]["out_fm"].T for b in range(B_)])
    return np.ascontiguousarray(out.astype(np.float32))
```

```python
from contextlib import ExitStack
import numpy as np
import concourse.bass as bass
import concourse.mybir as mybir
from concourse.bass_utils import run_bass_kernel_spmd

F32 = mybir.dt.float32
BF16 = mybir.dt.bfloat16
ALU = mybir.AluOpType
AF = mybir.ActivationFunctionType

COMPUTE = ("pe", "dve", "act", "pool")
QUEUES = ("sp",)


class Buf:
    __slots__ = ("name", "w", "r")

    def __init__(self, name=""):
        self.name = name
        self.w = None
        self.r = []


class Sched:
    def __init__(self, nc, stack, n_dma_sems=32):
        self.nc = nc
        self.ops = {e: [] for e in COMPUTE + QUEUES}
        self.sems = {}
        self.cnt = {}
        for e in COMPUTE:
            self.sems[e] = stack.enter_context(nc.semaphore("sem_" + e))
            self.cnt[e] = 0
        self.dma_sems = []
        for i in range(n_dma_sems):
            k = "dma%d" % i
            self.sems[k] = stack.enter_context(nc.semaphore("sem_" + k))
            self.cnt[k] = 0
            self.dma_sems.append(k)
        self.dma_rr = 0
        self.dma_last_tok = {k: None for k in self.dma_sems}
        self.seen = {e: {} for e in COMPUTE + QUEUES}
        self.n_inst = 0

    def buf(self, name=""):
        return Buf(name)

    def _collect(self, eng, reads, writes, extra=()):
        waits = {}
        seen = self.seen[eng]

        def add(tok):
            if tok is None:
                return
            k, v = tok
            if seen.get(k, 0) >= v:
                return
            if waits.get(k, 0) < v:
                waits[k] = v

        for b in reads:
            add(b.w)
        for b in writes:
            add(b.w)
            for t in b.r:
                add(t)
        for t in extra:
            add(t)
        for k, v in waits.items():
            seen[k] = v
        return list(waits.items())

    def _commit(self, tok, reads, writes):
        for b in reads:
            b.r.append(tok)
            if len(b.r) > 64:
                m = {}
                for k, v in b.r:
                    if m.get(k, 0) < v:
                        m[k] = v
                b.r = list(m.items())
        for b in writes:
            b.w = tok
            b.r = []

    def op(self, eng, fn, reads=(), writes=()):
        waits = self._collect(eng, reads, writes)
        self.cnt[eng] += 1
        tok = (eng, self.cnt[eng])
        sems = self.sems
        mysem = sems[eng]

        def emit(e):
            for k, v in waits:
                e.wait_ge(sems[k], v)
            fn(e).then_inc(mysem, 1)

        self.ops[eng].append(emit)
        self.n_inst += 1
        self._commit(tok, reads, writes)
        return tok

    def dma(self, fn, reads=(), writes=(), queue="sp"):
        k = self.dma_sems[self.dma_rr]
        self.dma_rr = (self.dma_rr + 1) % len(self.dma_sems)
        prev = self.dma_last_tok[k]
        extra = (prev,) if prev is not None else ()
        waits = self._collect(queue, reads, writes, extra)
        self.cnt[k] += 16
        tok = (k, self.cnt[k])
        self.dma_last_tok[k] = tok
        sems = self.sems
        dsem = sems[k]

        def emit(e):
            for kk, v in waits:
                e.wait_ge(sems[kk], v)
            fn(e).then_inc(dsem, 16)

        self.ops[queue].append(emit)
        self.n_inst += 1
        self._commit(tok, reads, writes)
        return tok

    def barrier(self):
        sems = self.sems
        for eng in COMPUTE + QUEUES:
            waits = []
            for k, c in self.cnt.items():
                if k == eng or c == 0:
                    continue
                if self.seen[eng].get(k, 0) < c:
                    self.seen[eng][k] = c
                    waits.append((k, c))

            def emit(e, waits=waits):
                for kk, v in waits:
                    e.wait_ge(sems[kk], v)

            self.ops[eng].append(emit)

    def finish(self, final_toks):
        nc = self.nc
        sems = self.sems
        ops = self.ops
        fin = {}
        for t in final_toks:
            if t is None:
                continue
            k, v = t
            fin[k] = max(fin.get(k, 0), v)
        with nc.Block() as block:
            @block.tensor
            def _(e):
                for f in ops["pe"]:
                    f(e)

            @block.vector
            def _(e):
                for f in ops["dve"]:
                    f(e)

            @block.scalar
            def _(e):
                for f in ops["act"]:
                    f(e)

            @block.gpsimd
            def _(e):
                for f in ops["pool"]:
                    f(e)

            @block.sync
            def _(e):
                for f in ops["sp"]:
                    f(e)
                for k, v in fin.items():
                    e.wait_ge(sems[k], v)


D = 1024
KC = D // 128
NT = 256
FFN_H = 2816
FC = FFN_H // 128
HYB_IN = 3600
EPS = 1e-6
N_CORES = 8


class Builder:
    def __init__(self, L, layers=(0, 1), mixers=True, dbg=None):
        self.L = L
        self.layers = layers
        self.mixers = mixers
        self.dbg = dbg or []
        self.nc = bass.Bass("TRN2", target_bir_lowering=False)
        self.stack = ExitStack()
        self.s = Sched(self.nc, self.stack)
        self.rr = 0
        self._tensors()

    def dram(self, name, shape, dt, kind="Internal"):
        return self.nc.dram_tensor(name, list(shape), dt, kind=kind).ap()

    def sb(self, name, shape, dt):
        t = self.stack.enter_context(self.nc.sbuf_tensor(name, list(shape), dt))
        return t

    def _tensors(self):
        nc, s, L = self.nc, self.s, self.L
        I = "ExternalInput"
        self.x_in = self.dram("x_fm", [D, L], F32, I)
        self.out = self.dram("out_fm", [D, L], F32, "ExternalOutput")
        self.c_in = self.dram("c_fm", [128, KC], F32, I)
        self.ada_w = self.dram("ada_w", [2, D, 6 * D], F32, I)
        self.ada_b = self.dram("ada_b", [128, 2, 48], F32, I)
        self.gains = self.dram("gains", [128, 4, 2, KC], F32, I)
        self.w_f32 = {}
        self.w_bf = {}
        self.w_buf = {}
        for name, shape in (("ffn_w_in0", [D, 2 * FFN_H]), ("ffn_w_in1", [D, 2 * FFN_H]),
                            ("ffn_w_out0", [FFN_H, D]), ("ffn_w_out1", [FFN_H, D]),
                            ("hyb_w_in", [D, HYB_IN]), ("hyb_w_out", [2 * D, D]),
                            ("s5_glu_w", [D, D]), ("attn_w_qkv", [D, 5 * D]), ("attn_w_o", [D, D])):
            self.w_f32[name] = self.dram(name, shape, F32, I)
            self.w_bf[name] = self.dram(name + "_bf", shape, BF16)
            self.w_buf[name] = s.buf(name)

        self.xt = self.sb("xt", [128, KC, NT], F32); self.b_x = [s.buf("x%d" % j) for j in range(KC)]
        self.ht = self.sb("ht", [128, KC, NT], BF16); self.b_h = [s.buf("h%d" % j) for j in range(KC)]
        self.yt = self.sb("yt", [128, KC, NT], F32); self.b_y = [s.buf("y%d" % j) for j in range(KC)]
        self.sq = self.sb("sq", [128, 2, NT], BF16); self.b_sq = [s.buf("sq0"), s.buf("sq1")]
        self.rstd = self.sb("rstd", [128, NT], F32); self.b_rstd = s.buf("rstd")
        self.tmpn = self.sb("tmpn", [128, 2, NT], F32); self.b_tmpn = [s.buf("tn0"), s.buf("tn1")]
        self.b_act = [s.buf("a%d" % j) for j in range(FC)]
        self.sg = self.sb("sg", [128, 2, NT], F32); self.b_sg = [s.buf("sg0"), s.buf("sg1")]
        self.NW = 2
        self.wp = self.sb("wp", [128, self.NW, 4096], BF16); self.b_wp = [s.buf("wp%d" % i) for i in range(self.NW)]
        self.wp_rr = 0
        self.b_arena = [s.buf("ar%d" % i) for i in range(2)]
        self._tensors_mix()
        self.arena = self.A32[:, 0:4096].rearrange("p (a n) -> p a n", a=2)
        self.act = self.A16[:, 0:FC * NT].rearrange("p (j t) -> p j t", j=FC)
        self.ones_bf = self.sb("ones_bf", [128, 128], BF16); self.b_const = s.buf("const")
        self.cst = self.sb("cst", [128, 4], F32)
        self.cond = self.sb("cond", [128, KC], F32)
        self.mod = self.sb("mod", [128, 2, 48], F32); self.b_mod = s.buf("mod")
        self.adab = self.sb("adab", [128, 2, 48], F32)
        self.gn = self.sb("gn", [128, 4, 2, KC], F32)
        self.vec = self.sb("vec", [128, 2, 6, KC], F32); self.b_vec = s.buf("vec")
        self.ps = []
        self.b_ps = []
        for i in range(8):
            self.ps.append(self.stack.enter_context(nc.psum_tensor("ps%d" % i, [128, 512], F32)))
            self.b_ps.append(s.buf("ps%d" % i))
        self.ps_rr = 0
        self.psbf_rr = 0

    POOL_BANKS = (0, 1, 2)
    BF_BANKS = (3, 7)

    def psum_bf(self):
        i = self.BF_BANKS[self.psbf_rr]
        self.psbf_rr = (self.psbf_rr + 1) % len(self.BF_BANKS)
        return self.ps[i], self.b_ps[i]

    def psum(self):
        i = self.POOL_BANKS[self.ps_rr]
        self.ps_rr = (self.ps_rr + 1) % len(self.POOL_BANKS)
        return self.ps[i], self.b_ps[i]

    def psum_fixed(self, i):
        return self.ps[i], self.b_ps[i]

    def eng3(self):
        e = ("act", "pool", "dve")[self.rr % 3]
        self.rr += 1
        return e

    def cast_weights(self, names):
        s = self.s
        i = 0
        for name in names:
            src, dst = self.w_f32[name], self.w_bf[name]
            K, N = src.shape
            for kb in range(K // 128):
                for c0 in range(0, N, 2048):
                    cw = min(2048, N - c0)
                    slot = i % 2
                    wslot = i % 2
                    i += 1
                    st32 = self.arena[:, slot, 0:cw]
                    stbf = self.wp[:, wslot, 0:cw]
                    s.dma(lambda e, a=st32, b=src[kb * 128:(kb + 1) * 128, c0:c0 + cw]: e.dma_start(out=a, in_=b),
                          writes=[self.b_arena[slot]])
                    eng = self.eng3()
                    if eng == "act":
                        s.op("act", lambda e, a=stbf, b=st32: e.activation(out=a, in_=b, func=AF.Copy),
                             reads=[self.b_arena[slot]], writes=[self.b_wp[wslot]])
                    else:
                        s.op(eng, lambda e, a=stbf, b=st32: e.tensor_copy(a, b),
                             reads=[self.b_arena[slot]], writes=[self.b_wp[wslot]])
                    s.dma(lambda e, a=dst[kb * 128:(kb + 1) * 128, c0:c0 + cw], b=stbf: e.dma_start(out=a, in_=b),
                          reads=[self.b_wp[wslot]], writes=[self.w_buf[name]])

    def setup_consts(self):
        s = self.s
        s.op("pool", lambda e: e.memset(self.ones_bf[:, :], 1.0), writes=[self.b_const])
        s.op("pool", lambda e: e.memset(self.cst[:, 0:1], EPS), writes=[self.b_const])
        s.op("pool", lambda e: e.memset(self.cst[:, 1:2], 1.0), writes=[self.b_const])
        s.op("pool", lambda e: e.memset(self.cst[:, 2:3], 0.0), writes=[self.b_const])
        s.dma(lambda e: e.dma_start(out=self.cond[:, :], in_=self.c_in), writes=[self.b_const])
        s.dma(lambda e: e.dma_start(out=self.adab[:, :, :], in_=self.ada_b), writes=[self.b_const])
        s.dma(lambda e: e.dma_start(out=self.gn[:, :, :, :], in_=self.gains), writes=[self.b_const])
        s.op("act", lambda e: e.activation(out=self.cond[:, :], in_=self.cond[:, :], func=AF.Silu),
             reads=[self.b_const], writes=[self.b_const])

    def ada_mod(self):
        s = self.s
        for li in range(2):
            ps, bps = self.psum()
            for cp in range(12):
                slot = cp % 3
                for half in range(2):
                    c0 = cp * 512 + half * 256
                    slot = (cp * 2 + half) % 2
                    view = self.arena[:, slot, 0:2048].rearrange("p (k n) -> p k n", k=KC)
                    s.dma(lambda e, a=view, b=self.ada_w[li, :, c0:c0 + 256].rearrange("(k p) n -> p k n", p=128):
                          e.dma_start(out=a, in_=b), writes=[self.b_arena[slot]])
                    for mm in range(2):
                        m = (c0 // 128) + mm
                        for k in range(KC):
                            s.op("pe", lambda e, o=ps[:, m:m + 1], w=view[:, k, mm * 128:(mm + 1) * 128], r=self.cond[:, k:k + 1],
                                 st=(k == 0), sp=(k == KC - 1): e.matmul(o, w, r, start=st, stop=sp),
                                 reads=[self.b_arena[slot], self.b_const], writes=[bps])
            s.op("dve", lambda e, o=self.mod[:, li, :], a=ps[:, 0:48], b=self.adab[:, li, :]:
                 e.tensor_tensor(o, a, b, ALU.add), reads=[bps, self.b_const], writes=[self.b_mod])
        for li in range(2):
            for half, (gpre, gpost) in enumerate(((0, 1), (2, 3))):
                o = half * 24
                sh = self.mod[:, li, o:o + 8]
                sc = self.mod[:, li, o + 8:o + 16]
                gt = self.mod[:, li, o + 16:o + 24]
                s.op("dve", lambda e, out=self.vec[:, li, half * 3 + 0, :], sc=sc, g=self.gn[:, gpre, li, :]:
                     e.scalar_tensor_tensor(out, sc, 1.0, g, ALU.add, ALU.mult),
                     reads=[self.b_mod, self.b_const], writes=[self.b_vec])
                s.op("dve", lambda e, out=self.vec[:, li, half * 3 + 1, :], sh=sh: e.tensor_copy(out, sh),
                     reads=[self.b_mod], writes=[self.b_vec])
                s.op("dve", lambda e, out=self.vec[:, li, half * 3 + 2, :], gt=gt, g=self.gn[:, gpost, li, :]:
                     e.tensor_tensor(out, gt, g, ALU.mult), reads=[self.b_mod, self.b_const], writes=[self.b_vec])

    def wpanel(self, name, kc, c0, cols, k0=0):
        s = self.s
        assert kc * cols <= 4096
        slot = self.wp_rr
        self.wp_rr = (self.wp_rr + 1) % self.NW
        view = self.wp[:, slot, 0:kc * cols].rearrange("p (k n) -> p k n", k=kc)
        src = self.w_bf[name][k0 * 128:(k0 + kc) * 128, c0:c0 + cols].rearrange("(k p) n -> p k n", p=128)
        s.dma(lambda e, a=view, b=src: e.dma_start(out=a, in_=b), reads=[self.w_buf[name]], writes=[self.b_wp[slot]])
        return view, self.b_wp[slot]

    def sumsq_rstd(self, src, b_src, nchunks, denom):
        s = self.s
        ps, bps = self.psum()
        for j in range(nchunks):
            q = j % 2
            eng = "pool" if j % 2 == 0 else "act"
            if eng == "act":
                s.op("act", lambda e, o=self.sq[:, q, :], a=src[j]: e.activation(out=o, in_=a, func=AF.Square),
                     reads=[b_src[j]], writes=[self.b_sq[q]])
            else:
                s.op("pool", lambda e, o=self.sq[:, q, :], a=src[j]: e.tensor_tensor(o, a, a, ALU.mult),
                     reads=[b_src[j]], writes=[self.b_sq[q]])
            s.op("pe", lambda e, o=ps[:, 0:NT], r=self.sq[:, q, :], st=(j == 0), sp=(j == nchunks - 1):
                 e.matmul(o, self.ones_bf[:, :], r, start=st, stop=sp),
                 reads=[self.b_sq[q], self.b_const], writes=[bps])
        s.op("act", lambda e, o=self.rstd[:, :], a=ps[:, 0:NT]: e.activation(out=o, in_=a, func=AF.Sqrt,
                                                                        bias=self.cst[:, 0:1], scale=1.0 / denom),
             reads=[bps, self.b_const], writes=[self.b_rstd])
        s.op("dve", lambda e, o=self.rstd[:, :]: e.reciprocal(o, o), reads=[self.b_rstd], writes=[self.b_rstd])

    def pre_norm(self, li, which):
        s = self.s
        self.sumsq_rstd([self.xt[:, j, :] for j in range(KC)], self.b_x, KC, float(D))
        for j in range(KC):
            q = j % 2
            s.op("dve", lambda e, o=self.tmpn[:, q, :], a=self.xt[:, j, :]: e.tensor_tensor(o, a, self.rstd[:, :], ALU.mult),
                 reads=[self.b_x[j], self.b_rstd], writes=[self.b_tmpn[q]])
            s.op("act", lambda e, o=self.ht[:, j, :], a=self.tmpn[:, q, :], sc=self.vec[:, li, which * 3 + 0, j:j + 1],
                 bi=self.vec[:, li, which * 3 + 1, j:j + 1]: e.activation(out=o, in_=a, func=AF.Identity, bias=bi, scale=sc),
                 reads=[self.b_tmpn[q], self.b_vec], writes=[self.b_h[j]])

    def post_norm_residual(self, li, which):
        s = self.s
        self.sumsq_rstd([self.yt[:, j, :] for j in range(KC)], self.b_y, KC, float(D))
        for j in range(KC):
            q = j % 2
            eng = "dve" if j % 2 == 0 else "pool"
            s.op(eng, lambda e, o=self.tmpn[:, q, :], a=self.yt[:, j, :]: e.tensor_tensor(o, a, self.rstd[:, :], ALU.mult),
                 reads=[self.b_y[j], self.b_rstd], writes=[self.b_tmpn[q]])
            s.op("dve", lambda e, o=self.xt[:, j, :], a=self.tmpn[:, q, :], g=self.vec[:, li, which * 3 + 2, j:j + 1]:
                 e.scalar_tensor_tensor(o, a, g, o, ALU.mult, ALU.add),
                 reads=[self.b_tmpn[q], self.b_vec], writes=[self.b_x[j]])

    def ffn(self, li):
        s = self.s
        win = "ffn_w_in%d" % li
        wout = "ffn_w_out%d" % li
        self.pre_norm(li, 1)
        for mp in range(0, FC, 4):
            nm = min(4, FC - mp)
            wg, bwg = self.wpanel(win, KC, mp * 128, nm * 128)
            wu, bwu = self.wpanel(win, KC, FFN_H + mp * 128, nm * 128)
            for mi in range(nm):
                m = mp + mi
                pg, bpg = self.psum()
                pu, bpu = self.psum()
                for k in range(KC):
                    s.op("pe", lambda e, o=pg[:, 0:NT], w=wg[:, k, mi * 128:(mi + 1) * 128], r=self.ht[:, k, :], st=(k == 0), sp=(k == KC - 1):
                         e.matmul(o, w, r, start=st, stop=sp), reads=[bwg, self.b_h[k]], writes=[bpg])
                for k in range(KC):
                    s.op("pe", lambda e, o=pu[:, 0:NT], w=wu[:, k, mi * 128:(mi + 1) * 128], r=self.ht[:, k, :], st=(k == 0), sp=(k == KC - 1):
                         e.matmul(o, w, r, start=st, stop=sp), reads=[bwu, self.b_h[k]], writes=[bpu])
                q = m % 2
                s.op("act", lambda e, o=self.sg[:, q, :], a=pg[:, 0:NT]: e.activation(out=o, in_=a, func=AF.Silu),
                     reads=[bpg], writes=[self.b_sg[q]])
                s.op("dve", lambda e, o=self.act[:, m, :], a=pu[:, 0:NT], b=self.sg[:, q, :]: e.tensor_tensor(o, a, b, ALU.mult),
                     reads=[bpu, self.b_sg[q]], writes=[self.b_act[m]])
        for m in range(KC):
            w, bw = self.wpanel(wout, FC, m * 128, 128)
            po, bpo = self.psum()
            for k in range(FC):
                s.op("pe", lambda e, o=po[:, 0:NT], w_=w[:, k, :], r=self.act[:, k, :], st=(k == 0), sp=(k == FC - 1):
                     e.matmul(o, w_, r, start=st, stop=sp), reads=[bw, self.b_act[k]], writes=[bpo])
            s.op("act", lambda e, o=self.yt[:, m, :], a=po[:, 0:NT]: e.activation(out=o, in_=a, func=AF.Copy),
                 reads=[bpo], writes=[self.b_y[m]])
        self.post_norm_residual(li, 1)

    def _tensors_mix(self):
        nc, s, L = self.nc, self.s, self.L
        I = "ExternalInput"
        self.d_cmask = self.dram("cmask", [128, 11, 128], F32, I)
        self.d_sel = self.dram("sel", [16, 16 * 128], F32, I)
        self.d_convw = self.dram("convw", [128, 12, 4], F32, I)
        self.d_fvec = self.dram("fvec", [128, 5, 12], F32, I)
        self.d_ssd16 = self.dram("ssd16", [16, 2], F32, I)
        self.d_s5v1 = self.dram("s5v1", [128, 3, 32], F32, I)
        self.d_s5v2 = self.dram("s5v2", [128, 5, 1024], F32, I)
        self.d_s5wc = self.dram("s5wc", [128, 32 * 2 * 128], F32, I)
        self.Kd = self.dram("Kd", [8, 128, 2048 + L], BF16)
        self.Vd = self.dram("Vd", [8, 128, 2048 + L], BF16)
        self.b_Kd = s.buf("Kd"); self.b_Vd = s.buf("Vd")
        self.cmask = self.sb("cmask_sb", [128, 11, 128], F32)
        self.ident_bf = self.sb("ident_bf", [128, 128], BF16)
        self.sel = self.sb("sel_sb", [16, 16, 128], F32)
        self.ones16 = self.sb("ones16", [16, 128], F32)
        self.convw = self.sb("convw_sb", [128, 12, 4], F32)
        self.fvec = self.sb("fvec_sb", [128, 5, 12], F32)
        self.ssd16 = self.sb("ssd16_sb", [16, 4], F32)
        self.A32 = self.sb("A32", [128, 7424], F32); self.b_A32 = s.buf("A32")
        self.A16 = self.sb("A16", [128, 19456], BF16); self.b_A16 = s.buf("A16")
        self.Hst = self.sb("Hst", [128, 1024], F32); self.b_Hst = [s.buf("Hst%d" % h) for h in range(16)]
        self.Hbf = self.sb("Hbf", [128, 1024], BF16); self.b_Hbf = [s.buf("Hbf%d" % h) for h in range(16)]
        self.ctail = self.sb("ctail", [128, 12, 3], F32); self.b_ctail = s.buf("ctail")
        self.dtt = self.sb("dtt", [16, 5, NT], F32); self.b_dtt = [s.buf("dtt%d" % i) for i in range(5)]
        self.tok = self.sb("tok", [128, 2, 48], F32); self.b_tok = [s.buf("tok0"), s.buf("tok1")]
        self.xdt = self.sb("xdt", [128, 2, 1024], BF16); self.b_xdt = [s.buf("xdt"), s.buf("xdtw")]
        self.Btok = self.sb("Btok", [128, 256], BF16); self.b_Btok = s.buf("Btok")
        self.cbm = self.sb("cbm", [128, 256], F32); self.b_cbm = s.buf("cbm")
        self.D1 = self.sb("D1", [128, 2, 128], F32); self.b_D1 = [s.buf("D1a"), s.buf("D1b")]
        self.Eh = self.sb("Eh", [128, 2, 128], F32); self.b_Eh = [s.buf("Eha"), s.buf("Ehb")]
        self.Mh = self.sb("Mh", [128, 2, 128], BF16); self.b_Mh = [s.buf("Mha"), s.buf("Mhb")]
        self.Csh = self.sb("Csh", [128, 2, 128], BF16); self.b_Csh = [s.buf("Csa"), s.buf("Csb")]
        self.Wc = self.sb("Wc", [128, 32, 2, 128], BF16)
        self.Wb = self.sb("Wb", [128, 8, 2, 128], BF16)
        self.s5c = self.sb("s5c", [128, 4, 32], F32)
        self.TS = 16
        TS = self.TS
        self.Hall = self.sb("Hall", [128, 2, 32, TS + 1], F32); self.b_Hall = [s.buf("Hall0"), s.buf("Hall1")]
        self.Hs16 = self.sb("Hs16", [128, 2, 32, TS], BF16); self.b_Hs16 = [s.buf("Hs16a"), s.buf("Hs16b")]
        self.Tinv = self.sb("Tinv", [128, 2, 32, TS], F32)
        self.Tfwd = self.sb("Tfwd", [128, 2, 32, TS], F32)
        self.smask = self.sb("smask", [128, 2, 32, TS + 1], BF16)
        self.b_m12 = s.buf("m12"); self.b_m3 = s.buf("m3")
        self.b_s5w = s.buf("s5w")
        a32 = self.A32
        self.convbuf = a32[:, 0:12 * (NT + 3)].rearrange("p (j t) -> p j t", j=12)
        o = 12 * (NT + 3)
        self.cacc = a32[:, o:o + 2 * NT].rearrange("p (j t) -> p j t", j=2); o += 2 * NT
        self.ys = a32[:, o:o + 8 * NT].rearrange("p (j t) -> p j t", j=8); o += 8 * NT
        self.yg = a32[:, 0:8 * NT].rearrange("p (j t) -> p j t", j=8)
        assert o <= 7424, o
        self.mtmp = a32[:, 2048:2048 + 3 * 32 * self.TS].rearrange("p (a q t) -> p a q t", a=3, q=32)
        a16 = self.A16
        o = 0
        def v16(n, j):
            nonlocal o
            r = a16[:, o:o + n].rearrange("p (j t) -> p j t", j=j)
            o += n
            return r
        self.zs = v16(8 * NT, 8)
        self.xsb = v16(8 * NT, 8)
        self.Bfm = v16(2 * NT, 2)
        self.Cfm = v16(2 * NT, 2)
        self.ubf = v16(8 * NT, 8)
        self.cat = v16(16 * NT, 16)
        self.g5b = v16(8 * NT, 8)
        assert o <= 19456, o
        self.b_zs = [s.buf() for _ in range(8)]
        self.b_xsb = [s.buf() for _ in range(8)]
        self.b_BC = [s.buf() for _ in range(4)]
        self.b_ubf = [s.buf() for _ in range(8)]
        self.b_cat = [s.buf() for _ in range(16)]
        self.b_g5b = [s.buf() for _ in range(8)]
        self.b_conv = [s.buf() for _ in range(12)]
        self.b_cacc = [s.buf(), s.buf()]
        self.b_ys = s.buf()
        self.b_yg = [s.buf() for _ in range(8)]
        o = 0
        self.Qb = v16(24 * NT, 24)
        self.kvst = v16(2 * NT, 2)
        self.Kw = a16[:, o:o + 2048 + NT]; o += 2048 + NT
        self.Vw = a16[:, o:o + 2048 + NT]; o += 2048 + NT
        self.attn = v16(8 * NT, 8)
        self.Vtok = v16(43 * 128, 43)
        self.PT = v16(4 * 128, 4)
        assert o <= 19456, o
        self.b_Qb = [s.buf() for _ in range(24)]
        self.b_kvst = [s.buf(), s.buf()]
        self.b_Kw = s.buf(); self.b_Vw = s.buf()
        self.b_attn = [s.buf() for _ in range(8)]
        self.b_Vtok = [s.buf() for _ in range(6)]
        self.b_PT = [s.buf() for _ in range(4)]
        self.Oacc = a32[:, 0:2 * NT].rearrange("p (j t) -> p j t", j=2)
        self.b_Oacc = [s.buf(), s.buf()]
        self.pt_rr = 0

    def setup_mix(self):
        s = self.s
        bc = self.b_const
        s.dma(lambda e: e.dma_start(out=self.cmask[:, :, :], in_=self.d_cmask), writes=[bc])
        s.dma(lambda e: e.dma_start(out=self.sel[:, :, :], in_=self.d_sel.rearrange("k (h m) -> k h m", h=16)), writes=[bc])
        s.dma(lambda e: e.dma_start(out=self.convw[:, :, :], in_=self.d_convw), writes=[bc])
        s.dma(lambda e: e.dma_start(out=self.fvec[:, :, :], in_=self.d_fvec), writes=[bc])
        s.dma(lambda e: e.dma_start(out=self.ssd16[:, 0:2], in_=self.d_ssd16), writes=[bc])
        s.op("pool", lambda e: e.memset(self.ones16[:, :], 1.0), writes=[bc])
        s.op("dve", lambda e: e.tensor_copy(self.ident_bf[:, :], self.cmask[:, 0, :]), reads=[bc], writes=[bc])
        s.op("act", lambda e: e.activation(out=self.ssd16[:, 2:3], in_=self.ssd16[:, 1:2], func=AF.Exp), reads=[bc], writes=[bc])
        s.op("dve", lambda e: e.tensor_scalar(self.ssd16[:, 2:3], self.ssd16[:, 2:3], -1.0, None, ALU.mult), reads=[bc], writes=[bc])
        for h in range(16):
            s.op("pool", lambda e, h=h: e.memset(self.Hst[:, h * 64:(h + 1) * 64], 0.0), writes=[self.b_Hst[h]])
            s.op("pool", lambda e, h=h: e.memset(self.Hbf[:, h * 64:(h + 1) * 64], 0.0), writes=[self.b_Hbf[h]])
        s.op("pool", lambda e: e.memset(self.ctail[:, :, :], 0.0), writes=[self.b_ctail])
        for ri in range(2):
            s.op("pool", lambda e, ri=ri: e.memset(self.Hall[:, ri, :, :], 0.0), writes=[self.b_Hall[ri]])
        s.op("pool", lambda e: e.memset(self.smask[:, :, :, :], 1.0), writes=[self.b_s5w])
        s.op("pool", lambda e: e.memset(self.smask[:, :, :, 0:1], 0.0), writes=[self.b_s5w])
        self.setup_s5()
        if 1 in self.layers:
            s.op("pool", lambda e: e.memset(self.wp[:, 0, 0:2048], 0.0), writes=[self.b_wp[0]])
            for h in range(8):
                s.dma(lambda e, h=h: e.dma_start(out=self.Kd[h, :, 0:2048], in_=self.wp[:, 0, 0:2048]), reads=[self.b_wp[0]], writes=[self.b_Kd])
                s.dma(lambda e, h=h: e.dma_start(out=self.Vd[h, :, 0:2048], in_=self.wp[:, 0, 0:2048]), reads=[self.b_wp[0]], writes=[self.b_Vd])

    def setup_s5(self):
        s = self.s
        ba = self.b_A32
        PI = float(np.pi)
        sc = self.A32
        I32 = mybir.dt.int32

        def dve(fn, reads=(), writes=()):
            s.op("dve", fn, reads=[ba] + list(reads), writes=[ba] + list(writes))

        def act(fn):
            s.op("act", fn, reads=[ba], writes=[ba])

        def coeffs(lr, li, ld, N, base, ts=1.0):
            t = [sc[:, base + i * N: base + (i + 1) * N] for i in range(8)]
            dt, mag, ang, r, kf, m, ar, ai = t
            ki = kf.bitcast(I32)
            act(lambda e: e.activation(out=dt, in_=ld, func=AF.Exp))
            if ts != 1.0:
                dve(lambda e: e.tensor_scalar(dt, dt, float(ts), None, ALU.mult))
            dve(lambda e: e.tensor_tensor(mag, lr, dt, ALU.mult))
            act(lambda e: e.activation(out=mag, in_=mag, func=AF.Exp))
            dve(lambda e: e.tensor_tensor(ang, li, dt, ALU.mult))

            def reduce_sin(shift, out):
                dve(lambda e: e.tensor_scalar(r, ang, shift, None, ALU.add))
                dve(lambda e: e.tensor_scalar(m, r, 1.0 / (2 * PI), None, ALU.mult))
                dve(lambda e: e.tensor_copy(ki, m))
                dve(lambda e: e.tensor_copy(m, ki))
                dve(lambda e: e.scalar_tensor_tensor(r, m, -2 * PI, r, ALU.mult, ALU.add))
                dve(lambda e: e.tensor_scalar(m, r, PI, None, ALU.is_gt))
                dve(lambda e: e.scalar_tensor_tensor(r, m, -2 * PI, r, ALU.mult, ALU.add))
                dve(lambda e: e.tensor_scalar(m, r, -PI, None, ALU.is_lt))
                dve(lambda e: e.scalar_tensor_tensor(r, m, 2 * PI, r, ALU.mult, ALU.add))
                act(lambda e: e.activation(out=out, in_=r, func=AF.Sin))

            reduce_sin(0.0, ai)
            reduce_sin(PI / 2, ar)
            dve(lambda e: e.tensor_tensor(ar, ar, mag, ALU.mult))
            dve(lambda e: e.tensor_tensor(ai, ai, mag, ALU.mult))
            return ar, ai

        v1 = sc[:, 0:96].rearrange("p (a q) -> p a q", a=3)
        s.dma(lambda e: e.dma_start(out=v1, in_=self.d_s5v1), writes=[ba])
        ar, ai = coeffs(v1[:, 0, :], v1[:, 1, :], v1[:, 2, :], 32, 128)
        dve(lambda e, ar=ar: e.tensor_copy(self.s5c[:, 0, :], ar), writes=[self.b_s5w])
        dve(lambda e, ar=ar: e.tensor_copy(self.s5c[:, 1, :], ar), writes=[self.b_s5w])
        dve(lambda e, ai=ai: e.tensor_copy(self.s5c[:, 2, :], ai), writes=[self.b_s5w])
        dve(lambda e, ai=ai: e.tensor_scalar(self.s5c[:, 3, :], ai, -1.0, None, ALU.mult), writes=[self.b_s5w])
        for t in range(1, self.TS + 1):
            for tab, sgn in ((self.Tfwd, 1.0), (self.Tinv, -1.0)):
                ar_t, ai_t = coeffs(v1[:, 0, :], v1[:, 1, :], v1[:, 2, :], 32, 128, ts=sgn * t)
                dve(lambda e, ar_t=ar_t, tab=tab, t=t: e.tensor_copy(tab[:, 0, :, t - 1], ar_t), writes=[self.b_s5w])
                dve(lambda e, ai_t=ai_t, tab=tab, t=t: e.tensor_copy(tab[:, 1, :, t - 1], ai_t), writes=[self.b_s5w])
        N = 128
        for fc in range(8):
            inp = [sc[:, i * N:(i + 1) * N] for i in range(5)]
            for i in range(5):
                s.dma(lambda e, i=i, fc=fc: e.dma_start(out=inp[i], in_=self.d_s5v2[:, i, fc * 128:(fc + 1) * 128]), writes=[ba])
            lr, li, ld, Bre, Bim = inp
            ar, ai = coeffs(lr, li, ld, N, 5 * N)
            den, am1, qre, qim, t1, t2 = [sc[:, (13 + i) * N:(14 + i) * N] for i in range(6)]
            dve(lambda e: e.tensor_tensor(den, lr, lr, ALU.mult))
            dve(lambda e: e.tensor_tensor(t1, li, li, ALU.mult))
            dve(lambda e: e.tensor_tensor(den, den, t1, ALU.add))
            dve(lambda e: e.reciprocal(den, den))
            dve(lambda e: e.tensor_scalar(am1, ar, -1.0, None, ALU.add))
            dve(lambda e: e.tensor_tensor(t1, am1, lr, ALU.mult))
            dve(lambda e: e.tensor_tensor(t2, ai, li, ALU.mult))
            dve(lambda e: e.tensor_tensor(qre, t1, t2, ALU.add))
            dve(lambda e: e.tensor_tensor(qre, qre, den, ALU.mult))
            dve(lambda e: e.tensor_tensor(t1, ai, lr, ALU.mult))
            dve(lambda e: e.tensor_tensor(t2, am1, li, ALU.mult))
            dve(lambda e: e.tensor_tensor(qim, t1, t2, ALU.subtract))
            dve(lambda e: e.tensor_tensor(qim, qim, den, ALU.mult))
            dve(lambda e: e.tensor_tensor(t1, qre, Bre, ALU.mult))
            dve(lambda e: e.tensor_tensor(t2, qim, Bim, ALU.mult))
            dve(lambda e, fc=fc: e.tensor_tensor(self.Wb[:, fc, 0, :], t1, t2, ALU.subtract), writes=[self.b_s5w])
            dve(lambda e: e.tensor_tensor(t1, qre, Bim, ALU.mult))
            dve(lambda e: e.tensor_tensor(t2, qim, Bre, ALU.mult))
            dve(lambda e, fc=fc: e.tensor_tensor(self.Wb[:, fc, 1, :], t1, t2, ALU.add), writes=[self.b_s5w])
        for q in range(32):
            st = sc[:, 0:256].rearrange("p (r m) -> p r m", r=2)
            s.dma(lambda e, q=q: e.dma_start(out=sc[:, 0:256], in_=self.d_s5wc[:, q * 256:(q + 1) * 256]), writes=[ba])
            dve(lambda e, q=q: e.tensor_copy(self.Wc[:, q, 0, :], st[:, 0, :]), writes=[self.b_s5w])
            dve(lambda e, q=q: e.tensor_scalar(self.Wc[:, q, 1, :], st[:, 1, :], -1.0, None, ALU.mult), writes=[self.b_s5w])

    def evac(self, eng, out, in_, reads, writes):
        s = self.s
        if eng == "act":
            s.op("act", lambda e: e.activation(out=out, in_=in_, func=AF.Copy), reads=reads, writes=writes)
        else:
            s.op(eng, lambda e: e.tensor_copy(out, in_), reads=reads, writes=writes)

    def proj(self, wname, c0, ncols, nk, rhs_fn, rhs_bufs, on_chunk, k0=0, panel_cols=512):
        s = self.s
        pc = min(panel_cols, (4096 // nk) // 128 * 128) if ncols >= 128 else ncols
        mi = 0
        for p0 in range(0, ncols, pc):
            pw = min(pc, ncols - p0)
            w, bw = self.wpanel(wname, nk, c0 + p0, pw, k0=k0)
            for m0 in range(0, pw, 128):
                mw = min(128, pw - m0)
                ps, bps = self.psum()
                for k in range(nk):
                    s.op("pe", lambda e, o=ps[0:mw, 0:NT], w_=w[:, k, m0:m0 + mw], r=rhs_fn(k), st=(k == 0), sp=(k == nk - 1):
                         e.matmul(o, w_, r, start=st, stop=sp), reads=[bw, rhs_bufs[k]], writes=[bps])
                on_chunk(mi, mw, ps, bps)
                mi += 1

    def mixer0(self, ti):
        s = self.s
        fv = self.fvec
        self.pre_norm(0, 0)
        hrhs = lambda k: self.ht[:, k, :]
        def on_z(mi, mw, ps, bps):
            s.op("act", lambda e: e.activation(out=self.zs[:, mi, :], in_=ps[:, 0:NT], func=AF.Silu),
                 reads=[bps], writes=[self.b_zs[mi], self.b_A16])
        self.proj("hyb_w_in", 0, 1024, KC, hrhs, self.b_h, on_z)

        def on_xbc(mi, mw, ps, bps):
            self.evac("dve", self.convbuf[:, mi, 3:3 + NT], ps[:, 0:NT], [bps], [self.b_conv[mi], self.b_A32])
        self.proj("hyb_w_in", 1024, 1536, KC, hrhs, self.b_h, on_xbc)

        def on_dt(mi, mw, ps, bps):
            s.op("act", lambda e: e.activation(out=self.dtt[:, 0, :], in_=ps[0:16, 0:NT], func=AF.Exp, bias=self.ssd16[:, 0:1], scale=1.0),
                 reads=[bps, self.b_const], writes=[self.b_dtt[0]])
        self.proj("hyb_w_in", 2560, 16, KC, hrhs, self.b_h, on_dt)

        def on_u(mi, mw, ps, bps):
            self.evac("act", self.ubf[:, mi, :], ps[:, 0:NT], [bps], [self.b_ubf[mi]])
        self.proj("hyb_w_in", 2576, 1024, KC, hrhs, self.b_h, on_u)

        for j in range(12):
            cb = self.convbuf
            q = j % 2
            acc = self.cacc[:, q, :]
            s.op("dve", lambda e, j=j: e.tensor_copy(cb[:, j, 0:3], self.ctail[:, j, :]), reads=[self.b_ctail], writes=[self.b_conv[j]])
            s.op("dve", lambda e, j=j, acc=acc: e.tensor_scalar(acc, cb[:, j, 0:NT], self.convw[:, j, 0:1], None, ALU.mult),
                 reads=[self.b_conv[j], self.b_const], writes=[self.b_cacc[q]])
            for k in range(1, 4):
                s.op("dve", lambda e, j=j, k=k, acc=acc: e.scalar_tensor_tensor(acc, cb[:, j, k:k + NT], self.convw[:, j, k:k + 1], acc, ALU.mult, ALU.add),
                     reads=[self.b_conv[j], self.b_const], writes=[self.b_cacc[q]])
            s.op("dve", lambda e, j=j: e.tensor_copy(self.ctail[:, j, :], cb[:, j, NT:NT + 3]), reads=[self.b_conv[j]], writes=[self.b_ctail])
            if j < 8:
                dst, bd = self.xsb[:, j, :], self.b_xsb[j]
            elif j < 10:
                dst, bd = self.Bfm[:, j - 8, :], self.b_BC[j - 8]
            else:
                dst, bd = self.Cfm[:, j - 10, :], self.b_BC[j - 8]
            s.op("act", lambda e, j=j, acc=acc, dst=dst: e.activation(out=dst, in_=acc, func=AF.Silu, bias=fv[:, 0, j:j + 1], scale=1.0),
                 reads=[self.b_cacc[q], self.b_const], writes=[bd])

        dt_e, dtv, a_, ac, dtw = [self.dtt[:, i, :] for i in range(5)]
        bd = self.b_dtt
        s.op("act", lambda e: e.activation(out=dtv, in_=dt_e, func=AF.Ln, bias=self.cst[0:16, 1:2], scale=1.0),
             reads=[bd[0], self.b_const], writes=[bd[1]])
        s.op("dve", lambda e: e.tensor_scalar(a_, dtv, self.ssd16[:, 2:3], None, ALU.mult), reads=[bd[1], self.b_const], writes=[bd[2]])
        NCH = NT // 128
        for c in range(NCH):
            sl = slice(c * 128, (c + 1) * 128)
            s.op("dve", lambda e, sl=sl: e.tensor_tensor_scan(ac[:, sl], self.ones16[:, :], a_[:, sl], 0.0, ALU.mult, ALU.add),
                 reads=[bd[2], self.b_const], writes=[bd[3]])
        for c in range(NCH):
            sl = slice(c * 128, (c + 1) * 128)
            s.op("act", lambda e, sl=sl, c=c: e.activation(out=dt_e[:, sl], in_=ac[:, sl], func=AF.Exp, bias=ac[:, c * 128 + 127:c * 128 + 128], scale=-1.0),
                 reads=[bd[3]], writes=[bd[0]])
        s.op("dve", lambda e: e.tensor_tensor(dtw, dtv, dt_e, ALU.mult), reads=[bd[0], bd[1]], writes=[bd[4]])

        idf = self.cmask[0:16, 0, 0:16]
        self._s5_pending = [i * self.TS for i in range(NT // self.TS)]
        for c in range(NCH):
            self.ssd_chunk(c, ac, dtv, dtw, bd, idf)
        while self._s5_pending:
            self.s5_sub(self._s5_pending.pop(0), self.TS)
        for g in range(2):
            self.sumsq_rstd([self.yg[:, 4 * g + jj, :] for jj in range(4)], self.b_yg[4 * g:4 * g + 4], 4, 512.0)
            for jj in range(4):
                j = 4 * g + jj
                q = j % 2
                s.op("dve", lambda e, j=j, q=q: e.tensor_tensor(self.tmpn[:, q, :], self.yg[:, j, :], self.rstd[:, :], ALU.mult),
                     reads=[self.b_yg[j], self.b_rstd], writes=[self.b_tmpn[q]])
                s.op("act", lambda e, j=j, q=q: e.activation(out=self.cat[:, j, :], in_=self.tmpn[:, q, :], func=AF.Copy, scale=fv[:, 2, j:j + 1]),
                     reads=[self.b_tmpn[q], self.b_const], writes=[self.b_cat[j]])
        self.s5_tile()
        if "cat" in self.dbg and ti == 0:
            self.d_dbg = self.dram("dbg_cat", [128, 16, NT], BF16, "ExternalOutput")
            s.dma(lambda e: e.dma_start(out=self.d_dbg, in_=self.cat), reads=self.b_cat)
            self.d_dbg2 = self.dram("dbg_yg", [128, 8, NT], F32, "ExternalOutput")
            s.dma(lambda e: e.dma_start(out=self.d_dbg2, in_=self.yg), reads=self.b_yg)
            self.d_dbg4 = self.dram("dbg_xsb", [128, 8, NT], BF16, "ExternalOutput")
            s.dma(lambda e: e.dma_start(out=self.d_dbg4, in_=self.xsb), reads=self.b_xsb)
            self.d_dbg5 = self.dram("dbg_bc", [128, 4, NT], BF16, "ExternalOutput")
            s.dma(lambda e: e.dma_start(out=self.d_dbg5[:, 0:2, :], in_=self.Bfm), reads=self.b_BC)
            s.dma(lambda e: e.dma_start(out=self.d_dbg5[:, 2:4, :], in_=self.Cfm), reads=self.b_BC)
            self.d_dbg6 = self.dram("dbg_dtt", [16, 5, NT], F32, "ExternalOutput")
            s.dma(lambda e: e.dma_start(out=self.d_dbg6, in_=self.dtt[:, :, :]), reads=self.b_dtt)
            self.d_dbg3 = self.dram("dbg_ys", [128, 8, NT], F32, "ExternalOutput")
            s.dma(lambda e: e.dma_start(out=self.d_dbg3, in_=self.ys), reads=[self.b_ys])
            s.barrier()
        def on_o(mi, mw, ps, bps):
            self.evac("act", self.yt[:, mi, :], ps[:, 0:NT], [bps], [self.b_y[mi]])
        self.proj("hyb_w_out", 0, 1024, 16, lambda k: self.cat[:, k, :], self.b_cat, on_o, panel_cols=256)
        self.post_norm_residual(0, 0)

    def ssd_chunk(self, c, ac, dtv, dtw, bd, idf):
        s = self.s
        fv = self.fvec
        sl = slice(c * 128, (c + 1) * 128)
        tq = c % 2
        tok = self.tok[:, tq, :]
        btok = self.b_tok[tq]
        pt, bpt = self.psum()
        for i, (src, bsrc) in enumerate(((ac, bd[3]), (dtv, bd[1]), (dtw, bd[4]))):
            s.op("pe", lambda e, i=i, src=src, sl=sl: e.transpose(pt[:, i * 16:(i + 1) * 16], src[:, sl], idf),
                 reads=[bsrc, self.b_const], writes=[bpt])
        self.evac("dve", tok, pt[:, 0:48], [bpt], [btok])
        px, bpx = self.psum_bf()
        pxb = px[:, :].bitcast(BF16)
        for j in range(8):
            s.op("pe", lambda e, j=j, sl=sl: e.transpose(pxb[:, j * 128:(j + 1) * 128], self.xsb[:, j, sl], self.ident_bf[:, :]),
                 reads=[self.b_xsb[j], self.b_const], writes=[bpx])
        for i in range(2):
            s.op("dve", lambda e, i=i: e.tensor_tensor(self.xdt[:, i, :].rearrange("p (h d) -> p h d", h=16),
                                                       pxb.rearrange("p (h d) -> p h d", h=16),
                                                       tok[:, 16 * (i + 1):16 * (i + 2)].unsqueeze(2).to_broadcast([128, 16, 64]), ALU.mult),
                 reads=[bpx, btok], writes=[self.b_xdt[i]])
        pb, bpb = self.psum_bf()
        pbb = pb[:, :].bitcast(BF16)
        for g in range(2):
            s.op("pe", lambda e, g=g, sl=sl: e.transpose(pbb[:, g * 128:(g + 1) * 128], self.Bfm[:, g, sl], self.ident_bf[:, :]),
                 reads=[self.b_BC[g], self.b_const], writes=[bpb])
        self.evac("act", self.Btok[:, :], pbb[:, 0:256], [bpb], [self.b_Btok])
        pc, bpc = self.psum()
        for g in range(2):
            s.op("pe", lambda e, g=g, sl=sl: e.matmul(pc[:, g * 128:(g + 1) * 128], self.Bfm[:, g, sl], self.Cfm[:, g, sl], start=True, stop=True),
                 reads=[self.b_BC[g], self.b_BC[2 + g]], writes=[bpc])
        s.op("dve", lambda e: e.tensor_tensor(self.cbm[:, :].rearrange("p (g l) -> p g l", g=2), pc[:, 0:256].rearrange("p (g l) -> p g l", g=2),
                                              self.cmask[:, 1, :].unsqueeze(1).to_broadcast([128, 2, 128]), ALU.mult),
             reads=[bpc, self.b_const], writes=[self.b_cbm])
        pst = []
        for g in range(2):
            p_, bp_ = self.psum_fixed(4 + g)
            s.op("pe", lambda e, g=g, p_=p_: e.matmul(p_[:, :], self.Btok[:, g * 128:(g + 1) * 128], self.xdt[:, 1, g * 512:(g + 1) * 512], start=True, stop=True),
                 reads=[self.b_Btok, self.b_xdt[1]], writes=[bp_])
            pst.append((p_, bp_))
        py = None
        for h in range(16):
            g = h // 8
            hq = h % 2
            if h % 8 == 0:
                py, bpy = self.psum_fixed(6)
            j4 = (h % 8) // 2
            pa, bpa = self.psum()
            s.op("pe", lambda e, h=h, sl=sl, pa=pa: e.matmul(pa[:, 0:128], self.sel[:, h, :], ac[:, sl], start=True, stop=True),
                 reads=[bd[3], self.b_const], writes=[bpa])
            s.op("dve", lambda e, h=h, hq=hq, pa=pa: e.tensor_scalar(self.D1[:, hq, :], pa[:, 0:128], tok[:, h:h + 1], 0.0, ALU.subtract, ALU.min),
                 reads=[bpa, btok], writes=[self.b_D1[hq]])
            s.op("act", lambda e, hq=hq: e.activation(out=self.D1[:, hq, :], in_=self.D1[:, hq, :], func=AF.Exp),
                 reads=[self.b_D1[hq]], writes=[self.b_D1[hq]])
            s.op("pool", lambda e, hq=hq, g=g: e.tensor_tensor(self.Mh[:, hq, :], self.D1[:, hq, :], self.cbm[:, g * 128:(g + 1) * 128], ALU.mult),
                 reads=[self.b_D1[hq], self.b_cbm], writes=[self.b_Mh[hq]])
            s.op("act", lambda e, hq=hq, pa=pa: e.activation(out=self.Eh[:, hq, :], in_=pa[:, 0:128], func=AF.Exp),
                 reads=[bpa], writes=[self.b_Eh[hq]])
            s.op("pool", lambda e, hq=hq, g=g, sl=sl: e.tensor_tensor(self.Csh[:, hq, :], self.Cfm[:, g, sl], self.Eh[:, hq, :], ALU.mult),
                 reads=[self.b_BC[2 + g], self.b_Eh[hq]], writes=[self.b_Csh[hq]])
            yo = py[hq * 64:(hq + 1) * 64, j4 * 128:(j4 + 1) * 128]
            s.op("pe", lambda e, h=h, hq=hq, yo=yo: e.matmul(yo, self.xdt[:, 0, h * 64:(h + 1) * 64], self.Mh[:, hq, :], start=True, stop=False),
                 reads=[self.b_xdt[0], self.b_Mh[hq]], writes=[bpy])
            s.op("pe", lambda e, h=h, hq=hq, yo=yo: e.matmul(yo, self.Hbf[:, h * 64:(h + 1) * 64], self.Csh[:, hq, :], start=False, stop=True),
                 reads=[self.b_Hbf[h], self.b_Csh[hq]], writes=[bpy])
            p_, bp_ = pst[g]
            hs = slice(h * 64, (h + 1) * 64)
            s.op("dve", lambda e, hs=hs, hq=hq, p_=p_, h=h: e.scalar_tensor_tensor(self.Hst[:, hs], self.Hst[:, hs], self.Eh[:, hq, 127:128],
                                                                                p_[:, (h % 8) * 64:(h % 8 + 1) * 64], ALU.mult, ALU.add),
                 reads=[self.b_Eh[hq], bp_], writes=[self.b_Hst[h]])
            s.op("pool", lambda e, hs=hs: e.tensor_copy(self.Hbf[:, hs], self.Hst[:, hs]), reads=[self.b_Hst[h]], writes=[self.b_Hbf[h]])
            if h % 2 == 1 and self._s5_pending:
                self.s5_sub(self._s5_pending.pop(0), self.TS)
            if h % 8 == 7:
                for jj in range(4):
                    j = 4 * g + jj
                    s.op("dve", lambda e, j=j, jj=jj, sl=sl, py=py: e.scalar_tensor_tensor(self.yg[:, j, sl], self.xsb[:, j, sl], fv[:, 1, j:j + 1],
                                                                                             py[:, jj * 128:(jj + 1) * 128], ALU.mult, ALU.add),
                         reads=[self.b_xsb[j], bpy, self.b_const], writes=[self.b_yg[j]] + self.b_conv)
                    s.op("pool", lambda e, j=j, sl=sl: e.tensor_tensor(self.yg[:, j, sl], self.yg[:, j, sl], self.zs[:, j, sl], ALU.mult),
                         reads=[self.b_zs[j]], writes=[self.b_yg[j]])

    def s5_sub(self, ts, T):
        s = self.s
        H = self.Hall
        bH = self.b_Hall
        m = self.mtmp
        alias = self.b_conv + self.b_cacc
        for ri in range(2):
            pb, bpb = self.psum()
            for q in range(32):
                fc, rb = q // 4, q % 4
                s.op("pe", lambda e, pb=pb, q=q, fc=fc, rb=rb, ri=ri: e.matmul(
                    pb[:, q * T:(q + 1) * T], self.Wb[32 * rb:32 * rb + 32, fc, ri, :], self.ubf[32 * rb:32 * rb + 32, fc, ts:ts + T],
                    start=True, stop=True, tile_position=(32 * rb, 0)), reads=[self.b_s5w, self.b_ubf[fc]], writes=[bpb])
            s.op("act", lambda e, pb=pb, ri=ri: e.activation(out=H[:, ri, :, 1:1 + T], in_=pb[:, 0:32 * T].rearrange("p (q t) -> p q t", q=32), func=AF.Copy),
                 reads=[bpb], writes=[bH[ri]])
        Br, Bi = H[:, 0, :, 1:1 + T], H[:, 1, :, 1:1 + T]
        s.op("dve", lambda e: e.tensor_tensor(m[:, 0:2, :, :], H[:, :, :, 1:1 + T], self.Tinv[:, :, :, :], ALU.mult),
             reads=[bH[0], bH[1], self.b_s5w], writes=[self.b_m12] + alias)
        s.op("pool", lambda e: e.tensor_tensor(m[:, 2, :, :], Br, self.Tinv[:, 1, :, :], ALU.mult),
             reads=[bH[0], self.b_s5w], writes=[self.b_m3] + alias)
        s.op("dve", lambda e: e.tensor_tensor(Bi, Bi, self.Tinv[:, 0, :, :], ALU.mult), reads=[self.b_s5w], writes=[bH[1]])
        s.op("dve", lambda e: e.tensor_tensor(Br, m[:, 0, :, :], m[:, 1, :, :], ALU.subtract), reads=[self.b_m12], writes=[bH[0]])
        s.op("dve", lambda e: e.tensor_tensor(Bi, Bi, m[:, 2, :, :], ALU.add), reads=[self.b_m3], writes=[bH[1]])
        flat = H[:, :, :, :].rearrange("p a q t -> p (a q t)")
        s.op("dve", lambda e: e.tensor_tensor_scan(flat, self.smask[:, :, :, :].rearrange("p a q t -> p (a q t)"), flat, 0.0, ALU.mult, ALU.add),
             reads=[self.b_s5w], writes=[bH[0], bH[1]])
        s.op("dve", lambda e: e.tensor_tensor(m[:, 0:2, :, :], H[:, :, :, 1:1 + T], self.Tfwd[:, :, :, :], ALU.mult),
             reads=[bH[0], bH[1], self.b_s5w], writes=[self.b_m12])
        s.op("pool", lambda e: e.tensor_tensor(m[:, 2, :, :], Br, self.Tfwd[:, 1, :, :], ALU.mult),
             reads=[bH[0], self.b_s5w], writes=[self.b_m3])
        s.op("dve", lambda e: e.tensor_tensor(Bi, Bi, self.Tfwd[:, 0, :, :], ALU.mult), reads=[self.b_s5w], writes=[bH[1]])
        s.op("dve", lambda e: e.tensor_tensor(self.Hs16[:, 0, :, :], m[:, 0, :, :], m[:, 1, :, :], ALU.subtract),
             reads=[self.b_m12], writes=[self.b_Hs16[0]])
        s.op("pool", lambda e: e.tensor_tensor(self.Hs16[:, 1, :, :], m[:, 2, :, :], Bi, ALU.add),
             reads=[self.b_m3, bH[1]], writes=[self.b_Hs16[1]])
        s.op("dve", lambda e: e.tensor_tensor(H[:, 0, :, 0], m[:, 0, :, T - 1], m[:, 1, :, T - 1], ALU.subtract),
             reads=[self.b_m12], writes=[bH[0]])
        s.op("pool", lambda e: e.tensor_tensor(H[:, 1, :, 0], m[:, 2, :, T - 1], H[:, 1, :, T], ALU.add),
             reads=[self.b_m3], writes=[bH[1]])
        po, bpo = self.psum()
        for fc in range(8):
            n = 0
            for rb in range(4):
                q = 4 * fc + rb
                for ri in range(2):
                    s.op("pe", lambda e, fc=fc, q=q, ri=ri, n=n: e.matmul(po[:, fc * T:(fc + 1) * T], self.Wc[:, q, ri, :], self.Hs16[:, ri, q, :],
                                                                         start=(n == 0), stop=(n == 7)),
                         reads=[self.b_s5w, self.b_Hs16[ri]], writes=[bpo])
                    n += 1
        s.op("act", lambda e, po=po: e.activation(out=self.ys[:, :, ts:ts + T], in_=po[:, 0:8 * T].rearrange("p (j t) -> p j t", j=8), func=AF.Copy),
             reads=[bpo], writes=[self.b_ys])

    def s5_tile(self):
        s = self.s
        fv = self.fvec
        for j in range(8):
            q = j % 2
            s.op("dve", lambda e, j=j, q=q: e.scalar_tensor_tensor(self.tmpn[:, q, :], self.ubf[:, j, :], fv[:, 3, j:j + 1], self.ys[:, j, :], ALU.mult, ALU.add),
                 reads=[self.b_ubf[j], self.b_ys, self.b_const], writes=[self.b_tmpn[q]])
            s.op("act", lambda e, j=j, q=q: e.activation(out=self.g5b[:, j, :], in_=self.tmpn[:, q, :], func=AF.Gelu),
                 reads=[self.b_tmpn[q]], writes=[self.b_g5b[j]])

        def on_g(mi, mw, ps, bps):
            q = mi % 2
            s.op("act", lambda e: e.activation(out=self.tmpn[:, q, :], in_=ps[:, 0:NT], func=AF.Sigmoid, bias=fv[:, 4, mi:mi + 1], scale=1.0),
                 reads=[bps, self.b_const], writes=[self.b_tmpn[q]])
            s.op("dve", lambda e: e.tensor_tensor(self.cat[:, 8 + mi, :], self.g5b[:, mi, :], self.tmpn[:, q, :], ALU.mult),
                 reads=[self.b_tmpn[q], self.b_g5b[mi]], writes=[self.b_cat[8 + mi]])
        self.proj("s5_glu_w", 0, 1024, KC, lambda k: self.g5b[:, k, :], self.b_g5b, on_g)

    def mixer1(self, ti):
        s = self.s
        t0 = ti * NT
        self.pre_norm(1, 0)
        hrhs = lambda k: self.ht[:, k, :]

        def on_q(mi, mw, ps, bps):
            self.evac("act" if mi % 2 == 0 else "dve", self.Qb[:, mi, :], ps[:, 0:NT], [bps], [self.b_Qb[mi]])
        self.proj("attn_w_qkv", 0, 3072, KC, hrhs, self.b_h, on_q)

        def mk_kv(dst, bdst):
            def on_kv(mi, mw, ps, bps):
                q = mi % 2
                self.evac("act" if mi % 2 == 0 else "dve", self.kvst[:, q, :], ps[:, 0:NT], [bps], [self.b_kvst[q]])
                s.dma(lambda e: e.dma_start(out=dst[mi, :, 2048 + t0:2048 + t0 + NT], in_=self.kvst[:, q, :]),
                      reads=[self.b_kvst[q]], writes=[bdst])
            return on_kv
        self.proj("attn_w_qkv", 3072, 1024, KC, hrhs, self.b_h, mk_kv(self.Kd, self.b_Kd))
        self.proj("attn_w_qkv", 4096, 1024, KC, hrhs, self.b_h, mk_kv(self.Vd, self.b_Vd))
        for h in range(8):
            self.attn_head(ti, h)

        def on_o(mi, mw, ps, bps):
            self.evac("act", self.yt[:, mi, :], ps[:, 0:NT], [bps], [self.b_y[mi]])
        self.proj("attn_w_o", 0, 1024, KC, lambda k: self.attn[:, k, :], self.b_attn, on_o)
        self.post_norm_residual(1, 0)

    def attn_pat(self, ti, h, d, qoff, vprev, vcur, ncur, vb, exp_mask, PT, Ob, bO, Db, bD, Oa, Da):
        s = self.s
        cm = self.cmask
        W = 2048 + NT
        nq = NT // d
        kprev0 = 2048 - 128 * d
        u = ti * nq // 16
        has_prev = ti > 0
        if nq * ti >= 128:
            pmask = cm[:, 2, 0:nq]
        elif has_prev:
            pmask = cm[:, 3 + (nq * ti) // 16, 0:nq]
        pS, bpS = self.psum()
        for r in range(d):
            if has_prev:
                s.op("pe", lambda e, r=r: e.matmul(pS[:, r * nq:(r + 1) * nq], self.Kw[:, kprev0 + r:2048:d], self.Qb[:, qoff + h, r:NT:d], start=True, stop=True),
                     reads=[self.b_Kw, self.b_Qb[qoff + h]], writes=[bpS])
            s.op("pe", lambda e, r=r: e.matmul(pS[0:ncur, 256 + r * nq:256 + (r + 1) * nq], self.Kw[:, 2048 + r:W:d], self.Qb[:, qoff + h, r:NT:d], start=True, stop=True),
                 reads=[self.b_Kw, self.b_Qb[qoff + h]], writes=[bpS])
        parts = []
        if has_prev:
            parts.append((128, 0, NT, pmask.unsqueeze(1).to_broadcast([128, d, nq]), d, nq))
        parts.append((ncur, 256, NT, cm[0:ncur, 3, 0:nq].unsqueeze(1).to_broadcast([ncur, d, nq]), d, nq))
        exp_mask(pS, bpS, parts)
        dview = Db[:, 0:NT]
        if has_prev:
            s.op("pe", lambda e: e.matmul(dview, self.ones_bf[:, :], PT[:, 0:NT], start=True, stop=False),
                 reads=[self.b_PT[0], self.b_const], writes=[bD])
        s.op("pe", lambda e: e.matmul(dview, self.ones_bf[0:ncur, :], PT[0:ncur, 256:256 + NT], start=(not has_prev), stop=True),
             reads=[self.b_PT[0], self.b_const], writes=[bD])
        for r in range(d):
            if has_prev:
                s.op("pe", lambda e, r=r: e.matmul(Ob[:, r:NT:d], self.Vtok[:, vprev + r, :], PT[:, r * nq:(r + 1) * nq], start=True, stop=False),
                     reads=[self.b_PT[0], vb[vprev + r]], writes=[bO])
            s.op("pe", lambda e, r=r: e.matmul(Ob[:, r:NT:d], self.Vtok[0:ncur, vcur + r, :], PT[0:ncur, 256 + r * nq:256 + (r + 1) * nq], start=(not has_prev), stop=True),
                 reads=[self.b_PT[0], vb[vcur + r]], writes=[bO])
        s.op("dve", lambda e: e.tensor_tensor(Oa, Oa, Ob[:, 0:NT], ALU.add), reads=[bO], writes=[self.b_Oacc[0]])
        s.op("dve", lambda e: e.tensor_tensor(Da.rearrange("p (q r) -> p r q", r=d), Da.rearrange("p (q r) -> p r q", r=d),
                                              Db[:, 0:NT].rearrange("p (r q) -> p r q", r=d), ALU.add), reads=[bD], writes=[self.b_Oacc[1]])

    def attn_head(self, ti, h):
        s = self.s
        t0 = ti * NT
        W = 2048 + NT
        cm = self.cmask
        scale = float(128 ** -0.5)
        s.dma(lambda e: e.dma_start(out=self.Kw[:, 0:W], in_=self.Kd[h, :, t0:t0 + W]), reads=[self.b_Kd], writes=[self.b_Kw])
        s.dma(lambda e: e.dma_start(out=self.Vw[:, 0:W], in_=self.Vd[h, :, t0:t0 + W]), reads=[self.b_Vd], writes=[self.b_Vw])
        blocks = []
        for i in range(3):
            blocks.append((i, slice(1920 + 128 * i, 2048 + 128 * i), 128))
        for r in range(4):
            blocks.append((3 + r, slice(1536 + r, 2048, 4), 128))
        for r in range(4):
            blocks.append((7 + r, slice(2048 + r, W, 4), 64))
        for r in range(16):
            blocks.append((11 + r, slice(r, 2048, 16), 128))
        for r in range(16):
            blocks.append((27 + r, slice(2048 + r, W, 16), 16))
        groups = [(0, 7, 128), (7, 11, 64), (11, 19, 128), (19, 27, 128), (27, 35, 16), (35, 43, 16)]
        vb = {}
        for gi, (a, b, nk) in enumerate(groups):
            pv, bpv = self.psum_bf()
            pvb = pv[:, :].bitcast(BF16)
            for (vidx, sl, nk_) in blocks[a:b]:
                c0 = (vidx - a) * 128
                s.op("pe", lambda e, pvb=pvb, sl=sl, c0=c0, nk_=nk_: e.transpose(pvb[0:nk_, c0:c0 + 128], self.Vw[:, sl], self.ident_bf[:, :]),
                     reads=[self.b_Vw, self.b_const], writes=[bpv])
                vb[vidx] = self.b_Vtok[gi]
            eng = "dve" if gi % 2 == 0 else "act"
            self.evac(eng, self.Vtok[0:nk, a:b, :], pvb[0:nk, 0:(b - a) * 128].rearrange("p (j e) -> p j e", j=b - a), [bpv], [self.b_Vtok[gi]])
        Ob, bO = self.psum_fixed(4)
        Db, bD = self.psum_fixed(5)
        Oa, Da = self.Oacc[:, 0, :], self.Oacc[:, 1, :]
        PT = self.PT[:, :, :].rearrange("p a n -> p (a n)")

        def exp_mask(pS, bpS, parts):
            for (nk, c0, nc_, mask, nrep, nq) in parts:
                s.op("act", lambda e, nk=nk, c0=c0, nc_=nc_: e.activation(out=PT[0:nk, c0:c0 + nc_], in_=pS[0:nk, c0:c0 + nc_], func=AF.Exp, scale=scale),
                     reads=[bpS], writes=[self.b_PT[0]])
                s.op("dve", lambda e, nk=nk, c0=c0, nc_=nc_, mask=mask, nrep=nrep, nq=nq: e.tensor_tensor(
                    PT[0:nk, c0:c0 + nc_].rearrange("p (a q) -> p a q", a=nrep), PT[0:nk, c0:c0 + nc_].rearrange("p (a q) -> p a q", a=nrep),
                    mask, ALU.mult), reads=[self.b_const], writes=[self.b_PT[0]])

        pS, bpS = self.psum()
        NQB = NT // 128
        for qb in range(NQB):
            for blk in range(2):
                c0 = (qb * 2 + blk) * 128
                s.op("pe", lambda e, qb=qb, blk=blk, c0=c0: e.matmul(pS[:, c0:c0 + 128], self.Kw[:, 1920 + 128 * (qb + blk):2048 + 128 * (qb + blk)],
                                                                    self.Qb[:, h, qb * 128:(qb + 1) * 128], start=True, stop=True),
                     reads=[self.b_Kw, self.b_Qb[h]], writes=[bpS])
        exp_mask(pS, bpS, [(128, qb * 256, 256, cm[:, 2:4, :], 2, 128) for qb in range(NQB)])
        if ti == 0:
            s.op("dve", lambda e: e.memset(PT[:, 0:128], 0.0), writes=[self.b_PT[0]])
        PT4 = PT[:, 0:NQB * 256].rearrange("p (a b q) -> p a b q", a=NQB, b=2)
        for blk in range(2):
            s.op("pe", lambda e, blk=blk: e.matmul(Db[:, 0:NT].rearrange("p (a q) -> p a q", a=NQB), self.ones_bf[:, :], PT4[:, :, blk, :],
                                                  start=(blk == 0), stop=(blk == 1)), reads=[self.b_PT[0], self.b_const], writes=[bD])
        for qb in range(NQB):
            for blk in range(2):
                s.op("pe", lambda e, qb=qb, blk=blk: e.matmul(Ob[:, qb * 128:(qb + 1) * 128], self.Vtok[:, qb + blk, :], PT4[:, qb, blk, :],
                                                             start=(blk == 0), stop=(blk == 1)), reads=[self.b_PT[0], vb[qb + blk]], writes=[bO])
        self.evac("act", Oa, Ob[:, 0:NT], [bO], [self.b_Oacc[0]])
        self.evac("act", Da, Db[:, 0:NT], [bD], [self.b_Oacc[1]])

        for (d, qoff, vprev, vcur, ncur) in ((4, 8, 3, 7, 64), (16, 16, 11, 27, 16)):
            self.attn_pat(ti, h, d, qoff, vprev, vcur, ncur, vb, exp_mask, PT, Ob, bO, Db, bD, Oa, Da)
        s.op("dve", lambda e: e.reciprocal(Da, Da), reads=[self.b_Oacc[1]], writes=[self.b_Oacc[1]])
        s.op("dve", lambda e: e.tensor_tensor(self.attn[:, h, :], Oa, Da, ALU.mult), reads=self.b_Oacc, writes=[self.b_attn[h]])

    def build(self):
        s = self.s
        L = self.L
        self.setup_consts()
        self.ada_mod()
        names = []
        for li in self.layers:
            names += ["ffn_w_in%d" % li, "ffn_w_out%d" % li]
        if self.mixers and 0 in self.layers:
            names += ["hyb_w_in", "hyb_w_out", "s5_glu_w"]
        if self.mixers and 1 in self.layers:
            names += ["attn_w_qkv", "attn_w_o"]
        self.cast_weights(names)
        s.barrier()
        if self.mixers:
            self.setup_mix()
            s.barrier()
        last = []
        for ti in range(L // NT):
            t0 = ti * NT
            for j in range(KC):
                s.dma(lambda e, a=self.xt[:, j, :], b=self.x_in[j * 128:(j + 1) * 128, t0:t0 + NT]: e.dma_start(out=a, in_=b),
                      writes=[self.b_x[j]])
            for li in self.layers:
                if self.mixers:
                    if li == 0:
                        self.mixer0(ti)
                    else:
                        self.mixer1(ti)
                self.ffn(li)
            for j in range(KC):
                t = s.dma(lambda e, a=self.out[j * 128:(j + 1) * 128, t0:t0 + NT], b=self.xt[:, j, :]: e.dma_start(out=a, in_=b),
                          reads=[self.b_x[j]])
                last.append(t)
        s.finish(last)
        return self.nc


def make_inmaps(inp, L, n_cores=N_CORES):
    f = np.float32
    common = {}
    common["ada_w"] = np.ascontiguousarray(inp["ada_w"], dtype=f)
    common["ada_b"] = np.ascontiguousarray(inp["ada_b"].reshape(2, 48, 128).transpose(2, 0, 1), dtype=f)
    g = np.stack([inp["mix_pre_g"], inp["mix_post_g"], inp["ffn_pre_g"], inp["ffn_post_g"]])
    common["gains"] = np.ascontiguousarray(g.reshape(4, 2, KC, 128).transpose(3, 0, 1, 2), dtype=f)
    for li in range(2):
        common["ffn_w_in%d" % li] = np.ascontiguousarray(inp["ffn_w_in"][li], dtype=f)
        common["ffn_w_out%d" % li] = np.ascontiguousarray(inp["ffn_w_out"][li], dtype=f)
    common["hyb_w_in"] = np.ascontiguousarray(inp["hyb_w_in"][0], dtype=f)
    common["hyb_w_out"] = np.ascontiguousarray(inp["hyb_w_out"][0], dtype=f)
    common["s5_glu_w"] = np.ascontiguousarray(inp["s5_glu_w"][0], dtype=f)
    common["attn_w_qkv"] = np.ascontiguousarray(inp["attn_w_qkv"][0], dtype=f)
    common["attn_w_o"] = np.ascontiguousarray(inp["attn_w_o"][0], dtype=f)
    kj = np.arange(128)[:, None]; qi = np.arange(128)[None, :]
    cm = np.zeros((128, 11, 128), f)
    cm[:, 0, :] = np.eye(128, dtype=f)
    cm[:, 1, :] = (qi >= kj)
    cm[:, 2, :] = (kj >= qi)
    cm[:, 3, :] = (kj <= qi)
    for u in range(1, 8):
        cm[:, 3 + u, :] = (kj >= qi) & (kj >= 128 - 16 * u)
    common["cmask"] = cm
    sel = np.zeros((16, 16, 128), f)
    for h in range(16):
        sel[h, h, :] = 1.0
    common["sel"] = sel.reshape(16, 16 * 128)
    common["convw"] = np.ascontiguousarray(inp["ssd_conv_w"][0].reshape(4, 12, 128).transpose(2, 1, 0), dtype=f)
    fv = np.zeros((128, 5, 12), f)
    fv[:, 0, :] = inp["ssd_conv_b"][0].reshape(12, 128).T
    fv[:, 1, :8] = np.repeat(inp["ssd_d"][0], 64).reshape(8, 128).T
    fv[:, 2, :8] = inp["ssd_norm_g"][0].reshape(8, 128).T
    fv[:, 3, :8] = inp["s5_d"][0].reshape(8, 128).T
    fv[:, 4, :8] = inp["s5_glu_b"][0].reshape(8, 128).T
    common["fvec"] = fv
    common["ssd16"] = np.ascontiguousarray(np.stack([inp["ssd_dt_bias"][0], inp["ssd_a_log"][0]], axis=1), dtype=f)
    lre, lim, ldt = inp["s5_lambda_re"][0], inp["s5_lambda_im"][0], inp["s5_log_dt"][0]
    v1 = np.zeros((128, 3, 32), f)
    for q in range(32):
        for gi in range(2):
            g = 2 * q + gi
            v1[gi * 64:(gi + 1) * 64, 0, q] = lre[g]
            v1[gi * 64:(gi + 1) * 64, 1, q] = lim[g]
            v1[gi * 64:(gi + 1) * 64, 2, q] = ldt[g]
    common["s5v1"] = v1
    v2 = np.zeros((128, 5, 8, 2, 64), f)
    bre, bim = inp["s5_b_re"][0], inp["s5_b_im"][0]
    for fc in range(8):
        for rb in range(4):
            for gi2 in range(2):
                g = 8 * fc + 2 * rb + gi2
                rows = slice(32 * rb, 32 * rb + 32)
                v2[rows, 0, fc, gi2, :] = lre[g][None, :]
                v2[rows, 1, fc, gi2, :] = lim[g][None, :]
                v2[rows, 2, fc, gi2, :] = ldt[g]
                r2 = slice(32 * rb + 16 * gi2, 32 * rb + 16 * gi2 + 16)
                v2[r2, 3, fc, gi2, :] = bre[g].T
                v2[r2, 4, fc, gi2, :] = bim[g].T
    common["s5v2"] = v2.reshape(128, 5, 1024)
    cre, cim = inp["s5_c_re"][0], inp["s5_c_im"][0]
    wc = np.zeros((2, 64, 32, 2, 8, 16), f)
    for q in range(32):
        rb = q % 4
        for gi in range(2):
            g = 2 * q + gi
            wc[gi, :, q, 0, 2 * rb + gi, :] = cre[g].T
            wc[gi, :, q, 1, 2 * rb + gi, :] = cim[g].T
    common["s5wc"] = wc.reshape(128, 32 * 2 * 128)
    maps = []
    nb = inp["x"].shape[0]
    for c in range(n_cores):
        b = c % nb
        m = dict(common)
        m["x_fm"] = np.ascontiguousarray(inp["x"][b, :L].T, dtype=f)
        m["c_fm"] = np.ascontiguousarray(inp["c"][b].reshape(KC, 128).T, dtype=f)
        maps.append(m)
    return maps


_NC_CACHE = {}


def kernel(**inputs):
    inp = {k: np.asarray(v) for k, v in inputs.items()}
    B_, L, _ = inp["x"].shape
    if L not in _NC_CACHE:
        _NC_CACHE[L] = Builder(L).build()
    nc = _NC_CACHE[L]
    maps = make_inmaps(inp, L)
    res = run_bass_kernel_spmd(nc, maps, core_ids=list(range(N_CORES)))
    out = np.stack([res.results[b]["out_fm"].T for b in range(B_)])
    return np.ascontiguousarray(out.astype(np.float32))
```

```python
from contextlib import ExitStack
import numpy as np
import concourse.bass as bass
import concourse.mybir as mybir
from concourse.bass_utils import run_bass_kernel_spmd

F32 = mybir.dt.float32
BF16 = mybir.dt.bfloat16
ALU = mybir.AluOpType
AF = mybir.ActivationFunctionType

COMPUTE = ("pe", "dve", "act", "pool")
QUEUES = ("sp",)


class Buf:
    __slots__ = ("name", "w", "r")

    def __init__(self, name=""):
        self.name = name
        self.w = None
        self.r = []


class Sched:
    def __init__(self, nc, stack, n_dma_sems=32):
        self.nc = nc
        self.ops = {e: [] for e in COMPUTE + QUEUES}
        self.sems = {}
        self.cnt = {}
        for e in COMPUTE:
            self.sems[e] = stack.enter_context(nc.semaphore("sem_" + e))
            self.cnt[e] = 0
        self.dma_sems = []
        for i in range(n_dma_sems):
            k = "dma%d" % i
            self.sems[k] = stack.enter_context(nc.semaphore("sem_" + k))
            self.cnt[k] = 0
            self.dma_sems.append(k)
        self.dma_rr = 0
        self.dma_last_tok = {k: None for k in self.dma_sems}
        self.seen = {e: {} for e in COMPUTE + QUEUES}
        self.n_inst = 0

    def buf(self, name=""):
        return Buf(name)

    def _collect(self, eng, reads, writes, extra=()):
        waits = {}
        seen = self.seen[eng]

        def add(tok):
            if tok is None:
                return
            k, v = tok
            if seen.get(k, 0) >= v:
                return
            if waits.get(k, 0) < v:
                waits[k] = v

        for b in reads:
            add(b.w)
        for b in writes:
            add(b.w)
            for t in b.r:
                add(t)
        for t in extra:
            add(t)
        for k, v in waits.items():
            seen[k] = v
        return list(waits.items())

    def _commit(self, tok, reads, writes):
        for b in reads:
            b.r.append(tok)
            if len(b.r) > 64:
                m = {}
                for k, v in b.r:
                    if m.get(k, 0) < v:
                        m[k] = v
                b.r = list(m.items())
        for b in writes:
            b.w = tok
            b.r = []

    def op(self, eng, fn, reads=(), writes=()):
        waits = self._collect(eng, reads, writes)
        self.cnt[eng] += 1
        tok = (eng, self.cnt[eng])
        sems = self.sems
        mysem = sems[eng]

        def emit(e):
            for k, v in waits:
                e.wait_ge(sems[k], v)
            fn(e).then_inc(mysem, 1)

        self.ops[eng].append(emit)
        self.n_inst += 1
        self._commit(tok, reads, writes)
        return tok

    def dma(self, fn, reads=(), writes=(), queue="sp"):
        k = self.dma_sems[self.dma_rr]
        self.dma_rr = (self.dma_rr + 1) % len(self.dma_sems)
        prev = self.dma_last_tok[k]
        extra = (prev,) if prev is not None else ()
        waits = self._collect(queue, reads, writes, extra)
        self.cnt[k] += 16
        tok = (k, self.cnt[k])
        self.dma_last_tok[k] = tok
        sems = self.sems
        dsem = sems[k]

        def emit(e):
            for kk, v in waits:
                e.wait_ge(sems[kk], v)
            fn(e).then_inc(dsem, 16)

        self.ops[queue].append(emit)
        self.n_inst += 1
        self._commit(tok, reads, writes)
        return tok

    def barrier(self):
        sems = self.sems
        for eng in COMPUTE + QUEUES:
            waits = []
            for k, c in self.cnt.items():
                if k == eng or c == 0:
                    continue
                if self.seen[eng].get(k, 0) < c:
                    self.seen[eng][k] = c
                    waits.append((k, c))

            def emit(e, waits=waits):
                for kk, v in waits:
                    e.wait_ge(sems[kk], v)

            self.ops[eng].append(emit)

    def finish(self, final_toks):
        nc = self.nc
        sems = self.sems
        ops = self.ops
        fin = {}
        for t in final_toks:
            if t is None:
                continue
            k, v = t
            fin[k] = max(fin.get(k, 0), v)
        with nc.Block() as block:
            @block.tensor
            def _(e):
                for f in ops["pe"]:
                    f(e)

            @block.vector
            def _(e):
                for f in ops["dve"]:
                    f(e)

            @block.scalar
            def _(e):
                for f in ops["act"]:
                    f(e)

            @block.gpsimd
            def _(e):
                for f in ops["pool"]:
                    f(e)

            @block.sync
            def _(e):
                for f in ops["sp"]:
                    f(e)
                for k, v in fin.items():
                    e.wait_ge(sems[k], v)


D = 1024
KC = D // 128
NT = 256
FFN_H = 2816
FC = FFN_H // 128
HYB_IN = 3600
EPS = 1e-6
N_CORES = 8


class Builder:
    def __init__(self, L, layers=(0, 1), mixers=True, dbg=None):
        self.L = L
        self.layers = layers
        self.mixers = mixers
        self.dbg = dbg or []
        self.nc = bass.Bass("TRN2", target_bir_lowering=False)
        self.stack = ExitStack()
        self.s = Sched(self.nc, self.stack)
        self.rr = 0
        self._tensors()

    def dram(self, name, shape, dt, kind="Internal"):
        return self.nc.dram_tensor(name, list(shape), dt, kind=kind).ap()

    def sb(self, name, shape, dt):
        t = self.stack.enter_context(self.nc.sbuf_tensor(name, list(shape), dt))
        return t

    def _tensors(self):
        nc, s, L = self.nc, self.s, self.L
        I = "ExternalInput"
        self.x_in = self.dram("x_fm", [D, L], F32, I)
        self.out = self.dram("out_fm", [D, L], F32, "ExternalOutput")
        self.c_in = self.dram("c_fm", [128, KC], F32, I)
        self.ada_w = self.dram("ada_w", [2, D, 6 * D], F32, I)
        self.ada_b = self.dram("ada_b", [128, 2, 48], F32, I)
        self.gains = self.dram("gains", [128, 4, 2, KC], F32, I)
        self.w_f32 = {}
        self.w_bf = {}
        self.w_buf = {}
        for name, shape in (("ffn_w_in0", [D, 2 * FFN_H]), ("ffn_w_in1", [D, 2 * FFN_H]),
                            ("ffn_w_out0", [FFN_H, D]), ("ffn_w_out1", [FFN_H, D]),
                            ("hyb_w_in", [D, HYB_IN]), ("hyb_w_out", [2 * D, D]),
                            ("s5_glu_w", [D, D]), ("attn_w_qkv", [D, 5 * D]), ("attn_w_o", [D, D])):
            self.w_f32[name] = self.dram(name, shape, F32, I)
            self.w_bf[name] = self.dram(name + "_bf", shape, BF16)
            self.w_buf[name] = s.buf(name)

        self.xt = self.sb("xt", [128, KC, NT], F32); self.b_x = [s.buf("x%d" % j) for j in range(KC)]
        self.ht = self.sb("ht", [128, KC, NT], BF16); self.b_h = [s.buf("h%d" % j) for j in range(KC)]
        self.yt = self.sb("yt", [128, KC, NT], F32); self.b_y = [s.buf("y%d" % j) for j in range(KC)]
        self.sq = self.sb("sq", [128, 2, NT], BF16); self.b_sq = [s.buf("sq0"), s.buf("sq1")]
        self.rstd = self.sb("rstd", [128, NT], F32); self.b_rstd = s.buf("rstd")
        self.tmpn = self.sb("tmpn", [128, 2, NT], F32); self.b_tmpn = [s.buf("tn0"), s.buf("tn1")]
        self.b_act = [s.buf("a%d" % j) for j in range(FC)]
        self.sg = self.sb("sg", [128, 2, NT], F32); self.b_sg = [s.buf("sg0"), s.buf("sg1")]
        self.NW = 4
        self.wp = self.sb("wp", [128, self.NW, 4096], BF16); self.b_wp = [s.buf("wp%d" % i) for i in range(self.NW)]
        self.wp_rr = 0
        self.b_arena = [s.buf("ar%d" % i) for i in range(2)]
        self._tensors_mix()
        self.arena = self.A32[:, 0:4096].rearrange("p (a n) -> p a n", a=2)
        self.act = self.A16[:, 0:FC * NT].rearrange("p (j t) -> p j t", j=FC)
        self.ones_bf = self.sb("ones_bf", [128, 128], BF16); self.b_const = s.buf("const")
        self.cst = self.sb("cst", [128, 4], F32)
        self.cond = self.sb("cond", [128, KC], F32)
        self.mod = self.sb("mod", [128, 2, 48], F32); self.b_mod = s.buf("mod")
        self.adab = self.sb("adab", [128, 2, 48], F32)
        self.gn = self.sb("gn", [128, 4, 2, KC], F32)
        self.vec = self.sb("vec", [128, 2, 6, KC], F32); self.b_vec = s.buf("vec")
        self.ps = []
        self.b_ps = []
        for i in range(8):
            self.ps.append(self.stack.enter_context(nc.psum_tensor("ps%d" % i, [128, 512], F32)))
            self.b_ps.append(s.buf("ps%d" % i))
        self.ps_rr = 0
        self.psbf_rr = 0

    POOL_BANKS = (0, 1, 2)
    BF_BANKS = (3, 7)

    def psum_bf(self):
        i = self.BF_BANKS[self.psbf_rr]
        self.psbf_rr = (self.psbf_rr + 1) % len(self.BF_BANKS)
        return self.ps[i], self.b_ps[i]

    def psum(self):
        i = self.POOL_BANKS[self.ps_rr]
        self.ps_rr = (self.ps_rr + 1) % len(self.POOL_BANKS)
        return self.ps[i], self.b_ps[i]

    def psum_fixed(self, i):
        return self.ps[i], self.b_ps[i]

    def eng3(self):
        e = ("act", "pool", "dve")[self.rr % 3]
        self.rr += 1
        return e

    def cast_weights(self, names):
        s = self.s
        i = 0
        for name in names:
            src, dst = self.w_f32[name], self.w_bf[name]
            K, N = src.shape
            for kb in range(K // 128):
                for c0 in range(0, N, 2048):
                    cw = min(2048, N - c0)
                    slot = i % 2
                    wslot = i % 2
                    i += 1
                    st32 = self.arena[:, slot, 0:cw]
                    stbf = self.wp[:, wslot, 0:cw]
                    s.dma(lambda e, a=st32, b=src[kb * 128:(kb + 1) * 128, c0:c0 + cw]: e.dma_start(out=a, in_=b),
                          writes=[self.b_arena[slot]])
                    eng = self.eng3()
                    if eng == "act":
                        s.op("act", lambda e, a=stbf, b=st32: e.activation(out=a, in_=b, func=AF.Copy),
                             reads=[self.b_arena[slot]], writes=[self.b_wp[wslot]])
                    else:
                        s.op(eng, lambda e, a=stbf, b=st32: e.tensor_copy(a, b),
                             reads=[self.b_arena[slot]], writes=[self.b_wp[wslot]])
                    s.dma(lambda e, a=dst[kb * 128:(kb + 1) * 128, c0:c0 + cw], b=stbf: e.dma_start(out=a, in_=b),
                          reads=[self.b_wp[wslot]], writes=[self.w_buf[name]])

    def setup_consts(self):
        s = self.s
        s.op("pool", lambda e: e.memset(self.ones_bf[:, :], 1.0), writes=[self.b_const])
        s.op("pool", lambda e: e.memset(self.cst[:, 0:1], EPS), writes=[self.b_const])
        s.op("pool", lambda e: e.memset(self.cst[:, 1:2], 1.0), writes=[self.b_const])
        s.op("pool", lambda e: e.memset(self.cst[:, 2:3], 0.0), writes=[self.b_const])
        s.dma(lambda e: e.dma_start(out=self.cond[:, :], in_=self.c_in), writes=[self.b_const])
        s.dma(lambda e: e.dma_start(out=self.adab[:, :, :], in_=self.ada_b), writes=[self.b_const])
        s.dma(lambda e: e.dma_start(out=self.gn[:, :, :, :], in_=self.gains), writes=[self.b_const])
        s.op("act", lambda e: e.activation(out=self.cond[:, :], in_=self.cond[:, :], func=AF.Silu),
             reads=[self.b_const], writes=[self.b_const])

    def ada_mod(self):
        s = self.s
        for li in range(2):
            ps, bps = self.psum()
            for cp in range(12):
                slot = cp % 3
                for half in range(2):
                    c0 = cp * 512 + half * 256
                    slot = (cp * 2 + half) % 2
                    view = self.arena[:, slot, 0:2048].rearrange("p (k n) -> p k n", k=KC)
                    s.dma(lambda e, a=view, b=self.ada_w[li, :, c0:c0 + 256].rearrange("(k p) n -> p k n", p=128):
                          e.dma_start(out=a, in_=b), writes=[self.b_arena[slot]])
                    for mm in range(2):
                        m = (c0 // 128) + mm
                        for k in range(KC):
                            s.op("pe", lambda e, o=ps[:, m:m + 1], w=view[:, k, mm * 128:(mm + 1) * 128], r=self.cond[:, k:k + 1],
                                 st=(k == 0), sp=(k == KC - 1): e.matmul(o, w, r, start=st, stop=sp),
                                 reads=[self.b_arena[slot], self.b_const], writes=[bps])
            s.op("dve", lambda e, o=self.mod[:, li, :], a=ps[:, 0:48], b=self.adab[:, li, :]:
                 e.tensor_tensor(o, a, b, ALU.add), reads=[bps, self.b_const], writes=[self.b_mod])
        for li in range(2):
            for half, (gpre, gpost) in enumerate(((0, 1), (2, 3))):
                o = half * 24
                sh = self.mod[:, li, o:o + 8]
                sc = self.mod[:, li, o + 8:o + 16]
                gt = self.mod[:, li, o + 16:o + 24]
                s.op("dve", lambda e, out=self.vec[:, li, half * 3 + 0, :], sc=sc, g=self.gn[:, gpre, li, :]:
                     e.scalar_tensor_tensor(out, sc, 1.0, g, ALU.add, ALU.mult),
                     reads=[self.b_mod, self.b_const], writes=[self.b_vec])
                s.op("dve", lambda e, out=self.vec[:, li, half * 3 + 1, :], sh=sh: e.tensor_copy(out, sh),
                     reads=[self.b_mod], writes=[self.b_vec])
                s.op("dve", lambda e, out=self.vec[:, li, half * 3 + 2, :], gt=gt, g=self.gn[:, gpost, li, :]:
                     e.tensor_tensor(out, gt, g, ALU.mult), reads=[self.b_mod, self.b_const], writes=[self.b_vec])

    def wpanel(self, name, kc, c0, cols, k0=0):
        s = self.s
        assert kc * cols <= 4096
        slot = self.wp_rr
        self.wp_rr = (self.wp_rr + 1) % self.NW
        view = self.wp[:, slot, 0:kc * cols].rearrange("p (k n) -> p k n", k=kc)
        src = self.w_bf[name][k0 * 128:(k0 + kc) * 128, c0:c0 + cols].rearrange("(k p) n -> p k n", p=128)
        s.dma(lambda e, a=view, b=src: e.dma_start(out=a, in_=b), reads=[self.w_buf[name]], writes=[self.b_wp[slot]])
        return view, self.b_wp[slot]

    def sumsq_rstd(self, src, b_src, nchunks, denom):
        s = self.s
        ps, bps = self.psum()
        for j in range(nchunks):
            q = j % 2
            eng = "pool" if j % 2 == 0 else "act"
            if eng == "act":
                s.op("act", lambda e, o=self.sq[:, q, :], a=src[j]: e.activation(out=o, in_=a, func=AF.Square),
                     reads=[b_src[j]], writes=[self.b_sq[q]])
            else:
                s.op("pool", lambda e, o=self.sq[:, q, :], a=src[j]: e.tensor_tensor(o, a, a, ALU.mult),
                     reads=[b_src[j]], writes=[self.b_sq[q]])
            s.op("pe", lambda e, o=ps[:, 0:NT], r=self.sq[:, q, :], st=(j == 0), sp=(j == nchunks - 1):
                 e.matmul(o, self.ones_bf[:, :], r, start=st, stop=sp),
                 reads=[self.b_sq[q], self.b_const], writes=[bps])
        s.op("act", lambda e, o=self.rstd[:, :], a=ps[:, 0:NT]: e.activation(out=o, in_=a, func=AF.Sqrt,
                                                                        bias=self.cst[:, 0:1], scale=1.0 / denom),
             reads=[bps, self.b_const], writes=[self.b_rstd])
        s.op("dve", lambda e, o=self.rstd[:, :]: e.reciprocal(o, o), reads=[self.b_rstd], writes=[self.b_rstd])

    def pre_norm(self, li, which):
        s = self.s
        self.sumsq_rstd([self.xt[:, j, :] for j in range(KC)], self.b_x, KC, float(D))
        for j in range(KC):
            q = j % 2
            s.op("dve", lambda e, o=self.tmpn[:, q, :], a=self.xt[:, j, :]: e.tensor_tensor(o, a, self.rstd[:, :], ALU.mult),
                 reads=[self.b_x[j], self.b_rstd], writes=[self.b_tmpn[q]])
            s.op("act", lambda e, o=self.ht[:, j, :], a=self.tmpn[:, q, :], sc=self.vec[:, li, which * 3 + 0, j:j + 1],
                 bi=self.vec[:, li, which * 3 + 1, j:j + 1]: e.activation(out=o, in_=a, func=AF.Identity, bias=bi, scale=sc),
                 reads=[self.b_tmpn[q], self.b_vec], writes=[self.b_h[j]])

    def post_norm_residual(self, li, which):
        s = self.s
        self.sumsq_rstd([self.yt[:, j, :] for j in range(KC)], self.b_y, KC, float(D))
        for j in range(KC):
            q = j % 2
            eng = "dve" if j % 2 == 0 else "pool"
            s.op(eng, lambda e, o=self.tmpn[:, q, :], a=self.yt[:, j, :]: e.tensor_tensor(o, a, self.rstd[:, :], ALU.mult),
                 reads=[self.b_y[j], self.b_rstd], writes=[self.b_tmpn[q]])
            s.op("dve", lambda e, o=self.xt[:, j, :], a=self.tmpn[:, q, :], g=self.vec[:, li, which * 3 + 2, j:j + 1]:
                 e.scalar_tensor_tensor(o, a, g, o, ALU.mult, ALU.add),
                 reads=[self.b_tmpn[q], self.b_vec], writes=[self.b_x[j]])

    def ffn(self, li):
        s = self.s
        win = "ffn_w_in%d" % li
        wout = "ffn_w_out%d" % li
        self.pre_norm(li, 1)
        for mp in range(0, FC, 4):
            nm = min(4, FC - mp)
            wg, bwg = self.wpanel(win, KC, mp * 128, nm * 128)
            wu, bwu = self.wpanel(win, KC, FFN_H + mp * 128, nm * 128)
            for mi in range(nm):
                m = mp + mi
                pg, bpg = self.psum()
                pu, bpu = self.psum()
                for k in range(KC):
                    s.op("pe", lambda e, o=pg[:, 0:NT], w=wg[:, k, mi * 128:(mi + 1) * 128], r=self.ht[:, k, :], st=(k == 0), sp=(k == KC - 1):
                         e.matmul(o, w, r, start=st, stop=sp), reads=[bwg, self.b_h[k]], writes=[bpg])
                for k in range(KC):
                    s.op("pe", lambda e, o=pu[:, 0:NT], w=wu[:, k, mi * 128:(mi + 1) * 128], r=self.ht[:, k, :], st=(k == 0), sp=(k == KC - 1):
                         e.matmul(o, w, r, start=st, stop=sp), reads=[bwu, self.b_h[k]], writes=[bpu])
                q = m % 2
                s.op("act", lambda e, o=self.sg[:, q, :], a=pg[:, 0:NT]: e.activation(out=o, in_=a, func=AF.Silu),
                     reads=[bpg], writes=[self.b_sg[q]])
                s.op("dve", lambda e, o=self.act[:, m, :], a=pu[:, 0:NT], b=self.sg[:, q, :]: e.tensor_tensor(o, a, b, ALU.mult),
                     reads=[bpu, self.b_sg[q]], writes=[self.b_act[m]])
        for m in range(KC):
            w, bw = self.wpanel(wout, FC, m * 128, 128)
            po, bpo = self.psum()
            for k in range(FC):
                s.op("pe", lambda e, o=po[:, 0:NT], w_=w[:, k, :], r=self.act[:, k, :], st=(k == 0), sp=(k == FC - 1):
                     e.matmul(o, w_, r, start=st, stop=sp), reads=[bw, self.b_act[k]], writes=[bpo])
            s.op("act", lambda e, o=self.yt[:, m, :], a=po[:, 0:NT]: e.activation(out=o, in_=a, func=AF.Copy),
                 reads=[bpo], writes=[self.b_y[m]])
        self.post_norm_residual(li, 1)

    def _tensors_mix(self):
        nc, s, L = self.nc, self.s, self.L
        I = "ExternalInput"
        self.d_cmask = self.dram("cmask", [128, 11, 128], F32, I)
        self.d_sel = self.dram("sel", [16, 16 * 128], F32, I)
        self.d_convw = self.dram("convw", [128, 12, 4], F32, I)
        self.d_fvec = self.dram("fvec", [128, 5, 12], F32, I)
        self.d_ssd16 = self.dram("ssd16", [16, 2], F32, I)
        self.d_s5v1 = self.dram("s5v1", [128, 3, 32], F32, I)
        self.d_s5v2 = self.dram("s5v2", [128, 5, 1024], F32, I)
        self.d_s5wc = self.dram("s5wc", [128, 32 * 2 * 128], F32, I)
        self.Kd = self.dram("Kd", [8, 128, 2048 + L], BF16)
        self.Vd = self.dram("Vd", [8, 128, 2048 + L], BF16)
        self.b_Kd = s.buf("Kd"); self.b_Vd = s.buf("Vd")
        self.cmask = self.sb("cmask_sb", [128, 11, 128], F32)
        self.ident_bf = self.sb("ident_bf", [128, 128], BF16)
        self.sel = self.sb("sel_sb", [16, 16, 128], F32)
        self.ones16 = self.sb("ones16", [16, 128], F32)
        self.convw = self.sb("convw_sb", [128, 12, 4], F32)
        self.fvec = self.sb("fvec_sb", [128, 5, 12], F32)
        self.ssd16 = self.sb("ssd16_sb", [16, 4], F32)
        self.A32 = self.sb("A32", [128, 7424], F32); self.b_A32 = s.buf("A32")
        self.A16 = self.sb("A16", [128, 19456], BF16); self.b_A16 = s.buf("A16")
        self.Hst = self.sb("Hst", [128, 1024], F32); self.b_Hst = [s.buf("Hst%d" % h) for h in range(16)]
        self.Hbf = self.sb("Hbf", [128, 1024], BF16); self.b_Hbf = [s.buf("Hbf%d" % h) for h in range(16)]
        self.ctail = self.sb("ctail", [128, 12, 3], F32); self.b_ctail = s.buf("ctail")
        self.dtt = self.sb("dtt", [16, 5, NT], F32); self.b_dtt = [s.buf("dtt%d" % i) for i in range(5)]
        self.tok = self.sb("tok", [128, 2, 48], F32); self.b_tok = [s.buf("tok0"), s.buf("tok1")]
        self.xdt = self.sb("xdt", [128, 2, 1024], BF16); self.b_xdt = [s.buf("xdt"), s.buf("xdtw")]
        self.Btok = self.sb("Btok", [128, 256], BF16); self.b_Btok = s.buf("Btok")
        self.cbm = self.sb("cbm", [128, 256], F32); self.b_cbm = s.buf("cbm")
        self.D1 = self.sb("D1", [128, 2, 128], F32); self.b_D1 = [s.buf("D1a"), s.buf("D1b")]
        self.Eh = self.sb("Eh", [128, 2, 128], F32); self.b_Eh = [s.buf("Eha"), s.buf("Ehb")]
        self.Mh = self.sb("Mh", [128, 2, 128], BF16); self.b_Mh = [s.buf("Mha"), s.buf("Mhb")]
        self.Csh = self.sb("Csh", [128, 2, 128], BF16); self.b_Csh = [s.buf("Csa"), s.buf("Csb")]
        self.Wc = self.sb("Wc", [128, 32, 2, 128], BF16)
        self.Wb = self.sb("Wb", [128, 8, 2, 128], BF16)
        self.s5c = self.sb("s5c", [128, 4, 32], F32)
        self.TS = 16
        TS = self.TS
        self.Hall = self.sb("Hall", [128, 2, 32, TS + 1], F32); self.b_Hall = [s.buf("Hall0"), s.buf("Hall1")]
        self.Hs16 = self.sb("Hs16", [128, 2, 32, TS], BF16); self.b_Hs16 = [s.buf("Hs16a"), s.buf("Hs16b")]
        self.Tinv = self.sb("Tinv", [128, 2, 32, TS], F32)
        self.Tfwd = self.sb("Tfwd", [128, 2, 32, TS], F32)
        self.smask = self.sb("smask", [128, 2, 32, TS + 1], BF16)
        self.b_m12 = s.buf("m12"); self.b_m3 = s.buf("m3")
        self.b_s5w = s.buf("s5w")
        a32 = self.A32
        self.convbuf = a32[:, 0:12 * (NT + 3)].rearrange("p (j t) -> p j t", j=12)
        o = 12 * (NT + 3)
        self.cacc = a32[:, o:o + 2 * NT].rearrange("p (j t) -> p j t", j=2); o += 2 * NT
        self.ys = a32[:, o:o + 8 * NT].rearrange("p (j t) -> p j t", j=8); o += 8 * NT
        self.yg = a32[:, 0:8 * NT].rearrange("p (j t) -> p j t", j=8)
        assert o <= 7424, o
        self.mtmp = a32[:, 2048:2048 + 3 * 32 * self.TS].rearrange("p (a q t) -> p a q t", a=3, q=32)
        a16 = self.A16
        o = 0
        def v16(n, j):
            nonlocal o
            r = a16[:, o:o + n].rearrange("p (j t) -> p j t", j=j)
            o += n
            return r
        self.zs = v16(8 * NT, 8)
        self.xsb = v16(8 * NT, 8)
        self.Bfm = v16(2 * NT, 2)
        self.Cfm = v16(2 * NT, 2)
        self.ubf = v16(8 * NT, 8)
        self.cat = v16(16 * NT, 16)
        self.g5b = v16(8 * NT, 8)
        assert o <= 19456, o
        self.b_zs = [s.buf() for _ in range(8)]
        self.b_xsb = [s.buf() for _ in range(8)]
        self.b_BC = [s.buf() for _ in range(4)]
        self.b_ubf = [s.buf() for _ in range(8)]
        self.b_cat = [s.buf() for _ in range(16)]
        self.b_g5b = [s.buf() for _ in range(8)]
        self.b_conv = [s.buf() for _ in range(12)]
        self.b_cacc = [s.buf(), s.buf()]
        self.b_ys = s.buf()
        self.b_yg = [s.buf() for _ in range(8)]
        o = 0
        self.Qb = v16(24 * NT, 24)
        self.kvst = v16(2 * NT, 2)
        self.Kw = a16[:, o:o + 2048 + NT]; o += 2048 + NT
        self.Vw = a16[:, o:o + 2048 + NT]; o += 2048 + NT
        self.attn = v16(8 * NT, 8)
        self.Vtok = v16(43 * 128, 43)
        self.PT = v16(4 * 128, 4)
        assert o <= 19456, o
        self.b_Qb = [s.buf() for _ in range(24)]
        self.b_kvst = [s.buf(), s.buf()]
        self.b_Kw = s.buf(); self.b_Vw = s.buf()
        self.b_attn = [s.buf() for _ in range(8)]
        self.b_Vtok = [s.buf() for _ in range(6)]
        self.b_PT = [s.buf() for _ in range(4)]
        self.Oacc = a32[:, 0:2 * NT].rearrange("p (j t) -> p j t", j=2)
        self.b_Oacc = [s.buf(), s.buf()]
        self.pt_rr = 0

    def setup_mix(self):
        s = self.s
        bc = self.b_const
        s.dma(lambda e: e.dma_start(out=self.cmask[:, :, :], in_=self.d_cmask), writes=[bc])
        s.dma(lambda e: e.dma_start(out=self.sel[:, :, :], in_=self.d_sel.rearrange("k (h m) -> k h m", h=16)), writes=[bc])
        s.dma(lambda e: e.dma_start(out=self.convw[:, :, :], in_=self.d_convw), writes=[bc])
        s.dma(lambda e: e.dma_start(out=self.fvec[:, :, :], in_=self.d_fvec), writes=[bc])
        s.dma(lambda e: e.dma_start(out=self.ssd16[:, 0:2], in_=self.d_ssd16), writes=[bc])
        s.op("pool", lambda e: e.memset(self.ones16[:, :], 1.0), writes=[bc])
        s.op("dve", lambda e: e.tensor_copy(self.ident_bf[:, :], self.cmask[:, 0, :]), reads=[bc], writes=[bc])
        s.op("act", lambda e: e.activation(out=self.ssd16[:, 2:3], in_=self.ssd16[:, 1:2], func=AF.Exp), reads=[bc], writes=[bc])
        s.op("dve", lambda e: e.tensor_scalar(self.ssd16[:, 2:3], self.ssd16[:, 2:3], -1.0, None, ALU.mult), reads=[bc], writes=[bc])
        for h in range(16):
            s.op("pool", lambda e, h=h: e.memset(self.Hst[:, h * 64:(h + 1) * 64], 0.0), writes=[self.b_Hst[h]])
            s.op("pool", lambda e, h=h: e.memset(self.Hbf[:, h * 64:(h + 1) * 64], 0.0), writes=[self.b_Hbf[h]])
        s.op("pool", lambda e: e.memset(self.ctail[:, :, :], 0.0), writes=[self.b_ctail])
        for ri in range(2):
            s.op("pool", lambda e, ri=ri: e.memset(self.Hall[:, ri, :, :], 0.0), writes=[self.b_Hall[ri]])
        s.op("pool", lambda e: e.memset(self.smask[:, :, :, :], 1.0), writes=[self.b_s5w])
        s.op("pool", lambda e: e.memset(self.smask[:, :, :, 0:1], 0.0), writes=[self.b_s5w])
        self.setup_s5()
        if 1 in self.layers:
            s.op("pool", lambda e: e.memset(self.wp[:, 0, 0:2048], 0.0), writes=[self.b_wp[0]])
            for h in range(8):
                s.dma(lambda e, h=h: e.dma_start(out=self.Kd[h, :, 0:2048], in_=self.wp[:, 0, 0:2048]), reads=[self.b_wp[0]], writes=[self.b_Kd])
                s.dma(lambda e, h=h: e.dma_start(out=self.Vd[h, :, 0:2048], in_=self.wp[:, 0, 0:2048]), reads=[self.b_wp[0]], writes=[self.b_Vd])

    def setup_s5(self):
        s = self.s
        ba = self.b_A32
        PI = float(np.pi)
        sc = self.A32
        I32 = mybir.dt.int32

        def dve(fn, reads=(), writes=()):
            s.op("dve", fn, reads=[ba] + list(reads), writes=[ba] + list(writes))

        def act(fn):
            s.op("act", fn, reads=[ba], writes=[ba])

        def coeffs(lr, li, ld, N, base, ts=1.0):
            t = [sc[:, base + i * N: base + (i + 1) * N] for i in range(8)]
            dt, mag, ang, r, kf, m, ar, ai = t
            ki = kf.bitcast(I32)
            act(lambda e: e.activation(out=dt, in_=ld, func=AF.Exp))
            if ts != 1.0:
                dve(lambda e: e.tensor_scalar(dt, dt, float(ts), None, ALU.mult))
            dve(lambda e: e.tensor_tensor(mag, lr, dt, ALU.mult))
            act(lambda e: e.activation(out=mag, in_=mag, func=AF.Exp))
            dve(lambda e: e.tensor_tensor(ang, li, dt, ALU.mult))

            def reduce_sin(shift, out):
                dve(lambda e: e.tensor_scalar(r, ang, shift, None, ALU.add))
                dve(lambda e: e.tensor_scalar(m, r, 1.0 / (2 * PI), None, ALU.mult))
                dve(lambda e: e.tensor_copy(ki, m))
                dve(lambda e: e.tensor_copy(m, ki))
                dve(lambda e: e.scalar_tensor_tensor(r, m, -2 * PI, r, ALU.mult, ALU.add))
                dve(lambda e: e.tensor_scalar(m, r, PI, None, ALU.is_gt))
                dve(lambda e: e.scalar_tensor_tensor(r, m, -2 * PI, r, ALU.mult, ALU.add))
                dve(lambda e: e.tensor_scalar(m, r, -PI, None, ALU.is_lt))
                dve(lambda e: e.scalar_tensor_tensor(r, m, 2 * PI, r, ALU.mult, ALU.add))
                act(lambda e: e.activation(out=out, in_=r, func=AF.Sin))

            reduce_sin(0.0, ai)
            reduce_sin(PI / 2, ar)
            dve(lambda e: e.tensor_tensor(ar, ar, mag, ALU.mult))
            dve(lambda e: e.tensor_tensor(ai, ai, mag, ALU.mult))
            return ar, ai

        v1 = sc[:, 0:96].rearrange("p (a q) -> p a q", a=3)
        s.dma(lambda e: e.dma_start(out=v1, in_=self.d_s5v1), writes=[ba])
        ar, ai = coeffs(v1[:, 0, :], v1[:, 1, :], v1[:, 2, :], 32, 128)
        dve(lambda e, ar=ar: e.tensor_copy(self.s5c[:, 0, :], ar), writes=[self.b_s5w])
        dve(lambda e, ar=ar: e.tensor_copy(self.s5c[:, 1, :], ar), writes=[self.b_s5w])
        dve(lambda e, ai=ai: e.tensor_copy(self.s5c[:, 2, :], ai), writes=[self.b_s5w])
        dve(lambda e, ai=ai: e.tensor_scalar(self.s5c[:, 3, :], ai, -1.0, None, ALU.mult), writes=[self.b_s5w])
        for t in range(1, self.TS + 1):
            for tab, sgn in ((self.Tfwd, 1.0), (self.Tinv, -1.0)):
                ar_t, ai_t = coeffs(v1[:, 0, :], v1[:, 1, :], v1[:, 2, :], 32, 128, ts=sgn * t)
                dve(lambda e, ar_t=ar_t, tab=tab, t=t: e.tensor_copy(tab[:, 0, :, t - 1], ar_t), writes=[self.b_s5w])
                dve(lambda e, ai_t=ai_t, tab=tab, t=t: e.tensor_copy(tab[:, 1, :, t - 1], ai_t), writes=[self.b_s5w])
        N = 128
        for fc in range(8):
            inp = [sc[:, i * N:(i + 1) * N] for i in range(5)]
            for i in range(5):
                s.dma(lambda e, i=i, fc=fc: e.dma_start(out=inp[i], in_=self.d_s5v2[:, i, fc * 128:(fc + 1) * 128]), writes=[ba])
            lr, li, ld, Bre, Bim = inp
            ar, ai = coeffs(lr, li, ld, N, 5 * N)
            den, am1, qre, qim, t1, t2 = [sc[:, (13 + i) * N:(14 + i) * N] for i in range(6)]
            dve(lambda e: e.tensor_tensor(den, lr, lr, ALU.mult))
            dve(lambda e: e.tensor_tensor(t1, li, li, ALU.mult))
            dve(lambda e: e.tensor_tensor(den, den, t1, ALU.add))
            dve(lambda e: e.reciprocal(den, den))
            dve(lambda e: e.tensor_scalar(am1, ar, -1.0, None, ALU.add))
            dve(lambda e: e.tensor_tensor(t1, am1, lr, ALU.mult))
            dve(lambda e: e.tensor_tensor(t2, ai, li, ALU.mult))
            dve(lambda e: e.tensor_tensor(qre, t1, t2, ALU.add))
            dve(lambda e: e.tensor_tensor(qre, qre, den, ALU.mult))
            dve(lambda e: e.tensor_tensor(t1, ai, lr, ALU.mult))
            dve(lambda e: e.tensor_tensor(t2, am1, li, ALU.mult))
            dve(lambda e: e.tensor_tensor(qim, t1, t2, ALU.subtract))
            dve(lambda e: e.tensor_tensor(qim, qim, den, ALU.mult))
            dve(lambda e: e.tensor_tensor(t1, qre, Bre, ALU.mult))
            dve(lambda e: e.tensor_tensor(t2, qim, Bim, ALU.mult))
            dve(lambda e, fc=fc: e.tensor_tensor(self.Wb[:, fc, 0, :], t1, t2, ALU.subtract), writes=[self.b_s5w])
            dve(lambda e: e.tensor_tensor(t1, qre, Bim, ALU.mult))
            dve(lambda e: e.tensor_tensor(t2, qim, Bre, ALU.mult))
            dve(lambda e, fc=fc: e.tensor_tensor(self.Wb[:, fc, 1, :], t1, t2, ALU.add), writes=[self.b_s5w])
        for q in range(32):
            st = sc[:, 0:256].rearrange("p (r m) -> p r m", r=2)
            s.dma(lambda e, q=q: e.dma_start(out=sc[:, 0:256], in_=self.d_s5wc[:, q * 256:(q + 1) * 256]), writes=[ba])
            dve(lambda e, q=q: e.tensor_copy(self.Wc[:, q, 0, :], st[:, 0, :]), writes=[self.b_s5w])
            dve(lambda e, q=q: e.tensor_scalar(self.Wc[:, q, 1, :], st[:, 1, :], -1.0, None, ALU.mult), writes=[self.b_s5w])

    def evac(self, eng, out, in_, reads, writes):
        s = self.s
        if eng == "act":
            s.op("act", lambda e: e.activation(out=out, in_=in_, func=AF.Copy), reads=reads, writes=writes)
        else:
            s.op(eng, lambda e: e.tensor_copy(out, in_), reads=reads, writes=writes)

    def proj(self, wname, c0, ncols, nk, rhs_fn, rhs_bufs, on_chunk, k0=0, panel_cols=512):
        s = self.s
        pc = min(panel_cols, (4096 // nk) // 128 * 128) if ncols >= 128 else ncols
        mi = 0
        for p0 in range(0, ncols, pc):
            pw = min(pc, ncols - p0)
            w, bw = self.wpanel(wname, nk, c0 + p0, pw, k0=k0)
            for m0 in range(0, pw, 128):
                mw = min(128, pw - m0)
                ps, bps = self.psum()
                for k in range(nk):
                    s.op("pe", lambda e, o=ps[0:mw, 0:NT], w_=w[:, k, m0:m0 + mw], r=rhs_fn(k), st=(k == 0), sp=(k == nk - 1):
                         e.matmul(o, w_, r, start=st, stop=sp), reads=[bw, rhs_bufs[k]], writes=[bps])
                on_chunk(mi, mw, ps, bps)
                mi += 1

    def mixer0(self, ti):
        s = self.s
        fv = self.fvec
        self.pre_norm(0, 0)
        hrhs = lambda k: self.ht[:, k, :]
        def on_z(mi, mw, ps, bps):
            s.op("act", lambda e: e.activation(out=self.zs[:, mi, :], in_=ps[:, 0:NT], func=AF.Silu),
                 reads=[bps], writes=[self.b_zs[mi], self.b_A16])
        self.proj("hyb_w_in", 0, 1024, KC, hrhs, self.b_h, on_z)

        def on_xbc(mi, mw, ps, bps):
            self.evac("dve", self.convbuf[:, mi, 3:3 + NT], ps[:, 0:NT], [bps], [self.b_conv[mi], self.b_A32])
        self.proj("hyb_w_in", 1024, 1536, KC, hrhs, self.b_h, on_xbc)

        def on_dt(mi, mw, ps, bps):
            s.op("act", lambda e: e.activation(out=self.dtt[:, 0, :], in_=ps[0:16, 0:NT], func=AF.Exp, bias=self.ssd16[:, 0:1], scale=1.0),
                 reads=[bps, self.b_const], writes=[self.b_dtt[0]])
        self.proj("hyb_w_in", 2560, 16, KC, hrhs, self.b_h, on_dt)

        def on_u(mi, mw, ps, bps):
            self.evac("act", self.ubf[:, mi, :], ps[:, 0:NT], [bps], [self.b_ubf[mi]])
        self.proj("hyb_w_in", 2576, 1024, KC, hrhs, self.b_h, on_u)

        for j in range(12):
            cb = self.convbuf
            q = j % 2
            acc = self.cacc[:, q, :]
            s.op("dve", lambda e, j=j: e.tensor_copy(cb[:, j, 0:3], self.ctail[:, j, :]), reads=[self.b_ctail], writes=[self.b_conv[j]])
            s.op("dve", lambda e, j=j, acc=acc: e.tensor_scalar(acc, cb[:, j, 0:NT], self.convw[:, j, 0:1], None, ALU.mult),
                 reads=[self.b_conv[j], self.b_const], writes=[self.b_cacc[q]])
            for k in range(1, 4):
                s.op("dve", lambda e, j=j, k=k, acc=acc: e.scalar_tensor_tensor(acc, cb[:, j, k:k + NT], self.convw[:, j, k:k + 1], acc, ALU.mult, ALU.add),
                     reads=[self.b_conv[j], self.b_const], writes=[self.b_cacc[q]])
            s.op("dve", lambda e, j=j: e.tensor_copy(self.ctail[:, j, :], cb[:, j, NT:NT + 3]), reads=[self.b_conv[j]], writes=[self.b_ctail])
            if j < 8:
                dst, bd = self.xsb[:, j, :], self.b_xsb[j]
            elif j < 10:
                dst, bd = self.Bfm[:, j - 8, :], self.b_BC[j - 8]
            else:
                dst, bd = self.Cfm[:, j - 10, :], self.b_BC[j - 8]
            s.op("act", lambda e, j=j, acc=acc, dst=dst: e.activation(out=dst, in_=acc, func=AF.Silu, bias=fv[:, 0, j:j + 1], scale=1.0),
                 reads=[self.b_cacc[q], self.b_const], writes=[bd])

        dt_e, dtv, a_, ac, dtw = [self.dtt[:, i, :] for i in range(5)]
        bd = self.b_dtt
        s.op("act", lambda e: e.activation(out=dtv, in_=dt_e, func=AF.Ln, bias=self.cst[0:16, 1:2], scale=1.0),
             reads=[bd[0], self.b_const], writes=[bd[1]])
        s.op("dve", lambda e: e.tensor_scalar(a_, dtv, self.ssd16[:, 2:3], None, ALU.mult), reads=[bd[1], self.b_const], writes=[bd[2]])
        NCH = NT // 128
        for c in range(NCH):
            sl = slice(c * 128, (c + 1) * 128)
            s.op("dve", lambda e, sl=sl: e.tensor_tensor_scan(ac[:, sl], self.ones16[:, :], a_[:, sl], 0.0, ALU.mult, ALU.add),
                 reads=[bd[2], self.b_const], writes=[bd[3]])
        for c in range(NCH):
            sl = slice(c * 128, (c + 1) * 128)
            s.op("act", lambda e, sl=sl, c=c: e.activation(out=dt_e[:, sl], in_=ac[:, sl], func=AF.Exp, bias=ac[:, c * 128 + 127:c * 128 + 128], scale=-1.0),
                 reads=[bd[3]], writes=[bd[0]])
        s.op("dve", lambda e: e.tensor_tensor(dtw, dtv, dt_e, ALU.mult), reads=[bd[0], bd[1]], writes=[bd[4]])

        idf = self.cmask[0:16, 0, 0:16]
        self._s5_pending = [i * self.TS for i in range(NT // self.TS)]
        for c in range(NCH):
            self.ssd_chunk(c, ac, dtv, dtw, bd, idf)
        while self._s5_pending:
            self.s5_sub(self._s5_pending.pop(0), self.TS)
        for g in range(2):
            self.sumsq_rstd([self.yg[:, 4 * g + jj, :] for jj in range(4)], self.b_yg[4 * g:4 * g + 4], 4, 512.0)
            for jj in range(4):
                j = 4 * g + jj
                q = j % 2
                s.op("dve", lambda e, j=j, q=q: e.tensor_tensor(self.tmpn[:, q, :], self.yg[:, j, :], self.rstd[:, :], ALU.mult),
                     reads=[self.b_yg[j], self.b_rstd], writes=[self.b_tmpn[q]])
                s.op("act", lambda e, j=j, q=q: e.activation(out=self.cat[:, j, :], in_=self.tmpn[:, q, :], func=AF.Copy, scale=fv[:, 2, j:j + 1]),
                     reads=[self.b_tmpn[q], self.b_const], writes=[self.b_cat[j]])
        self.s5_tile()
        if "cat" in self.dbg and ti == 0:
            self.d_dbg = self.dram("dbg_cat", [128, 16, NT], BF16, "ExternalOutput")
            s.dma(lambda e: e.dma_start(out=self.d_dbg, in_=self.cat), reads=self.b_cat)
            self.d_dbg2 = self.dram("dbg_yg", [128, 8, NT], F32, "ExternalOutput")
            s.dma(lambda e: e.dma_start(out=self.d_dbg2, in_=self.yg), reads=self.b_yg)
            self.d_dbg4 = self.dram("dbg_xsb", [128, 8, NT], BF16, "ExternalOutput")
            s.dma(lambda e: e.dma_start(out=self.d_dbg4, in_=self.xsb), reads=self.b_xsb)
            self.d_dbg5 = self.dram("dbg_bc", [128, 4, NT], BF16, "ExternalOutput")
            s.dma(lambda e: e.dma_start(out=self.d_dbg5[:, 0:2, :], in_=self.Bfm), reads=self.b_BC)
            s.dma(lambda e: e.dma_start(out=self.d_dbg5[:, 2:4, :], in_=self.Cfm), reads=self.b_BC)
            self.d_dbg6 = self.dram("dbg_dtt", [16, 5, NT], F32, "ExternalOutput")
            s.dma(lambda e: e.dma_start(out=self.d_dbg6, in_=self.dtt[:, :, :]), reads=self.b_dtt)
            self.d_dbg3 = self.dram("dbg_ys", [128, 8, NT], F32, "ExternalOutput")
            s.dma(lambda e: e.dma_start(out=self.d_dbg3, in_=self.ys), reads=[self.b_ys])
            s.barrier()
        def on_o(mi, mw, ps, bps):
            self.evac("act", self.yt[:, mi, :], ps[:, 0:NT], [bps], [self.b_y[mi]])
        self.proj("hyb_w_out", 0, 1024, 16, lambda k: self.cat[:, k, :], self.b_cat, on_o, panel_cols=256)
        self.post_norm_residual(0, 0)

    def ssd_chunk(self, c, ac, dtv, dtw, bd, idf):
        s = self.s
        fv = self.fvec
        sl = slice(c * 128, (c + 1) * 128)
        tq = c % 2
        tok = self.tok[:, tq, :]
        btok = self.b_tok[tq]
        pt, bpt = self.psum()
        for i, (src, bsrc) in enumerate(((ac, bd[3]), (dtv, bd[1]), (dtw, bd[4]))):
            s.op("pe", lambda e, i=i, src=src, sl=sl: e.transpose(pt[:, i * 16:(i + 1) * 16], src[:, sl], idf),
                 reads=[bsrc, self.b_const], writes=[bpt])
        self.evac("dve", tok, pt[:, 0:48], [bpt], [btok])
        px, bpx = self.psum_bf()
        pxb = px[:, :].bitcast(BF16)
        for j in range(8):
            s.op("pe", lambda e, j=j, sl=sl: e.transpose(pxb[:, j * 128:(j + 1) * 128], self.xsb[:, j, sl], self.ident_bf[:, :]),
                 reads=[self.b_xsb[j], self.b_const], writes=[bpx])
        for i in range(2):
            s.op("dve", lambda e, i=i: e.tensor_tensor(self.xdt[:, i, :].rearrange("p (h d) -> p h d", h=16),
                                                       pxb.rearrange("p (h d) -> p h d", h=16),
                                                       tok[:, 16 * (i + 1):16 * (i + 2)].unsqueeze(2).to_broadcast([128, 16, 64]), ALU.mult),
                 reads=[bpx, btok], writes=[self.b_xdt[i]])
        pb, bpb = self.psum_bf()
        pbb = pb[:, :].bitcast(BF16)
        for g in range(2):
            s.op("pe", lambda e, g=g, sl=sl: e.transpose(pbb[:, g * 128:(g + 1) * 128], self.Bfm[:, g, sl], self.ident_bf[:, :]),
                 reads=[self.b_BC[g], self.b_const], writes=[bpb])
        self.evac("act", self.Btok[:, :], pbb[:, 0:256], [bpb], [self.b_Btok])
        pc, bpc = self.psum()
        for g in range(2):
            s.op("pe", lambda e, g=g, sl=sl: e.matmul(pc[:, g * 128:(g + 1) * 128], self.Bfm[:, g, sl], self.Cfm[:, g, sl], start=True, stop=True),
                 reads=[self.b_BC[g], self.b_BC[2 + g]], writes=[bpc])
        s.op("dve", lambda e: e.tensor_tensor(self.cbm[:, :].rearrange("p (g l) -> p g l", g=2), pc[:, 0:256].rearrange("p (g l) -> p g l", g=2),
                                              self.cmask[:, 1, :].unsqueeze(1).to_broadcast([128, 2, 128]), ALU.mult),
             reads=[bpc, self.b_const], writes=[self.b_cbm])
        pst = []
        for g in range(2):
            p_, bp_ = self.psum_fixed(4 + g)
            s.op("pe", lambda e, g=g, p_=p_: e.matmul(p_[:, :], self.Btok[:, g * 128:(g + 1) * 128], self.xdt[:, 1, g * 512:(g + 1) * 512], start=True, stop=True),
                 reads=[self.b_Btok, self.b_xdt[1]], writes=[bp_])
            pst.append((p_, bp_))
        py = None
        for h in range(16):
            g = h // 8
            hq = h % 2
            if h % 8 == 0:
                py, bpy = self.psum_fixed(6)
            j4 = (h % 8) // 2
            pa, bpa = self.psum()
            s.op("pe", lambda e, h=h, sl=sl, pa=pa: e.matmul(pa[:, 0:128], self.sel[:, h, :], ac[:, sl], start=True, stop=True),
                 reads=[bd[3], self.b_const], writes=[bpa])
            s.op("dve", lambda e, h=h, hq=hq, pa=pa: e.tensor_scalar(self.D1[:, hq, :], pa[:, 0:128], tok[:, h:h + 1], 0.0, ALU.subtract, ALU.min),
                 reads=[bpa, btok], writes=[self.b_D1[hq]])
            s.op("act", lambda e, hq=hq: e.activation(out=self.D1[:, hq, :], in_=self.D1[:, hq, :], func=AF.Exp),
                 reads=[self.b_D1[hq]], writes=[self.b_D1[hq]])
            s.op("pool", lambda e, hq=hq, g=g: e.tensor_tensor(self.Mh[:, hq, :], self.D1[:, hq, :], self.cbm[:, g * 128:(g + 1) * 128], ALU.mult),
                 reads=[self.b_D1[hq], self.b_cbm], writes=[self.b_Mh[hq]])
            s.op("act", lambda e, hq=hq, pa=pa: e.activation(out=self.Eh[:, hq, :], in_=pa[:, 0:128], func=AF.Exp),
                 reads=[bpa], writes=[self.b_Eh[hq]])
            s.op("pool", lambda e, hq=hq, g=g, sl=sl: e.tensor_tensor(self.Csh[:, hq, :], self.Cfm[:, g, sl], self.Eh[:, hq, :], ALU.mult),
                 reads=[self.b_BC[2 + g], self.b_Eh[hq]], writes=[self.b_Csh[hq]])
            yo = py[hq * 64:(hq + 1) * 64, j4 * 128:(j4 + 1) * 128]
            s.op("pe", lambda e, h=h, hq=hq, yo=yo: e.matmul(yo, self.xdt[:, 0, h * 64:(h + 1) * 64], self.Mh[:, hq, :], start=True, stop=False),
                 reads=[self.b_xdt[0], self.b_Mh[hq]], writes=[bpy])
            s.op("pe", lambda e, h=h, hq=hq, yo=yo: e.matmul(yo, self.Hbf[:, h * 64:(h + 1) * 64], self.Csh[:, hq, :], start=False, stop=True),
                 reads=[self.b_Hbf[h], self.b_Csh[hq]], writes=[bpy])
            p_, bp_ = pst[g]
            hs = slice(h * 64, (h + 1) * 64)
            s.op("dve", lambda e, hs=hs, hq=hq, p_=p_, h=h: e.scalar_tensor_tensor(self.Hst[:, hs], self.Hst[:, hs], self.Eh[:, hq, 127:128],
                                                                                p_[:, (h % 8) * 64:(h % 8 + 1) * 64], ALU.mult, ALU.add),
                 reads=[self.b_Eh[hq], bp_], writes=[self.b_Hst[h]])
            s.op("pool", lambda e, hs=hs: e.tensor_copy(self.Hbf[:, hs], self.Hst[:, hs]), reads=[self.b_Hst[h]], writes=[self.b_Hbf[h]])
            if h % 2 == 1 and self._s5_pending:
                self.s5_sub(self._s5_pending.pop(0), self.TS)
            if h % 8 == 7:
                for jj in range(4):
                    j = 4 * g + jj
                    s.op("dve", lambda e, j=j, jj=jj, sl=sl, py=py: e.scalar_tensor_tensor(self.yg[:, j, sl], self.xsb[:, j, sl], fv[:, 1, j:j + 1],
                                                                                             py[:, jj * 128:(jj + 1) * 128], ALU.mult, ALU.add),
                         reads=[self.b_xsb[j], bpy, self.b_const], writes=[self.b_yg[j]] + self.b_conv)
                    s.op("pool", lambda e, j=j, sl=sl: e.tensor_tensor(self.yg[:, j, sl], self.yg[:, j, sl], self.zs[:, j, sl], ALU.mult),
                         reads=[self.b_zs[j]], writes=[self.b_yg[j]])

    def s5_sub(self, ts, T):
        s = self.s
        H = self.Hall
        bH = self.b_Hall
        m = self.mtmp
        alias = self.b_conv + self.b_cacc
        for ri in range(2):
            pb, bpb = self.psum()
            for q in range(32):
                fc, rb = q // 4, q % 4
                s.op("pe", lambda e, pb=pb, q=q, fc=fc, rb=rb, ri=ri: e.matmul(
                    pb[:, q * T:(q + 1) * T], self.Wb[32 * rb:32 * rb + 32, fc, ri, :], self.ubf[32 * rb:32 * rb + 32, fc, ts:ts + T],
                    start=True, stop=True, tile_position=(32 * rb, 0)), reads=[self.b_s5w, self.b_ubf[fc]], writes=[bpb])
            s.op("act", lambda e, pb=pb, ri=ri: e.activation(out=H[:, ri, :, 1:1 + T], in_=pb[:, 0:32 * T].rearrange("p (q t) -> p q t", q=32), func=AF.Copy),
                 reads=[bpb], writes=[bH[ri]])
        Br, Bi = H[:, 0, :, 1:1 + T], H[:, 1, :, 1:1 + T]
        s.op("dve", lambda e: e.tensor_tensor(m[:, 0:2, :, :], H[:, :, :, 1:1 + T], self.Tinv[:, :, :, :], ALU.mult),
             reads=[bH[0], bH[1], self.b_s5w], writes=[self.b_m12] + alias)
        s.op("pool", lambda e: e.tensor_tensor(m[:, 2, :, :], Br, self.Tinv[:, 1, :, :], ALU.mult),
             reads=[bH[0], self.b_s5w], writes=[self.b_m3] + alias)
        s.op("dve", lambda e: e.tensor_tensor(Bi, Bi, self.Tinv[:, 0, :, :], ALU.mult), reads=[self.b_s5w], writes=[bH[1]])
        s.op("dve", lambda e: e.tensor_tensor(Br, m[:, 0, :, :], m[:, 1, :, :], ALU.subtract), reads=[self.b_m12], writes=[bH[0]])
        s.op("dve", lambda e: e.tensor_tensor(Bi, Bi, m[:, 2, :, :], ALU.add), reads=[self.b_m3], writes=[bH[1]])
        flat = H[:, :, :, :].rearrange("p a q t -> p (a q t)")
        s.op("dve", lambda e: e.tensor_tensor_scan(flat, self.smask[:, :, :, :].rearrange("p a q t -> p (a q t)"), flat, 0.0, ALU.mult, ALU.add),
             reads=[self.b_s5w], writes=[bH[0], bH[1]])
        s.op("dve", lambda e: e.tensor_tensor(m[:, 0:2, :, :], H[:, :, :, 1:1 + T], self.Tfwd[:, :, :, :], ALU.mult),
             reads=[bH[0], bH[1], self.b_s5w], writes=[self.b_m12])
        s.op("pool", lambda e: e.tensor_tensor(m[:, 2, :, :], Br, self.Tfwd[:, 1, :, :], ALU.mult),
             reads=[bH[0], self.b_s5w], writes=[self.b_m3])
        s.op("dve", lambda e: e.tensor_tensor(Bi, Bi, self.Tfwd[:, 0, :, :], ALU.mult), reads=[self.b_s5w], writes=[bH[1]])
        s.op("dve", lambda e: e.tensor_tensor(self.Hs16[:, 0, :, :], m[:, 0, :, :], m[:, 1, :, :], ALU.subtract),
             reads=[self.b_m12], writes=[self.b_Hs16[0]])
        s.op("pool", lambda e: e.tensor_tensor(self.Hs16[:, 1, :, :], m[:, 2, :, :], Bi, ALU.add),
             reads=[self.b_m3, bH[1]], writes=[self.b_Hs16[1]])
        s.op("dve", lambda e: e.tensor_tensor(H[:, 0, :, 0], m[:, 0, :, T - 1], m[:, 1, :, T - 1], ALU.subtract),
             reads=[self.b_m12], writes=[bH[0]])
        s.op("pool", lambda e: e.tensor_tensor(H[:, 1, :, 0], m[:, 2, :, T - 1], H[:, 1, :, T], ALU.add),
             reads=[self.b_m3], writes=[bH[1]])
        po, bpo = self.psum()
        for fc in range(8):
            n = 0
            for rb in range(4):
                q = 4 * fc + rb
                for ri in range(2):
                    s.op("pe", lambda e, fc=fc, q=q, ri=ri, n=n: e.matmul(po[:, fc * T:(fc + 1) * T], self.Wc[:, q, ri, :], self.Hs16[:, ri, q, :],
                                                                         start=(n == 0), stop=(n == 7)),
                         reads=[self.b_s5w, self.b_Hs16[ri]], writes=[bpo])
                    n += 1
        s.op("act", lambda e, po=po: e.activation(out=self.ys[:, :, ts:ts + T], in_=po[:, 0:8 * T].rearrange("p (j t) -> p j t", j=8), func=AF.Copy),
             reads=[bpo], writes=[self.b_ys])

    def s5_tile(self):
        s = self.s
        fv = self.fvec
        for j in range(8):
            q = j % 2
            s.op("dve", lambda e, j=j, q=q: e.scalar_tensor_tensor(self.tmpn[:, q, :], self.ubf[:, j, :], fv[:, 3, j:j + 1], self.ys[:, j, :], ALU.mult, ALU.add),
                 reads=[self.b_ubf[j], self.b_ys, self.b_const], writes=[self.b_tmpn[q]])
            s.op("act", lambda e, j=j, q=q: e.activation(out=self.g5b[:, j, :], in_=self.tmpn[:, q, :], func=AF.Gelu),
                 reads=[self.b_tmpn[q]], writes=[self.b_g5b[j]])

        def on_g(mi, mw, ps, bps):
            q = mi % 2
            s.op("act", lambda e: e.activation(out=self.tmpn[:, q, :], in_=ps[:, 0:NT], func=AF.Sigmoid, bias=fv[:, 4, mi:mi + 1], scale=1.0),
                 reads=[bps, self.b_const], writes=[self.b_tmpn[q]])
            s.op("dve", lambda e: e.tensor_tensor(self.cat[:, 8 + mi, :], self.g5b[:, mi, :], self.tmpn[:, q, :], ALU.mult),
                 reads=[self.b_tmpn[q], self.b_g5b[mi]], writes=[self.b_cat[8 + mi]])
        self.proj("s5_glu_w", 0, 1024, KC, lambda k: self.g5b[:, k, :], self.b_g5b, on_g)

    def mixer1(self, ti):
        s = self.s
        t0 = ti * NT
        self.pre_norm(1, 0)
        hrhs = lambda k: self.ht[:, k, :]

        def on_q(mi, mw, ps, bps):
            self.evac("act" if mi % 2 == 0 else "dve", self.Qb[:, mi, :], ps[:, 0:NT], [bps], [self.b_Qb[mi]])
        self.proj("attn_w_qkv", 0, 3072, KC, hrhs, self.b_h, on_q)

        def mk_kv(dst, bdst):
            def on_kv(mi, mw, ps, bps):
                q = mi % 2
                self.evac("act" if mi % 2 == 0 else "dve", self.kvst[:, q, :], ps[:, 0:NT], [bps], [self.b_kvst[q]])
                s.dma(lambda e: e.dma_start(out=dst[mi, :, 2048 + t0:2048 + t0 + NT], in_=self.kvst[:, q, :]),
                      reads=[self.b_kvst[q]], writes=[bdst])
            return on_kv
        self.proj("attn_w_qkv", 3072, 1024, KC, hrhs, self.b_h, mk_kv(self.Kd, self.b_Kd))
        self.proj("attn_w_qkv", 4096, 1024, KC, hrhs, self.b_h, mk_kv(self.Vd, self.b_Vd))
        for h in range(8):
            self.attn_head(ti, h)

        def on_o(mi, mw, ps, bps):
            self.evac("act", self.yt[:, mi, :], ps[:, 0:NT], [bps], [self.b_y[mi]])
        self.proj("attn_w_o", 0, 1024, KC, lambda k: self.attn[:, k, :], self.b_attn, on_o)
        self.post_norm_residual(1, 0)

    def attn_pat(self, ti, h, d, qoff, vprev, vcur, ncur, vb, exp_mask, PT, Ob, bO, Db, bD, Oa, Da):
        s = self.s
        cm = self.cmask
        W = 2048 + NT
        nq = NT // d
        kprev0 = 2048 - 128 * d
        u = ti * nq // 16
        has_prev = ti > 0
        if nq * ti >= 128:
            pmask = cm[:, 2, 0:nq]
        elif has_prev:
            pmask = cm[:, 3 + (nq * ti) // 16, 0:nq]
        pS, bpS = self.psum()
        for r in range(d):
            if has_prev:
                s.op("pe", lambda e, r=r: e.matmul(pS[:, r * nq:(r + 1) * nq], self.Kw[:, kprev0 + r:2048:d], self.Qb[:, qoff + h, r:NT:d], start=True, stop=True),
                     reads=[self.b_Kw, self.b_Qb[qoff + h]], writes=[bpS])
            s.op("pe", lambda e, r=r: e.matmul(pS[0:ncur, 256 + r * nq:256 + (r + 1) * nq], self.Kw[:, 2048 + r:W:d], self.Qb[:, qoff + h, r:NT:d], start=True, stop=True),
                 reads=[self.b_Kw, self.b_Qb[qoff + h]], writes=[bpS])
        parts = []
        if has_prev:
            parts.append((128, 0, NT, pmask.unsqueeze(1).to_broadcast([128, d, nq]), d, nq))
        parts.append((ncur, 256, NT, cm[0:ncur, 3, 0:nq].unsqueeze(1).to_broadcast([ncur, d, nq]), d, nq))
        exp_mask(pS, bpS, parts)
        dview = Db[:, 0:NT]
        if has_prev:
            s.op("pe", lambda e: e.matmul(dview, self.ones_bf[:, :], PT[:, 0:NT], start=True, stop=False),
                 reads=[self.b_PT[0], self.b_const], writes=[bD])
        s.op("pe", lambda e: e.matmul(dview, self.ones_bf[0:ncur, :], PT[0:ncur, 256:256 + NT], start=(not has_prev), stop=True),
             reads=[self.b_PT[0], self.b_const], writes=[bD])
        for r in range(d):
            if has_prev:
                s.op("pe", lambda e, r=r: e.matmul(Ob[:, r:NT:d], self.Vtok[:, vprev + r, :], PT[:, r * nq:(r + 1) * nq], start=True, stop=False),
                     reads=[self.b_PT[0], vb[vprev + r]], writes=[bO])
            s.op("pe", lambda e, r=r: e.matmul(Ob[:, r:NT:d], self.Vtok[0:ncur, vcur + r, :], PT[0:ncur, 256 + r * nq:256 + (r + 1) * nq], start=(not has_prev), stop=True),
                 reads=[self.b_PT[0], vb[vcur + r]], writes=[bO])
        s.op("dve", lambda e: e.tensor_tensor(Oa, Oa, Ob[:, 0:NT], ALU.add), reads=[bO], writes=[self.b_Oacc[0]])
        s.op("dve", lambda e: e.tensor_tensor(Da.rearrange("p (q r) -> p r q", r=d), Da.rearrange("p (q r) -> p r q", r=d),
                                              Db[:, 0:NT].rearrange("p (r q) -> p r q", r=d), ALU.add), reads=[bD], writes=[self.b_Oacc[1]])

    def attn_head(self, ti, h):
        s = self.s
        t0 = ti * NT
        W = 2048 + NT
        cm = self.cmask
        scale = float(128 ** -0.5)
        s.dma(lambda e: e.dma_start(out=self.Kw[:, 0:W], in_=self.Kd[h, :, t0:t0 + W]), reads=[self.b_Kd], writes=[self.b_Kw])
        s.dma(lambda e: e.dma_start(out=self.Vw[:, 0:W], in_=self.Vd[h, :, t0:t0 + W]), reads=[self.b_Vd], writes=[self.b_Vw])
        blocks = []
        for i in range(3):
            blocks.append((i, slice(1920 + 128 * i, 2048 + 128 * i), 128))
        for r in range(4):
            blocks.append((3 + r, slice(1536 + r, 2048, 4), 128))
        for r in range(4):
            blocks.append((7 + r, slice(2048 + r, W, 4), 64))
        for r in range(16):
            blocks.append((11 + r, slice(r, 2048, 16), 128))
        for r in range(16):
            blocks.append((27 + r, slice(2048 + r, W, 16), 16))
        groups = [(0, 7, 128), (7, 11, 64), (11, 19, 128), (19, 27, 128), (27, 35, 16), (35, 43, 16)]
        vb = {}
        for gi, (a, b, nk) in enumerate(groups):
            pv, bpv = self.psum_bf()
            pvb = pv[:, :].bitcast(BF16)
            for (vidx, sl, nk_) in blocks[a:b]:
                c0 = (vidx - a) * 128
                s.op("pe", lambda e, pvb=pvb, sl=sl, c0=c0, nk_=nk_: e.transpose(pvb[0:nk_, c0:c0 + 128], self.Vw[:, sl], self.ident_bf[:, :]),
                     reads=[self.b_Vw, self.b_const], writes=[bpv])
                vb[vidx] = self.b_Vtok[gi]
            eng = "dve" if gi % 2 == 0 else "act"
            self.evac(eng, self.Vtok[0:nk, a:b, :], pvb[0:nk, 0:(b - a) * 128].rearrange("p (j e) -> p j e", j=b - a), [bpv], [self.b_Vtok[gi]])
        Ob, bO = self.psum_fixed(4)
        Db, bD = self.psum_fixed(5)
        Oa, Da = self.Oacc[:, 0, :], self.Oacc[:, 1, :]
        PT = self.PT[:, :, :].rearrange("p a n -> p (a n)")

        def exp_mask(pS, bpS, parts):
            for (nk, c0, nc_, mask, nrep, nq) in parts:
                s.op("act", lambda e, nk=nk, c0=c0, nc_=nc_: e.activation(out=PT[0:nk, c0:c0 + nc_], in_=pS[0:nk, c0:c0 + nc_], func=AF.Exp, scale=scale),
                     reads=[bpS], writes=[self.b_PT[0]])
                s.op("dve", lambda e, nk=nk, c0=c0, nc_=nc_, mask=mask, nrep=nrep, nq=nq: e.tensor_tensor(
                    PT[0:nk, c0:c0 + nc_].rearrange("p (a q) -> p a q", a=nrep), PT[0:nk, c0:c0 + nc_].rearrange("p (a q) -> p a q", a=nrep),
                    mask, ALU.mult), reads=[self.b_const], writes=[self.b_PT[0]])

        pS, bpS = self.psum()
        NQB = NT // 128
        for qb in range(NQB):
            for blk in range(2):
                c0 = (qb * 2 + blk) * 128
                s.op("pe", lambda e, qb=qb, blk=blk, c0=c0: e.matmul(pS[:, c0:c0 + 128], self.Kw[:, 1920 + 128 * (qb + blk):2048 + 128 * (qb + blk)],
                                                                    self.Qb[:, h, qb * 128:(qb + 1) * 128], start=True, stop=True),
                     reads=[self.b_Kw, self.b_Qb[h]], writes=[bpS])
        exp_mask(pS, bpS, [(128, qb * 256, 256, cm[:, 2:4, :], 2, 128) for qb in range(NQB)])
        if ti == 0:
            s.op("dve", lambda e: e.memset(PT[:, 0:128], 0.0), writes=[self.b_PT[0]])
        PT4 = PT[:, 0:NQB * 256].rearrange("p (a b q) -> p a b q", a=NQB, b=2)
        for blk in range(2):
            s.op("pe", lambda e, blk=blk: e.matmul(Db[:, 0:NT].rearrange("p (a q) -> p a q", a=NQB), self.ones_bf[:, :], PT4[:, :, blk, :],
                                                  start=(blk == 0), stop=(blk == 1)), reads=[self.b_PT[0], self.b_const], writes=[bD])
        for qb in range(NQB):
            for blk in range(2):
                s.op("pe", lambda e, qb=qb, blk=blk: e.matmul(Ob[:, qb * 128:(qb + 1) * 128], self.Vtok[:, qb + blk, :], PT4[:, qb, blk, :],
                                                             start=(blk == 0), stop=(blk == 1)), reads=[self.b_PT[0], vb[qb + blk]], writes=[bO])
        self.evac("act", Oa, Ob[:, 0:NT], [bO], [self.b_Oacc[0]])
        self.evac("act", Da, Db[:, 0:NT], [bD], [self.b_Oacc[1]])

        for (d, qoff, vprev, vcur, ncur) in ((4, 8, 3, 7, 64), (16, 16, 11, 27, 16)):
            self.attn_pat(ti, h, d, qoff, vprev, vcur, ncur, vb, exp_mask, PT, Ob, bO, Db, bD, Oa, Da)
        s.op("dve", lambda e: e.reciprocal(Da, Da), reads=[self.b_Oacc[1]], writes=[self.b_Oacc[1]])
        s.op("dve", lambda e: e.tensor_tensor(self.attn[:, h, :], Oa, Da, ALU.mult), reads=self.b_Oacc, writes=[self.b_attn[h]])

    def build(self):
        s = self.s
        L = self.L
        self.setup_consts()
        self.ada_mod()
        names = []
        for li in self.layers:
            names += ["ffn_w_in%d" % li, "ffn_w_out%d" % li]
        if self.mixers and 0 in self.layers:
            names += ["hyb_w_in", "hyb_w_out", "s5_glu_w"]
        if self.mixers and 1 in self.layers:
            names += ["attn_w_qkv", "attn_w_o"]
        self.cast_weights(names)
        s.barrier()
        if self.mixers:
            self.setup_mix()
            s.barrier()
        last = []
        for ti in range(L // NT):
            t0 = ti * NT
            for j in range(KC):
                s.dma(lambda e, a=self.xt[:, j, :], b=self.x_in[j * 128:(j + 1) * 128, t0:t0 + NT]: e.dma_start(out=a, in_=b),
                      writes=[self.b_x[j]])
            for li in self.layers:
                if self.mixers:
                    if li == 0:
                        self.mixer0(ti)
                    else:
                        self.mixer1(ti)
                self.ffn(li)
            for j in range(KC):
                t = s.dma(lambda e, a=self.out[j * 128:(j + 1) * 128, t0:t0 + NT], b=self.xt[:, j, :]: e.dma_start(out=a, in_=b),
                          reads=[self.b_x[j]])
                last.append(t)
        s.finish(last)
        return self.nc


def make_inmaps(inp, L, n_cores=N_CORES):
    f = np.float32
    common = {}
    common["ada_w"] = np.ascontiguousarray(inp["ada_w"], dtype=f)
    common["ada_b"] = np.ascontiguousarray(inp["ada_b"].reshape(2, 48, 128).transpose(2, 0, 1), dtype=f)
    g = np.stack([inp["mix_pre_g"], inp["mix_post_g"], inp["ffn_pre_g"], inp["ffn_post_g"]])
    common["gains"] = np.ascontiguousarray(g.reshape(4, 2, KC, 128).transpose(3, 0, 1, 2), dtype=f)
    for li in range(2):
        common["ffn_w_in%d" % li] = np.ascontiguousarray(inp["ffn_w_in"][li], dtype=f)
        common["ffn_w_out%d" % li] = np.ascontiguousarray(inp["ffn_w_out"][li], dtype=f)
    common["hyb_w_in"] = np.ascontiguousarray(inp["hyb_w_in"][0], dtype=f)
    common["hyb_w_out"] = np.ascontiguousarray(inp["hyb_w_out"][0], dtype=f)
    common["s5_glu_w"] = np.ascontiguousarray(inp["s5_glu_w"][0], dtype=f)
    common["attn_w_qkv"] = np.ascontiguousarray(inp["attn_w_qkv"][0], dtype=f)
    common["attn_w_o"] = np.ascontiguousarray(inp["attn_w_o"][0], dtype=f)
    kj = np.arange(128)[:, None]; qi = np.arange(128)[None, :]
    cm = np.zeros((128, 11, 128), f)
    cm[:, 0, :] = np.eye(128, dtype=f)
    cm[:, 1, :] = (qi >= kj)
    cm[:, 2, :] = (kj >= qi)
    cm[:, 3, :] = (kj <= qi)
    for u in range(1, 8):
        cm[:, 3 + u, :] = (kj >= qi) & (kj >= 128 - 16 * u)
    common["cmask"] = cm
    sel = np.zeros((16, 16, 128), f)
    for h in range(16):
        sel[h, h, :] = 1.0
    common["sel"] = sel.reshape(16, 16 * 128)
    common["convw"] = np.ascontiguousarray(inp["ssd_conv_w"][0].reshape(4, 12, 128).transpose(2, 1, 0), dtype=f)
    fv = np.zeros((128, 5, 12), f)
    fv[:, 0, :] = inp["ssd_conv_b"][0].reshape(12, 128).T
    fv[:, 1, :8] = np.repeat(inp["ssd_d"][0], 64).reshape(8, 128).T
    fv[:, 2, :8] = inp["ssd_norm_g"][0].reshape(8, 128).T
    fv[:, 3, :8] = inp["s5_d"][0].reshape(8, 128).T
    fv[:, 4, :8] = inp["s5_glu_b"][0].reshape(8, 128).T
    common["fvec"] = fv
    common["ssd16"] = np.ascontiguousarray(np.stack([inp["ssd_dt_bias"][0], inp["ssd_a_log"][0]], axis=1), dtype=f)
    lre, lim, ldt = inp["s5_lambda_re"][0], inp["s5_lambda_im"][0], inp["s5_log_dt"][0]
    v1 = np.zeros((128, 3, 32), f)
    for q in range(32):
        for gi in range(2):
            g = 2 * q + gi
            v1[gi * 64:(gi + 1) * 64, 0, q] = lre[g]
            v1[gi * 64:(gi + 1) * 64, 1, q] = lim[g]
            v1[gi * 64:(gi + 1) * 64, 2, q] = ldt[g]
    common["s5v1"] = v1
    v2 = np.zeros((128, 5, 8, 2, 64), f)
    bre, bim = inp["s5_b_re"][0], inp["s5_b_im"][0]
    for fc in range(8):
        for rb in range(4):
            for gi2 in range(2):
                g = 8 * fc + 2 * rb + gi2
                rows = slice(32 * rb, 32 * rb + 32)
                v2[rows, 0, fc, gi2, :] = lre[g][None, :]
                v2[rows, 1, fc, gi2, :] = lim[g][None, :]
                v2[rows, 2, fc, gi2, :] = ldt[g]
                r2 = slice(32 * rb + 16 * gi2, 32 * rb + 16 * gi2 + 16)
                v2[r2, 3, fc, gi2, :] = bre[g].T
                v2[r2, 4, fc, gi2, :] = bim[g].T
    common["s5v2"] = v2.reshape(128, 5, 1024)
    cre, cim = inp["s5_c_re"][0], inp["s5_c_im"][0]
    wc = np.zeros((2, 64, 32, 2, 8, 16), f)
    for q in range(32):
        rb = q % 4
        for gi in range(2):
            g = 2 * q + gi
            wc[gi, :, q, 0, 2 * rb + gi, :] = cre[g].T
            wc[gi, :, q, 1, 2 * rb + gi, :] = cim[g].T
    common["s5wc"] = wc.reshape(128, 32 * 2 * 128)
    maps = []
    nb = inp["x"].shape[0]
    for c in range(n_cores):
        b = c % nb
        m = dict(common)
        m["x_fm"] = np.ascontiguousarray(inp["x"][b, :L].T, dtype=f)
        m["c_fm"] = np.ascontiguousarray(inp["c"][b].reshape(KC, 128).T, dtype=f)
        maps.append(m)
    return maps


_NC_CACHE = {}


def kernel(**inputs):
    inp = {k: np.asarray(v) for k, v in inputs.items()}
    B_, L, _ = inp["x"].shape
    if L not in _NC_CACHE:
        _NC_CACHE[L] = Builder(L).build()
    nc = _NC_CACHE[L]
    maps = make_inmaps(inp, L)
    res = run_bass_kernel_spmd(nc, maps, core_ids=list(range(N_CORES)))
    out = np.stack([res.results[b]["out_fm"].T for b in range(B_)])
    return np.ascontiguousarray(out.astype(np.float32))
```
